# Optimizing a Trainium2 kernel written in Bass

```python
import math
import jax, jax.numpy as jnp
from jax import lax
import numpy as np

D_MODEL = 1024
BATCH = 16
SEQ = 2048
DEPTH = 2
DEC_BATCH = 128
DEC_SEQ = 1
PAST_LEN = 16384
PAGE_SIZE = 128

N_MIXERS = 2
N_ATTN_LAYERS = (DEPTH + 1) // 2
N_RET_LAYERS = DEPTH // 2
ATT_HEADS = 32
ATT_KV_HEADS = 4
ATT_GROUP = ATT_HEADS // ATT_KV_HEADS
ATT_HEAD_DIM = 64
ATT_WIDTH = ATT_HEADS * ATT_HEAD_DIM
ATT_KV_WIDTH = ATT_KV_HEADS * ATT_HEAD_DIM
ATT_IN = ATT_WIDTH + 2 * ATT_KV_WIDTH + ATT_WIDTH
WINDOW = 128
NUM_BUCKETS = 32
MAX_DISTANCE = 128
RET_HEADS = 4
RET_QK_DIM = 256
RET_V_DIM = 512
RET_QK_WIDTH = RET_HEADS * RET_QK_DIM
RET_WIDTH = RET_HEADS * RET_V_DIM
RET_IN = 2 * RET_QK_WIDTH + 2 * RET_WIDTH
RET_CHUNK = 128
ROPE_BASE = 10000.0
EPS = 1e-6
NEG = -1e30

kernel_name = 'hybrid_swa_sink_retention_step'


def _rmsnorm(x, g):
    xf = x.astype(jnp.float32)
    y = xf * lax.rsqrt(jnp.mean(xf * xf, axis=-1, keepdims=True) + EPS) * g.astype(jnp.float32)
    return y.astype(x.dtype)


def _t5_bucket(rel):
    n = jnp.maximum(rel, 0)
    max_exact = NUM_BUCKETS // 2
    nf = jnp.maximum(n, 1).astype(jnp.float32)
    large = max_exact + (jnp.log(nf / max_exact) / math.log(MAX_DISTANCE / max_exact)
                         * (NUM_BUCKETS - max_exact)).astype(jnp.int32)
    large = jnp.minimum(large, NUM_BUCKETS - 1)
    return jnp.where(n < max_exact, n, large)


def _gated_out(o, g, w_out):
    y = jax.nn.silu(g.astype(jnp.float32)) * o.astype(jnp.float32)
    return y.astype(w_out.dtype) @ w_out


def _attn_project(h, w_in):
    B, T, _ = h.shape
    proj = h @ w_in
    q, k, v, g = jnp.split(proj, [ATT_WIDTH, ATT_WIDTH + ATT_KV_WIDTH, ATT_WIDTH + 2 * ATT_KV_WIDTH], axis=-1)
    q = q.reshape(B, T, ATT_KV_HEADS, ATT_GROUP, ATT_HEAD_DIM)
    k = k.reshape(B, T, ATT_KV_HEADS, ATT_HEAD_DIM)
    v = v.reshape(B, T, ATT_KV_HEADS, ATT_HEAD_DIM)
    return q, k, v, g


def _swa_attend(q, k, v, q_pos, k_pos, rel_bias, sink):
    Tq, Tk = q.shape[1], k.shape[1]
    logits = jnp.einsum('bqkgd,bskd->bkgqs', q.astype(jnp.float32), k.astype(jnp.float32)) * (ATT_HEAD_DIM ** -0.5)
    rel = q_pos[:, None] - k_pos[None, :]
    bias = jnp.transpose(rel_bias.astype(jnp.float32)[_t5_bucket(rel)], (2, 0, 1))
    bias = bias.reshape(ATT_KV_HEADS, ATT_GROUP, Tq, Tk)
    valid = (k_pos[None, :] >= 0) & (rel >= 0) & (rel < WINDOW)
    logits = jnp.where(valid, logits + bias, NEG)
    s = sink.astype(jnp.float32).reshape(ATT_KV_HEADS, ATT_GROUP, 1, 1)
    m = jnp.maximum(jnp.max(logits, axis=-1, keepdims=True), s)
    p = jnp.exp(logits - m)
    w = p / (jnp.sum(p, axis=-1, keepdims=True) + jnp.exp(s - m))
    return jnp.einsum('bkgqs,bskd->bqkgd', w, v.astype(jnp.float32))


def _attn_prompt(h, w_in, sink, w_out, rel_bias):
    B, S, _ = h.shape
    q, k, v, g = _attn_project(h, w_in)
    nblk = S // WINDOW
    pad = jnp.zeros((B, WINDOW, ATT_KV_HEADS, ATT_HEAD_DIM), k.dtype)
    k_ext = jnp.concatenate([pad, k], axis=1).reshape(B, nblk + 1, WINDOW, ATT_KV_HEADS, ATT_HEAD_DIM)
    v_ext = jnp.concatenate([pad, v], axis=1).reshape(B, nblk + 1, WINDOW, ATT_KV_HEADS, ATT_HEAD_DIM)
    k_prev, k_cur = jnp.moveaxis(k_ext[:, :-1], 1, 0), jnp.moveaxis(k_ext[:, 1:], 1, 0)
    v_prev, v_cur = jnp.moveaxis(v_ext[:, :-1], 1, 0), jnp.moveaxis(v_ext[:, 1:], 1, 0)
    q_blk = jnp.moveaxis(q.reshape(B, nblk, WINDOW, ATT_KV_HEADS, ATT_GROUP, ATT_HEAD_DIM), 1, 0)
    starts = jnp.arange(nblk, dtype=jnp.int32) * WINDOW

    def one_block(args):
        qb, kp, kc, vp, vc, s0 = args
        kb = jnp.concatenate([kp, kc], axis=1)
        vb = jnp.concatenate([vp, vc], axis=1)
        q_pos = s0 + jnp.arange(WINDOW, dtype=jnp.int32)
        k_pos = s0 - WINDOW + jnp.arange(2 * WINDOW, dtype=jnp.int32)
        return _swa_attend(qb, kb, vb, q_pos, k_pos, rel_bias, sink)

    o = lax.map(one_block, (q_blk, k_prev, k_cur, v_prev, v_cur, starts))
    o = jnp.moveaxis(o, 0, 1).reshape(B, S, ATT_WIDTH)
    y = _gated_out(o, g, w_out)
    return y, k[:, S - WINDOW:], v[:, S - WINDOW:]


def _attn_sample(h, ck, cv, w_in, sink, w_out, rel_bias):
    B, T, _ = h.shape
    q, k, v, g = _attn_project(h, w_in)
    kb = jnp.concatenate([ck.astype(k.dtype), k], axis=1)
    vb = jnp.concatenate([cv.astype(v.dtype), v], axis=1)
    q_pos = PAST_LEN + jnp.arange(T, dtype=jnp.int32)
    k_pos = PAST_LEN - WINDOW + jnp.arange(WINDOW + T, dtype=jnp.int32)
    o = _swa_attend(q, kb, vb, q_pos, k_pos, rel_bias, sink).reshape(B, T, ATT_WIDTH)
    y = _gated_out(o, g, w_out)
    return y, kb[:, T:].astype(ck.dtype), vb[:, T:].astype(cv.dtype)


def _log_gamma():
    return jnp.log1p(-jnp.exp2(-5.0 - jnp.arange(RET_HEADS, dtype=jnp.float32)))


def _rotary(x, pos):
    half = RET_QK_DIM // 2
    inv = ROPE_BASE ** (-jnp.arange(half, dtype=jnp.float32) / half)
    ang = pos.astype(jnp.float32)[:, None] * inv[None, :]
    cos = jnp.cos(ang)[None, :, None, :]
    sin = jnp.sin(ang)[None, :, None, :]
    x1, x2 = x[..., :half], x[..., half:]
    return jnp.concatenate([x1 * cos - x2 * sin, x1 * sin + x2 * cos], axis=-1)


def _ret_project(h, w_in, pos):
    B, T, _ = h.shape
    proj = h @ w_in
    q, k, v, g = jnp.split(proj, [RET_QK_WIDTH, 2 * RET_QK_WIDTH, 2 * RET_QK_WIDTH + RET_WIDTH], axis=-1)
    q = _rotary(q.reshape(B, T, RET_HEADS, RET_QK_DIM).astype(jnp.float32), pos)
    k = _rotary(k.reshape(B, T, RET_HEADS, RET_QK_DIM).astype(jnp.float32), pos) * (RET_QK_DIM ** -0.5)
    v = v.reshape(B, T, RET_HEADS, RET_V_DIM).astype(jnp.float32)
    return q, k, v, g


def _retention_chunk(q, k, v, R, log_gamma):
    C = q.shape[1]
    idx = jnp.arange(C, dtype=jnp.float32)
    diff = idx[:, None] - idx[None, :]
    decay = jnp.where(diff >= 0, jnp.exp(jnp.maximum(diff, 0.0)[None] * log_gamma[:, None, None]), 0.0)
    scores = jnp.einsum('bqhd,bshd->bhqs', q, k) * decay
    o = jnp.einsum('bhqs,bshv->bqhv', scores, v)
    cross = jnp.exp((idx[:, None] + 1.0) * log_gamma[None, :])
    o = o + jnp.einsum('bqhd,bhdv->bqhv', q, R) * cross[None, :, :, None]
    k_dec = k * jnp.exp((C - 1.0 - idx)[:, None] * log_gamma[None, :])[None, :, :, None]
    R_new = jnp.exp(C * log_gamma)[None, :, None, None] * R + jnp.einsum('bshd,bshv->bhdv', k_dec, v)
    return o, R_new


def _ret_output(o, g, w_out):
    B, T = o.shape[0], o.shape[1]
    mu = jnp.mean(o, axis=-1, keepdims=True)
    var = jnp.mean(jnp.square(o - mu), axis=-1, keepdims=True)
    o = ((o - mu) * lax.rsqrt(var + EPS)).reshape(B, T, RET_WIDTH)
    return _gated_out(o, g, w_out)


def _ret_prompt(h, w_in, w_out):
    B, S, _ = h.shape
    lg = _log_gamma()
    q, k, v, g = _ret_project(h, w_in, jnp.arange(S, dtype=jnp.int32))
    nch = S // RET_CHUNK

    def to_chunks(t):
        return jnp.moveaxis(t.reshape(B, nch, RET_CHUNK, *t.shape[2:]), 1, 0)

    def step(R, xs):
        qc, kc, vc = xs
        o, R_new = _retention_chunk(qc, kc, vc, R, lg)
        return R_new, o

    R0 = jnp.zeros((B, RET_HEADS, RET_QK_DIM, RET_V_DIM), jnp.float32)
    R_fin, o = lax.scan(step, R0, (to_chunks(q), to_chunks(k), to_chunks(v)))
    o = jnp.moveaxis(o, 0, 1).reshape(B, S, RET_HEADS, RET_V_DIM)
    return _ret_output(o, g, w_out), R_fin.astype(h.dtype)


def _ret_sample(h, state, w_in, w_out):
    T = h.shape[1]
    lg = _log_gamma()
    q, k, v, g = _ret_project(h, w_in, PAST_LEN + jnp.arange(T, dtype=jnp.int32))
    o, R_new = _retention_chunk(q, k, v, state.astype(jnp.float32), lg)
    return _ret_output(o, g, w_out), R_new.astype(state.dtype)


def setup_inputs(seed: int = 0) -> dict:
    key = jax.random.key(seed)
    ks = jax.random.split(key, 14)
    f32 = jnp.float32
    x_prompt = jax.random.normal(ks[0], (BATCH, SEQ, D_MODEL), f32)
    x_sample = jax.random.normal(ks[1], (DEC_BATCH, DEC_SEQ, D_MODEL), f32)
    cache_swa_k = jax.random.normal(ks[2], (N_ATTN_LAYERS, DEC_BATCH, WINDOW, ATT_KV_HEADS, ATT_HEAD_DIM), f32)
    cache_swa_v = jax.random.normal(ks[3], (N_ATTN_LAYERS, DEC_BATCH, WINDOW, ATT_KV_HEADS, ATT_HEAD_DIM), f32)
    state_ret = 0.5 * jax.random.normal(ks[4], (N_RET_LAYERS, DEC_BATCH, RET_HEADS, RET_QK_DIM, RET_V_DIM), f32)
    norm_g = 1.0 + 0.02 * jax.random.normal(ks[5], (DEPTH, D_MODEL), f32)
    final_norm_g = 1.0 + 0.02 * jax.random.normal(ks[6], (D_MODEL,), f32)
    rel_bias = 0.5 * jax.random.normal(ks[7], (NUM_BUCKETS, ATT_HEADS), f32)
    w_in_attn = jax.random.normal(ks[8], (N_ATTN_LAYERS, D_MODEL, ATT_IN), f32) * D_MODEL ** -0.5
    attn_sinks = 0.5 * jax.random.normal(ks[9], (N_ATTN_LAYERS, ATT_HEADS), f32)
    w_out_attn = jax.random.normal(ks[10], (N_ATTN_LAYERS, ATT_WIDTH, D_MODEL), f32) * ATT_WIDTH ** -0.5
    w_in_ret = jax.random.normal(ks[11], (N_RET_LAYERS, D_MODEL, RET_IN), f32) * D_MODEL ** -0.5
    w_out_ret = jax.random.normal(ks[12], (N_RET_LAYERS, RET_WIDTH, D_MODEL), f32) * RET_WIDTH ** -0.5
    return {'x_prompt': x_prompt, 'x_sample': x_sample,
            'cache_swa_k': cache_swa_k, 'cache_swa_v': cache_swa_v, 'state_ret': state_ret,
            'norm_g': norm_g, 'final_norm_g': final_norm_g, 'rel_bias': rel_bias,
            'w_in_attn': w_in_attn, 'attn_sinks': attn_sinks, 'w_out_attn': w_out_attn,
            'w_in_ret': w_in_ret, 'w_out_ret': w_out_ret}


def reference(x_prompt, x_sample, cache_swa_k, cache_swa_v, state_ret, norm_g, final_norm_g, rel_bias,
              w_in_attn, attn_sinks, w_out_attn, w_in_ret, w_out_ret):
    xp, xs = x_prompt, x_sample
    kp_l, vp_l, rp_l, ks_l, vs_l, rs_l = [], [], [], [], [], []
    for i in range(DEPTH):
        hp = _rmsnorm(xp, norm_g[i])
        hs = _rmsnorm(xs, norm_g[i])
        j = i // N_MIXERS
        if i % N_MIXERS == 0:
            yp, kp, vp = _attn_prompt(hp, w_in_attn[j], attn_sinks[j], w_out_attn[j], rel_bias)
            ys, kn, vn = _attn_sample(hs, cache_swa_k[j], cache_swa_v[j], w_in_attn[j], attn_sinks[j],
                                      w_out_attn[j], rel_bias)
            kp_l.append(kp); vp_l.append(vp); ks_l.append(kn); vs_l.append(vn)
        else:
            yp, rp = _ret_prompt(hp, w_in_ret[j], w_out_ret[j])
            ys, rn = _ret_sample(hs, state_ret[j], w_in_ret[j], w_out_ret[j])
            rp_l.append(rp); rs_l.append(rn)
        xp = xp + yp.astype(xp.dtype)
        xs = xs + ys.astype(xs.dtype)
    y_prompt = _rmsnorm(xp, final_norm_g)
    y_sample = _rmsnorm(xs, final_norm_g)
    return (y_prompt, y_sample, jnp.stack(kp_l), jnp.stack(vp_l), jnp.stack(rp_l),
            jnp.stack(ks_l), jnp.stack(vs_l), jnp.stack(rs_l))
```

```python
import math
import types
from contextlib import ExitStack

import numpy as np
import ml_dtypes

import concourse.bass as bass
import concourse.mybir as mybir
from concourse.bass_utils import run_bass_kernel_spmd

F32 = mybir.dt.float32
BF16 = mybir.dt.bfloat16
AF = mybir.ActivationFunctionType
ALU = mybir.AluOpType

NCORES = 8
D = 1024
SEQ = 2048
NB = SEQ // 128
NSEQ = 2
SB = 16
EPS = 1e-6
PAST = 16384
NEG = -1.0e30
ROPE_BASE = 10000.0
ALEN = 130 * 256


def _freeze(fn):
    if fn.__closure__ is None:
        return fn
    cells = []
    for c in fn.__closure__:
        try:
            cells.append(types.CellType(c.cell_contents))
        except ValueError:
            cells.append(c)
    return types.FunctionType(fn.__code__, fn.__globals__, fn.__name__, fn.__defaults__, tuple(cells))


class Prog:
    def __init__(self, nc, es):
        self.nc = nc
        self.es = es
        self.ops = []
        self.muted = False
        self.engs = {"pe": nc.tensor, "dve": nc.vector, "act": nc.scalar, "pool": nc.gpsimd, "sp": nc.sync}

    def op(self, eng, fn, r=(), w=(), tiny=False):
        if self.muted:
            return
        self.ops.append(("c", eng, _freeze(fn), tuple(r), tuple(w), "tiny" if tiny else None))

    def dma(self, q, fn, r=(), w=(), key=None):
        assert key is not None
        if self.muted:
            return
        self.ops.append(("d", q, _freeze(fn), tuple(r), tuple(w), key))

    def barrier(self):
        if self.muted:
            return
        self.ops.append(("b", None, None, (), (), None))

    def emit(self):
        ops = self.ops
        n = len(ops)
        tl = [None] * n
        for i, o in enumerate(ops):
            if o[0] == "c":
                tl[i] = o[1]
            elif o[0] == "d":
                tl[i] = "dma:" + o[5]
        last_w = {}
        readers = {}
        last_on_tl = {}
        bar_deps = {}
        need = [None] * n
        signaling = [False] * n
        for i, o in enumerate(ops):
            kind, eng, fn, rs, ws, key = o
            if kind == "b":
                bar_deps = dict(last_on_tl)
                continue
            deps = {}

            def add(j):
                t = tl[j]
                if j > deps.get(t, -1):
                    deps[t] = j

            for b in rs:
                if b in last_w:
                    add(last_w[b])
            for b in ws:
                if b in last_w:
                    add(last_w[b])
                for j in readers.get(b, {}).values():
                    add(j)
            for j in bar_deps.values():
                add(j)
            nd = {}
            for t, j in deps.items():
                if kind == "c" and t == eng and ops[j][5] != "tiny":
                    continue
                nd[t] = j
                if ops[j][0] == "c":
                    signaling[j] = True
            need[i] = nd
            for b in rs:
                readers.setdefault(b, {})[tl[i]] = i
            for b in ws:
                last_w[b] = i
                readers[b] = {}
            last_on_tl[tl[i]] = i
        val = [0] * n
        cnt = {}
        for i, o in enumerate(ops):
            if o[0] == "c":
                if signaling[i]:
                    cnt[o[1]] = cnt.get(o[1], 0) + 1
                    val[i] = cnt[o[1]]
            elif o[0] == "d":
                cnt[tl[i]] = cnt.get(tl[i], 0) + 16
                val[i] = cnt[tl[i]]
        sems = {}
        for t in cnt:
            sems[t] = self.es.enter_context(self.nc.semaphore("s_" + t.replace(":", "_")))
        self.nsem = len(sems)
        waited = {e: {} for e in self.engs}
        for i, o in enumerate(ops):
            kind, eng, fn, rs, ws, key = o
            if kind == "b":
                continue
            e = self.engs[eng]
            for t, j in need[i].items():
                v = val[j]
                if waited[eng].get(t, 0) >= v:
                    continue
                e.wait_ge(sems[t], v)
                waited[eng][t] = v
            inst = fn()
            if kind == "d":
                inst.then_inc(sems[tl[i]], 16)
            elif signaling[i]:
                inst.then_inc(sems[eng], 1)
        sp = self.nc.sync
        for t, c in cnt.items():
            if waited["sp"].get(t, 0) < c:
                sp.wait_ge(sems[t], c)
        self.counts = cnt
        self.vals = val
        self.sig = signaling
        self.need = need


def _t5_bucket_np(rel):
    n = np.maximum(rel, 0)
    nf = np.maximum(n, 1).astype(np.float32)
    large = 16 + (np.log(nf / np.float32(16)) / np.float32(math.log(128 / 16)) * np.float32(16)).astype(np.int32)
    large = np.minimum(large, 31)
    return np.where(n < 16, n, large)


def _host_consts():
    c = {}
    c["ident"] = np.eye(128, dtype=np.float32).astype(ml_dtypes.bfloat16)
    E = np.zeros((33, 2, 256), np.float32)
    m = np.arange(256)
    rel_cur = m - 127
    ok_cur = (rel_cur >= 0) & (rel_cur <= 127)
    bc = _t5_bucket_np(rel_cur)
    rel_prev = m + 1
    ok_prev = m <= 126
    bp = _t5_bucket_np(rel_prev)
    for mm in range(256):
        if ok_prev[mm]:
            E[bp[mm], 0, mm] = 1.0
        else:
            E[32, 0, mm] = 1.0
        if ok_cur[mm]:
            E[bc[mm], 1, mm] = 1.0
        else:
            E[32, 1, mm] = 1.0
    c["etab"] = E.astype(ml_dtypes.bfloat16)
    Es = np.zeros((32, 128), np.float32)
    bs = _t5_bucket_np(127 - np.arange(128))
    Es[bs, np.arange(128)] = 1.0
    c["etab_s"] = Es.astype(ml_dtypes.bfloat16)
    jj = np.arange(128)[:, None]
    ii = np.arange(128)[None, :]
    c["maskT"] = (ii >= jj).astype(np.float32)
    half = 128
    inv = (np.float32(ROPE_BASE) ** (-np.arange(half, dtype=np.float32) / np.float32(half))).astype(np.float32)
    pos = np.arange(SEQ, dtype=np.float32)
    ang = pos[None, :] * inv[:, None]
    cos = np.cos(ang).astype(np.float32)
    sin = np.sin(ang).astype(np.float32)
    lg = np.log1p(-np.exp2(-5.0 - np.arange(4, dtype=np.float64)))
    tin = (np.arange(SEQ) % 128).astype(np.float64)
    rot = np.zeros((4, 4, 4, 128, 512), np.float32)
    for h in range(4):
        fq = np.exp((tin + 1.0) * lg[h])
        fk = np.exp(-(tin + 1.0) * lg[h]) / 16.0
        tabs = [cos * fq[None, :], sin * fq[None, :], cos * fk[None, :], sin * fk[None, :]]
        for t in range(4):
            rot[h, :, t] = tabs[t].astype(np.float32).reshape(128, 4, 512).transpose(1, 0, 2)
    c["rot"] = rot
    c["g128"] = [float(np.exp(128.0 * lg[h])) for h in range(4)]
    c["g1"] = [float(np.exp(lg[h])) for h in range(4)]
    angs = np.float32(PAST) * inv
    rs = np.stack([np.cos(angs), np.sin(angs)], axis=1).astype(np.float32)
    c["rot_s"] = np.concatenate([rs, rs / np.float32(16.0)], axis=1).astype(np.float32)
    c["eyeq"] = np.tile(np.eye(16, dtype=np.float32).reshape(1, 256), (128, 1)).astype(ml_dtypes.bfloat16)
    return c


HC = _host_consts()


def build(stage="full"):
    nc = bass.Bass("TRN2", target_bir_lowering=False)
    es = ExitStack()
    P = Prog(nc, es)

    def din(name, shape, dt=F32):
        return nc.dram_tensor(name, list(shape), dt, kind="ExternalInput").ap()

    def dout(name, shape, dt=F32):
        return nc.dram_tensor(name, list(shape), dt, kind="ExternalOutput").ap()

    x_prompt = din("x_prompt", [NSEQ, SEQ, D])
    w_in_attn = din("w_in_attn_r", [4, 128, 8 * 1344])
    w_out_attn = din("w_out_attn_r", [4, 128, 4096])
    w_in_ret = din("w_in_ret_r", [4, 128, 8 * 1536])
    w_out_ret = din("w_out_ret_r", [4, 128, 4096])
    norm_gT = din("norm_gT", [2, 128, 8])
    final_g = din("final_norm_g", [1, D])
    rel_bias = din("rel_bias", [32, 32])
    sinks = din("attn_sinks", [1, 32])
    c_ident = din("c_ident", [128, 128], BF16)
    c_etab = din("c_etab", [33, 2, 256], BF16)
    c_maskT = din("c_maskT", [128, 128])
    c_rot = din("c_rot", [4, 4, 4, 128, 512])

    if stage == "L0":
        din = lambda name, shape, dt=F32: None
        dout_real = dout
        dout = lambda name, shape, dt=F32: None
    x_sample = din("x_sample", [SB, D])
    cache_k = din("cache_k", [SB, 128, 256])
    cache_v = din("cache_v", [SB, 128, 256])
    state_ret = din("state_ret", [SB, 4, 256, 512])
    c_etab_s = din("c_etab_s", [32, 128], BF16)
    c_eyeq = din("c_eyeq", [128, 256], BF16)
    c_rot_s = din("c_rot_s", [128, 4])
    y_sample = dout("y_sample", [SB, D])
    swa_k_s = dout("swa_k_sample", [SB, 128, 256])
    swa_v_s = dout("swa_v_sample", [SB, 128, 256])
    ret_s = dout("ret_state_sample", [SB, 4, 256, 512])
    osamp = nc.dram_tensor("osamp", [4, SB, 512], F32, kind="Internal").ap()
    if stage == "L0":
        dout = dout_real
    y_prompt = dout("y_prompt", [NSEQ, SEQ, D])
    swa_k_p = dout("swa_k_prompt", [NSEQ, 128, 4, 64])
    swa_v_p = dout("swa_v_prompt", [NSEQ, 128, 4, 64])
    ret_p = dout("ret_state_prompt", [NSEQ, 4, 256, 512])
    dbg_x1 = dout("dbg_x1", [NSEQ, SEQ, D]) if stage == "L0" else None

    biasA = nc.dram_tensor("biasA", [2, 32, ALEN], BF16, kind="Internal")

    def sb(name, shape, dt=F32):
        return es.enter_context(nc.sbuf_tensor(name, list(shape), dt))

    def ps(name, shape=(128, 512), dt=F32):
        return es.enter_context(nc.psum_tensor(name, list(shape), dt))

    xres = sb("xres", [128, NB, D])
    hT = sb("hT", [128, 8, SEQ], BF16)
    win = [sb("win%d" % i, [128, 8, 1536], BF16) for i in range(2)]
    wout = [sb("wout0", [128, 4, D], BF16)]
    win_l0 = [w_[:, :, :].rearrange("p k c -> p (k c)")[:, 0:8 * 1344].rearrange("p (k c) -> p k c", k=8) for w_ in win]
    lscr = sb("lscr", [128, 4096])
    work = sb("work", [128, 37 * 256])
    ident = sb("ident", [128, 128], BF16)
    maskT = sb("maskT", [128, 128])
    gT = sb("gT", [128, 2, 8])
    es2 = sb("es2", [128, 32])
    ss = sb("ss", [128, NB])
    rstd = sb("rstd", [128, NB])
    small = sb("small", [128, 64])
    nhalf = sb("nhalf", [128, 2])

    banks = [ps("bank%d" % i) for i in range(8)]

    biasT = lscr[:, :].bitcast(BF16).rearrange("p (c h i) -> p c h i", c=2, h=32)
    rott = [lscr[:, s * 2048:(s + 1) * 2048].rearrange("p (t c) -> p t c", t=4) for s in range(2)]

    class Carver:
        def __init__(self):
            self.off = 0

        def f32(self, n):
            a = work[:, self.off:self.off + n]
            self.off += n
            assert self.off <= 37 * 256, self.off
            return a

        def bf(self, n):
            assert n % 2 == 0
            a = work[:, self.off:self.off + n // 2].bitcast(BF16)
            self.off += n // 2
            assert self.off <= 37 * 256, self.off
            return a

    cA = Carver()
    xs_bf = [cA.bf(1024), cA.bf(1024)]
    offA = cA.off
    c0 = Carver()
    c0.off = offA
    QT = [c0.bf(2048).rearrange("p (q t) -> p q t", q=4) for _ in range(2)]
    KTA = c0.bf(1024)
    KTB = c0.bf(1024)
    V65 = c0.bf(8 * 66).rearrange("p (b d) -> p b d", b=8)
    PT = [[[c0.bf(512).rearrange("p (q t) -> p q t", q=4) for _ in range(2)] for _ in range(2)] for _ in range(2)]
    t0 = c0.f32(512)
    u0 = [c0.f32(512), c0.f32(512)]
    o2 = c0.f32(512)
    og = c0.bf(512)
    ogT = c0.bf(512).rearrange("p (f t) -> p f t", f=4)
    kvout = c0.f32(128)
    c1 = Carver()
    c1.off = offA
    qT = [c1.bf(1024).rearrange("p (c t) -> p c t", c=2) for _ in range(2)]
    kT = [c1.bf(1024).rearrange("p (c t) -> p c t", c=2) for _ in range(2)]
    ra = c1.f32(512)
    rb = c1.f32(512)
    vb = [c1.bf(512), c1.bf(512)]
    t1 = c1.f32(512)
    u1 = [c1.f32(512), c1.f32(512)]
    sT = [c1.bf(128), c1.bf(128)]
    sraw = c1.f32(128)
    kt = [c1.bf(256), c1.bf(256)]
    Rflat = c1.f32(1024)
    Rst = Rflat.rearrange("p (c v) -> p c v", c=2)
    Rbf = c1.bf(1024).rearrange("p (c v) -> p c v", c=2)
    on = c1.f32(512)
    gtd = c1.bf(512)
    gTt = c1.bf(512).rearrange("p (f t) -> p f t", f=4)
    gfin = Rflat

    PP = [0, 1]
    BL = [[2, 3], [4, 5]]
    BO = 6
    BX = 7
    bkey = lambda i: ("ps", i)
    ppc = [0]

    def next_pp():
        b = PP[ppc[0] % len(PP)]
        ppc[0] += 1
        return b

    bx_bf = banks[BX][:, 0:256].bitcast(BF16)
    pT8 = banks[BX][:, :].bitcast(BF16).rearrange("p (k t) -> p k t", k=8)

    wslot = [0]

    def load_win(layer, g):
        s = wslot[0] % 2
        wslot[0] += 1
        flat = win[s][:, :, :].rearrange("p k c -> p (k c)")
        if layer == 0:
            n, src = 8 * 1344, w_in_attn
        else:
            n, src = 8 * 1536, w_in_ret
        P.dma("pool", lambda: nc.gpsimd.dma_start(out=flat[:, 0:n].rearrange("p (a b) -> p a b", a=6),
                                                  in_=src[g].rearrange("p (a b) -> p a b", a=6)),
              w=[("win", s)], key="win%d" % s)
        return s

    def load_wout(layer, g):
        wsrc = w_out_attn if layer == 0 else w_out_ret
        P.dma("pool", lambda: nc.gpsimd.dma_start(out=wout[0][:, :, :].rearrange("p f c -> p (f c)").rearrange("p (a b) -> p a b", a=2),
                                                  in_=wsrc[g].rearrange("p (a b) -> p a b", a=2)),
              w=[("wout", 0)], key="wout0")

    early_slot = None
    if stage != "L0":
        early_slot = load_win(0, 0)
        load_wout(0, 0)
    P.op("pool", lambda: nc.gpsimd.memset(nhalf[:, :], -0.5), w=["nhalf"])
    P.dma("sp", lambda: nc.sync.dma_start(out=ident[:, :], in_=c_ident[:, :]), w=["ident"], key="const")
    P.dma("sp", lambda: nc.sync.dma_start(out=maskT[:, :], in_=c_maskT[:, :]), w=["maskT"], key="const")
    P.dma("sp", lambda: nc.sync.dma_start(out=gT[:, :, :], in_=norm_gT.rearrange("l p k -> p l k")), w=["gT"], key="const")
    P.dma("sp", lambda: nc.sync.dma_start(out=es2[:, :], in_=sinks[0:1, :].to_broadcast([128, 32])), w=["es2"], key="const")
    rb33 = work[0:33, 0:32]
    rb33b = work[0:33, 32:48].bitcast(BF16)
    etab = work[0:33, 64:320].bitcast(BF16).rearrange("p (c m) -> p c m", c=2)
    tsb = work[0:32, 320:576].bitcast(BF16).rearrange("p (c m) -> p c m", c=2)
    P.op("dve", lambda: nc.vector.memset(work[0:64, 0:32], NEG), w=["rb33"], tiny=True)
    P.dma("sp", lambda: nc.sync.dma_start(out=work[0:32, 0:32], in_=rel_bias[:, :]), r=[], w=["rb33"], key="const")
    P.dma("sp", lambda: nc.sync.dma_start(out=etab, in_=c_etab[:, :, :]), w=["etab"], key="const")
    P.barrier()
    P.op("act", lambda: nc.scalar.activation(out=es2[:, :], in_=es2[:, :], func=AF.Exp), r=["es2"], w=["es2"], tiny=True)
    P.op("dve", lambda: nc.vector.tensor_scalar(out=es2[:, :], in0=es2[:, :], scalar1=2.0, scalar2=None, op0=ALU.mult),
         r=["es2"], w=["es2"], tiny=True)
    P.op("dve", lambda: nc.vector.tensor_scalar(out=rb33b, in0=rb33, scalar1=8.0, scalar2=None, op0=ALU.mult),
         r=["rb33"], w=["rb33b"], tiny=True)

    def mk_tp():
        nc.tensor.matmul(banks[0][0:32, 0:256], lhsT=rb33b, rhs=etab[:, 0, :], start=True, stop=True)
        return nc.tensor.matmul(banks[0][0:32, 256:512], lhsT=rb33b, rhs=etab[:, 1, :], start=True, stop=True)

    P.op("pe", mk_tp, r=["rb33b", "etab"], w=[bkey(0)])
    P.op("dve", lambda: nc.vector.tensor_copy(out=tsb, in_=banks[0][0:32, :].rearrange("p (c m) -> p c m", c=2)),
         r=[bkey(0)], w=["tsb"])
    def biasA_dmas():
        for pc in range(2):
            P.dma("sp", lambda pc=pc: nc.sync.dma_start(
                out=biasA.ap()[pc].rearrange("h (r m) -> h r m", m=256),
                in_=tsb[:, pc, :].unsqueeze(1).to_broadcast([32, 130, 256])), r=["tsb"], w=["biasA"], key="biasA")

    if stage == "L0":
        biasA_dmas()
    P.barrier()

    def phase_a(layer):
        P.op("pool", lambda: nc.gpsimd.memset(ss[:, :], 0.0), w=["ss"] + [("ss", b) for b in range(NB)])
        for b in range(NB):
            s = b % 2
            P.op("act", lambda b=b, s=s: nc.scalar.activation(out=xs_bf[s], in_=xres[:, b, :], func=AF.Square,
                                                               accum_out=ss[:, b:b + 1]),
                 r=[("x", b), "ss"], w=[("xs", s), ("ss", b)])
            P.op("pool", lambda b=b: nc.gpsimd.tensor_scalar(out=rstd[:, b:b + 1], in0=ss[:, b:b + 1], scalar1=1.0 / D,
                                                              scalar2=float(EPS), op0=ALU.mult, op1=ALU.add),
                 r=[("ss", b)], w=[("rstd", b)], tiny=True)
            P.op("pool", lambda b=b: nc.gpsimd.tensor_tensor(out=rstd[:, b:b + 1], in0=rstd[:, b:b + 1], in1=nhalf[:, 0:1],
                                                              op=ALU.pow),
                 r=[("rstd", b), "nhalf"], w=[("rstd", b)], tiny=True)
            P.op("dve", lambda b=b, s=s: nc.vector.tensor_scalar(out=xs_bf[s], in0=xres[:, b, :],
                                                                  scalar1=rstd[:, b:b + 1], scalar2=None,
                                                                  op0=ALU.mult),
                 r=[("x", b), ("rstd", b)], w=[("xs", s)])

            def tr(b=b, s=s):
                for k in range(8):
                    i = nc.tensor.transpose(out=pT8[:, k, :], in_=xs_bf[s][:, k * 128:(k + 1) * 128], identity=ident[:, :])
                return i

            P.op("pe", tr, r=[("xs", s), "ident"], w=[bkey(BX)])
            P.op("dve", lambda b=b, layer=layer: nc.vector.tensor_tensor(
                out=hT[:, :, b * 128:(b + 1) * 128], in0=pT8,
                in1=gT[:, layer, :].unsqueeze(2).to_broadcast([128, 8, 128]), op=ALU.mult),
                r=[bkey(BX), "gT"], w=[("hT", b)])

    def l0_group(seq, g, s, last_group_cb):
        wi, wo = win_l0[s], wout[0]
        wk = ("win", s)
        O7 = banks[BO][:, 0:455].rearrange("p (h d) -> p h d", h=7)
        den = small[:, 0:8]
        rr = small[:, 8:16]
        o2v = o2.rearrange("p (h d) -> p h d", h=8)

        def proj(sbi):
            ring = sbi % 2
            QTc = QT[sbi % 2]
            toks = slice(sbi * 512, (sbi + 1) * 512)
            hkeys = [("hT", sbi * 4 + i) for i in range(4)]
            for p in range(4):
                bk = next_pp()

                def mmq(p=p, bk=bk):
                    for k in range(8):
                        i = nc.tensor.matmul(banks[bk][:, :], lhsT=wi[:, k, p * 128:(p + 1) * 128], rhs=hT[:, k, toks],
                                             start=(k == 0), stop=(k == 7))
                    return i

                P.op("pe", mmq, r=[wk] + hkeys, w=[bkey(bk)])
                P.op("dve", lambda p=p, bk=bk: nc.vector.tensor_copy(out=QTc[:, p, :], in_=banks[bk][:, :]),
                     r=[bkey(bk)], w=[("QT", sbi % 2)])
            for which, (c0_, dst) in enumerate(((512, KTA), (640, KTB))):
                bk = next_pp()

                def mmk(c0_=c0_, bk=bk):
                    for k in range(8):
                        i = nc.tensor.matmul(banks[bk][:, :], lhsT=wi[:, k, c0_:c0_ + 128], rhs=hT[:, k, toks],
                                             start=(k == 0), stop=(k == 7))
                    return i

                P.op("pe", mmk, r=[wk] + hkeys, w=[bkey(bk)])
                P.op("act", lambda dst=dst, bk=bk: nc.scalar.copy(out=dst[:, ring * 512:(ring + 1) * 512], in_=banks[bk][:, :]),
                     r=[bkey(bk)], w=[("KT", which, ring)])
            bk = next_pp()

            def mmv(bk=bk):
                for blk in range(4):
                    b = sbi * 4 + blk
                    for k in range(8):
                        i = nc.tensor.matmul(banks[bk][:, blk * 64:(blk + 1) * 64], lhsT=hT[:, k, b * 128:(b + 1) * 128],
                                             rhs=wi[:, k, 768:832], start=(k == 0), stop=(k == 7))
                return i

            P.op("pe", mmv, r=[wk] + hkeys, w=[bkey(bk)])
            P.op("dve", lambda bk=bk: nc.vector.tensor_copy(
                out=V65[:, ring * 4:(ring + 1) * 4, 0:64], in_=banks[bk][:, 0:256].rearrange("p (b d) -> p b d", b=4)),
                r=[bkey(bk)], w=[("V", ring)])
            if sbi == 3:
                P.op("dve", lambda bk=bk: nc.vector.tensor_copy(out=kvout[:, 64:128], in_=banks[bk][:, 192:256]),
                     r=[bkey(bk)], w=["vout"])
                P.dma("sp", lambda: nc.sync.dma_start(out=swa_v_p[seq, :, g, :], in_=kvout[:, 64:128]), r=["vout"], w=[], key="vout")
                bk2 = next_pp()

                def mmko(bk2=bk2):
                    for k in range(8):
                        i = nc.tensor.matmul(banks[bk2][:, 0:64], lhsT=hT[:, k, 15 * 128:16 * 128], rhs=wi[:, k, 512:576],
                                             start=(k == 0), stop=(k == 7))
                    return i

                P.op("pe", mmko, r=[wk, ("hT", 15)], w=[bkey(bk2)])
                P.op("dve", lambda bk2=bk2: nc.vector.tensor_copy(out=kvout[:, 0:64], in_=banks[bk2][:, 0:64]),
                     r=[bkey(bk2)], w=["kout"])
                P.dma("sp", lambda: nc.sync.dma_start(out=swa_k_p[seq, :, g, :], in_=kvout[:, 0:64]), r=["kout"], w=[], key="kout")

        def stA(b):
            sbi, blk = b // 4, b % 4
            par = b % 2
            QTc = QT[sbi % 2]
            rb_cur = (sbi % 2) * 4 + blk
            rb_prev = (rb_cur - 1) % 8
            pcs = [1] if b == 0 else [0, 1]
            bg = next_pp()

            def mmg():
                for k in range(8):
                    i = nc.tensor.matmul(banks[bg][:, :], lhsT=hT[:, k, b * 128:(b + 1) * 128], rhs=wi[:, k, 832:1344],
                                         start=(k == 0), stop=(k == 7))
                return i

            P.op("pe", mmg, r=[wk, ("hT", b)], w=[bkey(bg)])
            P.op("act", lambda: nc.scalar.activation(out=t0, in_=banks[bg][:, :], func=AF.Tanh, scale=0.5), r=[bkey(bg)], w=["t0"])
            P.op("dve", lambda: nc.vector.scalar_tensor_tensor(out=u0[par], in0=t0, scalar=1.0, in1=banks[bg][:, :],
                                                               op0=ALU.add, op1=ALU.mult), r=[bkey(bg), "t0"], w=[("u0", par)])
            for m in range(2):
                KT = KTA if m == 0 else KTB
                for pc in pcs:
                    bl = BL[m][pc]
                    rbk = rb_prev if pc == 0 else rb_cur

                    def mml(bl=bl, KT=KT, rbk=rbk, pc=pc, m=m):
                        return nc.tensor.matmul(banks[bl][:, :], lhsT=KT[:, rbk * 128:(rbk + 1) * 128],
                                                rhs=QTc[:, :, blk * 128:(blk + 1) * 128], start=True, stop=True)

                    P.op("pe", mml, r=[("QT", sbi % 2), ("KT", m, rbk // 4)], w=[bkey(bl)])
                    P.op("act", lambda bl=bl, m=m, pc=pc: nc.scalar.activation(
                        out=PT[par][m][pc], in_=banks[bl][:, :].rearrange("p (q t) -> p q t", q=4),
                        func=AF.Exp, scale=0.125), r=[bkey(bl)], w=[("PT", par, m, pc)])
                    P.op("dve", lambda m=m, pc=pc: nc.vector.tensor_tensor(
                        out=PT[par][m][pc], in0=PT[par][m][pc], in1=biasT[:, pc, g * 8 + m:g * 8 + 8:2, :], op=ALU.mult),
                        r=[("PT", par, m, pc), "biasT"], w=[("PT", par, m, pc)])

        def stB(b):
            sbi, blk = b // 4, b % 4
            par = b % 2
            rb_cur = (sbi % 2) * 4 + blk
            rb_prev = (rb_cur - 1) % 8
            pcs = [1] if b == 0 else [0, 1]

            def mmpv():
                for hl in range(8):
                    p, m = hl // 2, hl % 2
                    out = banks[BO][:, hl * 65:(hl + 1) * 65] if hl < 7 else banks[BX][:, 256:321]
                    for n_, pc in enumerate(pcs):
                        rbk = rb_prev if pc == 0 else rb_cur
                        i = nc.tensor.matmul(out, lhsT=PT[par][m][pc][:, p, :], rhs=V65[:, rbk, 0:65],
                                             start=(n_ == 0), stop=(n_ == len(pcs) - 1))
                return i

            P.op("pe", mmpv, r=[("PT", par, 0, 0), ("PT", par, 0, 1), ("PT", par, 1, 0), ("PT", par, 1, 1), ("V", 0), ("V", 1)],
                 w=[bkey(BO), bkey(BX)])
            P.op("dve", lambda: nc.vector.scalar_tensor_tensor(
                out=den[:, 0:7], in0=O7[:, :, 64], scalar=2.0, in1=es2[:, g * 8:g * 8 + 7], op0=ALU.mult, op1=ALU.add),
                r=[bkey(BO), "es2"], w=["den7"], tiny=True)
            P.op("dve", lambda: nc.vector.scalar_tensor_tensor(
                out=den[:, 7:8], in0=banks[BX][:, 320:321], scalar=2.0, in1=es2[:, g * 8 + 7:g * 8 + 8],
                op0=ALU.mult, op1=ALU.add), r=[bkey(BX), "es2"], w=["den1"], tiny=True)
            P.op("dve", lambda: nc.vector.reciprocal(out=rr, in_=den), r=["den7", "den1"], w=["rr"], tiny=True)
            P.op("dve", lambda: nc.vector.tensor_tensor(out=o2v[:, 0:7, :], in0=O7[:, :, 0:64],
                                                        in1=rr[:, 0:7].unsqueeze(2).to_broadcast([128, 7, 64]), op=ALU.mult),
                 r=[bkey(BO), "rr"], w=["o2a"])
            P.op("dve", lambda: nc.vector.tensor_scalar(out=o2[:, 448:512], in0=banks[BX][:, 256:320], scalar1=rr[:, 7:8],
                                                        scalar2=None, op0=ALU.mult), r=[bkey(BX), "rr"], w=["o2b"])
            P.op("dve", lambda: nc.vector.tensor_tensor(out=og, in0=o2, in1=u0[par], op=ALU.mult),
                 r=["o2a", "o2b", ("u0", par)], w=["og"])

        def stT(b):
            def tro():
                for f in range(4):
                    i = nc.tensor.transpose(out=bx_bf[:, f * 128:(f + 1) * 128], in_=og[:, f * 128:(f + 1) * 128],
                                            identity=ident[:, :])
                return i

            P.op("pe", tro, r=["og", "ident"], w=[bkey(BX)])
            P.op("dve", lambda: nc.vector.tensor_copy(out=ogT, in_=bx_bf.rearrange("p (f t) -> p f t", f=4)),
                 r=[bkey(BX)], w=["ogT"])

        def stY(b):
            for n_ in range(2):
                by = next_pp()

                def mmy(by=by, n_=n_):
                    for f in range(4):
                        i = nc.tensor.matmul(banks[by][:, :], lhsT=ogT[:, f, :], rhs=wo[:, f, n_ * 512:(n_ + 1) * 512],
                                             start=(f == 0), stop=(f == 3))
                    return i

                P.op("pe", mmy, r=["ogT", ("wout", 0)], w=[bkey(by)])
                P.op("dve", lambda by=by, n_=n_: nc.vector.tensor_tensor(
                    out=xres[:, b, n_ * 512:(n_ + 1) * 512], in0=xres[:, b, n_ * 512:(n_ + 1) * 512],
                    in1=banks[by][:, :], op=ALU.add), r=[bkey(by), ("x", b)], w=[("x", b)])

        proj(0)
        for i in range(NB + 2):
            if 0 <= i - 2 < NB:
                stT(i - 2)
            if i < NB:
                stA(i)
            if 0 <= i - 1 < NB:
                stB(i - 1)
            if 0 <= i - 2 < NB:
                stY(i - 2)
            if i % 4 == 1 and i // 4 + 1 < 4:
                proj(i // 4 + 1)
        last_group_cb()

    def l1_head(seq, h, s, last_cb):
        wi, wo = win[s], wout[0]
        wk = ("win", s)
        G = HC["g128"][h]
        P.op("pool", lambda: nc.gpsimd.memset(Rst, 0.0), w=["R"])
        P.op("pool", lambda: nc.gpsimd.memset(Rbf, 0.0), w=["Rbf"])
        st6 = small[:, 16:22]
        mv = small[:, 22:24]
        rs_ = small[:, 24:25]
        nb_ = small[:, 25:26]
        b3v = banks[BX][:, 256:384].bitcast(BF16)
        kxa = kxb = bkey(BX)

        def proj(sbi, parts=(0, 1)):
            rs = (h * 4 + sbi) % 2
            par = sbi % 2
            toks = slice(sbi * 512, (sbi + 1) * 512)
            hkeys = [("hT", sbi * 4 + i) for i in range(4)]
            if 0 in parts:
                P.dma("sp", lambda: nc.sync.dma_start(out=rott[rs], in_=c_rot[h, sbi].rearrange("t j c -> j t c")),
                      w=[("rot", rs)], key="rot%d" % rs)
            for which, (c0_, dstT) in enumerate(((0, qT[par]), (256, kT[par]))):
                if which not in parts:
                    continue
                b1 = next_pp()
                b2 = next_pp()

                def mmqk(c0_=c0_, b1=b1, b2=b2):
                    for dc, bk in ((0, b1), (1, b2)):
                        for k in range(8):
                            i = nc.tensor.matmul(banks[bk][:, :], lhsT=wi[:, k, c0_ + dc * 128:c0_ + (dc + 1) * 128],
                                                 rhs=hT[:, k, toks], start=(k == 0), stop=(k == 7))
                    return i

                P.op("pe", mmqk, r=[wk] + hkeys, w=[bkey(b1), bkey(b2)])
                C = rott[rs][:, 2 * which, :]
                S = rott[rs][:, 2 * which + 1, :]

                def rot(b1=b1, b2=b2, C=C, S=S, dstT=dstT):
                    nc.vector.tensor_tensor(out=ra, in0=banks[b1][:, :], in1=C, op=ALU.mult)
                    nc.vector.tensor_tensor(out=rb, in0=banks[b2][:, :], in1=S, op=ALU.mult)
                    nc.vector.tensor_tensor(out=dstT[:, 0, :], in0=ra, in1=rb, op=ALU.subtract)
                    nc.vector.tensor_tensor(out=ra, in0=banks[b1][:, :], in1=S, op=ALU.mult)
                    nc.vector.tensor_tensor(out=rb, in0=banks[b2][:, :], in1=C, op=ALU.mult)
                    return nc.vector.tensor_tensor(out=dstT[:, 1, :], in0=ra, in1=rb, op=ALU.add)

                P.op("dve", rot, r=[bkey(b1), bkey(b2), ("rot", rs)], w=[("qk", which, par), "rab"])

        def stA(b):
            sbi, blk = b // 4, b % 4
            par, sp_ = b % 2, sbi % 2
            cs = slice(blk * 128, (blk + 1) * 128)

            bv = next_pp()

            def mmv():
                for k in range(8):
                    i = nc.tensor.matmul(banks[bv][:, :], lhsT=hT[:, k, b * 128:(b + 1) * 128], rhs=wi[:, k, 512:1024],
                                         start=(k == 0), stop=(k == 7))
                return i

            P.op("pe", mmv, r=[wk, ("hT", b)], w=[bkey(bv)])
            P.op("act", lambda: nc.scalar.copy(out=vb[par], in_=banks[bv][:, :]), r=[bkey(bv)], w=[("v", par)])
            bg = next_pp()

            def mmg():
                for k in range(8):
                    i = nc.tensor.matmul(banks[bg][:, :], lhsT=hT[:, k, b * 128:(b + 1) * 128], rhs=wi[:, k, 1024:1536],
                                         start=(k == 0), stop=(k == 7))
                return i

            P.op("pe", mmg, r=[wk, ("hT", b)], w=[bkey(bg)])
            P.op("act", lambda: nc.scalar.activation(out=t1, in_=banks[bg][:, :], func=AF.Tanh, scale=0.5), r=[bkey(bg)], w=["t1"])
            P.op("dve", lambda: nc.vector.scalar_tensor_tensor(out=u1[par], in0=t1, scalar=1.0, in1=banks[bg][:, :],
                                                               op0=ALU.add, op1=ALU.mult), r=[bkey(bg), "t1"], w=[("u1", par)])

            def mms():
                nc.tensor.matmul(banks[BX][:, 384:512], lhsT=kT[sp_][:, 0, cs], rhs=qT[sp_][:, 0, cs], start=True, stop=False)
                return nc.tensor.matmul(banks[BX][:, 384:512], lhsT=kT[sp_][:, 1, cs], rhs=qT[sp_][:, 1, cs], start=False, stop=True)

            P.op("pe", mms, r=[("qk", 0, sp_), ("qk", 1, sp_)], w=[kxb])
            P.op("act", lambda: nc.scalar.copy(out=sraw, in_=banks[BX][:, 384:512]), r=[kxb], w=["sraw"])
            P.op("pool", lambda: nc.gpsimd.tensor_tensor(out=sT[par], in0=sraw, in1=maskT[:, :], op=ALU.mult),
                 r=["sraw", "maskT"], w=[("sT", par)])

            def trk():
                nc.tensor.transpose(out=b3v[:, 0:128], in_=kT[sp_][:, 0, cs], identity=ident[:, :])
                return nc.tensor.transpose(out=b3v[:, 128:256], in_=kT[sp_][:, 1, cs], identity=ident[:, :])

            P.op("pe", trk, r=[("qk", 1, sp_), "ident"], w=[kxb])
            P.op("act", lambda: nc.scalar.activation(out=kt[par], in_=b3v, func=AF.Identity, scale=G), r=[kxb], w=[("kt", par)])

        def stB(b):
            sbi, blk = b // 4, b % 4
            par, sp_ = b % 2, sbi % 2
            cs = slice(blk * 128, (blk + 1) * 128)

            def mmo():
                nc.tensor.matmul(banks[4][:, :], lhsT=sT[par], rhs=vb[par], start=True, stop=False)
                nc.tensor.matmul(banks[4][:, :], lhsT=qT[sp_][:, 0, cs], rhs=Rbf[:, 0, :], start=False, stop=False)
                return nc.tensor.matmul(banks[4][:, :], lhsT=qT[sp_][:, 1, cs], rhs=Rbf[:, 1, :], start=False, stop=True)

            P.op("pe", mmo, r=[("sT", par), ("v", par), ("qk", 0, sp_), "Rbf"], w=[bkey(4)])

            def mmu():
                nc.tensor.matmul(banks[5][:, :], lhsT=kt[par][:, 0:128], rhs=vb[par], start=True, stop=True)
                return nc.tensor.matmul(banks[6][:, :], lhsT=kt[par][:, 128:256], rhs=vb[par], start=True, stop=True)

            P.op("pe", mmu, r=[("kt", par), ("v", par)], w=[bkey(5), bkey(6)])

            def upd():
                nc.vector.scalar_tensor_tensor(out=Rst[:, 0, :], in0=Rst[:, 0, :], scalar=G, in1=banks[5][:, :],
                                               op0=ALU.mult, op1=ALU.add)
                return nc.vector.scalar_tensor_tensor(out=Rst[:, 1, :], in0=Rst[:, 1, :], scalar=G, in1=banks[6][:, :],
                                                      op0=ALU.mult, op1=ALU.add)

            P.op("dve", upd, r=[bkey(5), bkey(6), "R"], w=["R"])
            if b < NB - 1:
                P.op("act", lambda: nc.scalar.copy(out=Rbf, in_=Rst), r=["R"], w=["Rbf"])
            else:
                P.dma("sp", lambda: nc.sync.dma_start(out=ret_p[seq, h].rearrange("(c p) v -> p c v", p=128), in_=Rst),
                      r=["R"], w=[], key="rout")
            P.op("dve", lambda: nc.vector.bn_stats(out=st6, in_=banks[4][:, :]), r=[bkey(4)], w=["st6"], tiny=True)
            P.op("dve", lambda: nc.vector.bn_aggr(out=mv, in_=st6), r=["st6"], w=["mv"], tiny=True)
            P.op("pool", lambda: nc.gpsimd.tensor_scalar(out=rs_, in0=mv[:, 1:2], scalar1=4.0, scalar2=float(4.0 * EPS),
                                                         op0=ALU.mult, op1=ALU.add), r=["mv"], w=["rs_"], tiny=True)
            P.op("pool", lambda: nc.gpsimd.tensor_tensor(out=rs_, in0=rs_, in1=nhalf[:, 0:1], op=ALU.pow),
                 r=["rs_", "nhalf"], w=["rs_"], tiny=True)
            P.op("pool", lambda: nc.gpsimd.tensor_scalar(out=nb_, in0=mv[:, 0:1], scalar1=rs_, scalar2=-1.0,
                                                         op0=ALU.mult, op1=ALU.mult), r=["mv", "rs_"], w=["nb_"], tiny=True)

        def stB2(b):
            par = b % 2
            P.op("act", lambda: nc.scalar.activation(out=on, in_=banks[4][:, :], func=AF.Identity, scale=rs_, bias=nb_),
                 r=[bkey(4), "rs_", "nb_"], w=["on"])
            P.op("dve", lambda: nc.vector.tensor_tensor(out=gtd, in0=on, in1=u1[par], op=ALU.mult),
                 r=["on", ("u1", par)], w=["gtd"])

        def stT(b):
            def trg():
                for f in range(4):
                    i = nc.tensor.transpose(out=bx_bf[:, f * 128:(f + 1) * 128], in_=gtd[:, f * 128:(f + 1) * 128],
                                            identity=ident[:, :])
                return i

            P.op("pe", trg, r=["gtd", "ident"], w=[kxa])
            P.op("act", lambda: nc.scalar.copy(out=gTt, in_=bx_bf.rearrange("p (f t) -> p f t", f=4)), r=[kxa], w=["gTt"])

        def stY(b):
            for n_ in range(2):
                by = next_pp()

                def mmy(by=by, n_=n_):
                    for f in range(4):
                        i = nc.tensor.matmul(banks[by][:, :], lhsT=gTt[:, f, :], rhs=wo[:, f, n_ * 512:(n_ + 1) * 512],
                                             start=(f == 0), stop=(f == 3))
                    return i

                P.op("pe", mmy, r=["gTt", ("wout", 0)], w=[bkey(by)])
                P.op("dve", lambda by=by, n_=n_: nc.vector.tensor_tensor(
                    out=xres[:, b, n_ * 512:(n_ + 1) * 512], in0=xres[:, b, n_ * 512:(n_ + 1) * 512],
                    in1=banks[by][:, :], op=ALU.add), r=[bkey(by), ("x", b)], w=[("x", b)])

        proj(0)
        for i in range(NB + 3):
            if 0 <= i - 3 < NB:
                stT(i - 3)
            if 0 <= i - 2 < NB:
                stB2(i - 2)
            if i < NB:
                stA(i)
            if 0 <= i - 1 < NB:
                stB(i - 1)
            if 0 <= i - 3 < NB:
                stY(i - 3)
            if i % 4 == 1 and i // 4 + 1 < 4:
                proj(i // 4 + 1, parts=(0,))
            if i % 4 == 2 and i // 4 + 1 < 4:
                proj(i // 4 + 1, parts=(1,))
        last_cb()

    def load_x(seq, b, q="sp"):
        eng = nc.sync if q == "sp" else nc.gpsimd
        P.dma(q, lambda: eng.dma_start(out=xres[:, b, :], in_=x_prompt[seq, b * 128:(b + 1) * 128, :]),
              w=[("x", b)], key="x%d" % b)

    def final_norm(seq):
        P.op("pool", lambda: nc.gpsimd.memset(ss[:, :], 0.0), w=["ss"] + [("ss", b) for b in range(NB)])
        for b in range(NB):
            P.op("act", lambda b=b: nc.scalar.activation(out=xs_bf[0], in_=xres[:, b, :], func=AF.Square, accum_out=ss[:, b:b + 1]),
                 r=[("x", b), "ss"], w=[("xs", 0), ("ss", b)])
            P.op("pool", lambda b=b: nc.gpsimd.tensor_scalar(out=rstd[:, b:b + 1], in0=ss[:, b:b + 1], scalar1=1.0 / D,
                                                              scalar2=float(EPS), op0=ALU.mult, op1=ALU.add),
                 r=[("ss", b)], w=[("rstd", b)], tiny=True)
            P.op("pool", lambda b=b: nc.gpsimd.tensor_tensor(out=rstd[:, b:b + 1], in0=rstd[:, b:b + 1], in1=nhalf[:, 0:1],
                                                              op=ALU.pow),
                 r=[("rstd", b), "nhalf"], w=[("rstd", b)], tiny=True)
            P.op("dve", lambda b=b: nc.vector.scalar_tensor_tensor(out=xres[:, b, :], in0=xres[:, b, :], scalar=rstd[:, b:b + 1],
                                                                   in1=gfin, op0=ALU.mult, op1=ALU.mult),
                 r=[("x", b), ("rstd", b), "gfin"], w=[("x", b)])
            P.dma("sp", lambda b=b: nc.sync.dma_start(out=y_prompt[seq, b * 128:(b + 1) * 128, :], in_=xres[:, b, :]),
                  r=[("x", b)], w=[], key="x%d" % b)
        if seq + 1 < NSEQ:
            for b in range(NB):
                load_x(seq + 1, b, q="pool")


    def sample_phase():
        hw = hT[:, :, :].rearrange("p k t -> p (k t)").bitcast(F32)
        off = [0]

        def f32(n):
            a = hw[:, off[0]:off[0] + n]
            off[0] += n
            assert off[0] <= 8192
            return a

        def bf(n):
            a = hw[:, off[0]:off[0] + n // 2].bitcast(BF16)
            off[0] += n // 2
            assert off[0] <= 8192
            return a

        xs_t = f32(1024)
        xsb = bf(1024)
        hsT = bf(128).rearrange("p (k t) -> p k t", k=8)
        us = f32(512)
        os_ = f32(512)
        t_s = f32(512)
        ogs = bf(512)
        gsT = bf(64).rearrange("p (f t) -> p f t", f=4)
        QTs = bf(64).rearrange("p (q t) -> p q t", q=4)
        knv = f32(128)
        Kpad2 = [bf(192), bf(192)]
        KTp2 = [bf(256).rearrange("p (m j) -> p m j", m=2) for _ in range(2)]
        PTs2 = [bf(8), bf(8)]
        biasS = bf(32)
        etS = bf(128)
        es2s = f32(8).rearrange("p (k m) -> p k m", k=4)
        sm = f32(16)
        ob2 = [f32(128).rearrange("p (m d) -> p m d", m=2) for _ in range(2)]
        qk4 = f32(64).rearrange("p (i t) -> p i t", i=4)
        qf = f32(32).rearrange("p (c t) -> p c t", c=2)
        kf = f32(32).rearrange("p (c t) -> p c t", c=2)
        tmpr = f32(16)
        prodf = bf(32).rearrange("p (c t) -> p c t", c=2)
        qTs = bf(32).rearrange("p (c t) -> p c t", c=2)
        kTs = bf(32).rearrange("p (c t) -> p c t", c=2)
        ktok = bf(256)
        vtokb = bf(512)
        o1 = f32(512)
        gts = bf(512)
        Zq = bf(512).rearrange("p (c s t) -> p c s t", c=2, s=16)
        Zk = work[:, 1024:3072].bitcast(BF16).rearrange("p (s d) -> p s d", s=16)
        eyeq = bf(256).rearrange("p (s t) -> p s t", s=16)
        rots = f32(4)
        onesf = bf(2)
        gfs = work[:, 3072:4096]
        Kw = xres[:, 0:4, :].rearrange("p a (b c) -> p (a b) c", c=256)
        Vw = xres[:, 4:8, :].rearrange("p a (b c) -> p (a b) c", c=256)
        Vw65 = xres[:, 8, :].bitcast(BF16)[:, 0:16 * 66].rearrange("p (b d) -> p b d", b=16)
        Rb = [xres[:, 9 + i, :].rearrange("p (c v) -> p c v", c=2) for i in range(4)]
        Rbfs = [xres[:, 13, :].bitcast(BF16)[:, i * 1024:(i + 1) * 1024].rearrange("p (c v) -> p c v", c=2) for i in range(2)]
        b3bf = banks[3][:, :].bitcast(BF16)
        b0bf = banks[0][:, :].bitcast(BF16)

        P.dma("sp", lambda: nc.sync.dma_start(out=xs_t[0:16, :], in_=x_sample[:, :]), w=["xs_t"], key="sconst")
        for (p0, p1) in ((0, 112), (112, 127)):
            P.dma("sp", lambda p0=p0, p1=p1: nc.sync.dma_start(out=Kw[p0:p1, :, :],
                                                                in_=cache_k[:, 1 + p0:1 + p1, :].rearrange("b j c -> j b c")),
                  w=["Kw"], key="sconst")
            P.dma("sp", lambda p0=p0, p1=p1: nc.sync.dma_start(out=Vw[p0:p1, :, :],
                                                                in_=cache_v[:, 1 + p0:1 + p1, :].rearrange("b j c -> j b c")),
                  w=["Vw"], key="sconst")
        P.dma("sp", lambda: nc.sync.dma_start(out=eyeq, in_=c_eyeq.rearrange("p (s t) -> p s t", s=16)), w=["eyeq"], key="sconst")
        P.dma("sp", lambda: nc.sync.dma_start(out=rots, in_=c_rot_s[:, :]), w=["rots"], key="sconst")
        P.dma("sp", lambda: nc.sync.dma_start(out=etS[0:32, :], in_=c_etab_s[:, :]), w=["etS"], key="sconst")
        P.dma("sp", lambda: nc.sync.dma_start(out=es2s[0:4, :, :],
                                              in_=sinks[0].rearrange("(k p m) -> p k m", k=4, p=4, m=2)),
              w=["es2s"], key="sconst")
        P.dma("sp", lambda: nc.sync.dma_start(out=gfs[0:16, :], in_=final_g[0:1, :].to_broadcast([16, D])), w=["gfs"], key="sconst")
        P.op("pool", lambda: nc.gpsimd.memset(onesf, 1.0), w=["onesf"], tiny=True)
        P.op("pool", lambda: nc.gpsimd.memset(Kpad2[0], 0.0), w=[("Kpad", 0)])
        P.op("pool", lambda: nc.gpsimd.memset(Kpad2[1], 0.0), w=[("Kpad", 1)])
        P.op("pool", lambda: nc.gpsimd.memset(Vw65[:, :, 64:66], 1.0), w=["Vw65"], tiny=True)
        P.barrier()
        P.dma("sp", lambda: nc.sync.dma_start(out=swa_k_s[:, 0:127, :], in_=cache_k[:, 1:128, :]), w=[], key="cshift")
        P.dma("sp", lambda: nc.sync.dma_start(out=swa_v_s[:, 0:127, :], in_=cache_v[:, 1:128, :]), w=[], key="cshift")
        biasA_dmas()
        P.op("act", lambda: nc.scalar.activation(out=es2s[0:4, :, :], in_=es2s[0:4, :, :], func=AF.Exp), r=["es2s"], w=["es2s"], tiny=True)
        P.op("dve", lambda: nc.vector.tensor_scalar(out=es2s[0:4, :, :], in0=es2s[0:4, :, :], scalar1=2.0, scalar2=None, op0=ALU.mult),
             r=["es2s"], w=["es2s"], tiny=True)
        P.op("pe", lambda: nc.tensor.matmul(banks[0][:, 0:32], lhsT=etS[0:32, :],
                                            rhs=rb33b[0:32, :].rearrange("b (k p m) -> b k m p", k=4, p=4, m=2),
                                            start=True, stop=True), r=["etS", "rb33b"], w=[bkey(0)])
        P.op("dve", lambda: nc.vector.tensor_copy(out=biasS, in_=banks[0][:, 0:32]), r=[bkey(0)], w=["biasS"], tiny=True)

        def s_norm(layer):
            P.op("pool", lambda: nc.gpsimd.memset(sm[0:16, 0:1], 0.0), w=["sm_ss"], tiny=True)
            P.op("act", lambda: nc.scalar.activation(out=xsb[0:16, :], in_=xs_t[0:16, :], func=AF.Square, accum_out=sm[0:16, 0:1]),
                 r=["xs_t", "sm_ss"], w=["sm_ss", "xsb"], tiny=True)
            P.op("pool", lambda: nc.gpsimd.tensor_scalar(out=sm[0:16, 1:2], in0=sm[0:16, 0:1], scalar1=1.0 / D, scalar2=float(EPS),
                                                         op0=ALU.mult, op1=ALU.add), r=["sm_ss"], w=["sm_r"], tiny=True)
            P.op("pool", lambda: nc.gpsimd.tensor_tensor(out=sm[0:16, 1:2], in0=sm[0:16, 1:2], in1=nhalf[0:16, 0:1], op=ALU.pow),
                 r=["sm_r", "nhalf"], w=["sm_r"], tiny=True)
            P.op("dve", lambda: nc.vector.tensor_scalar(out=xsb[0:16, :], in0=xs_t[0:16, :], scalar1=sm[0:16, 1:2], scalar2=None,
                                                        op0=ALU.mult), r=["xs_t", "sm_r"], w=["xsb"])

            def tr():
                for k in range(8):
                    i = nc.tensor.transpose(out=b3bf[:, k * 16:(k + 1) * 16], in_=xsb[0:16, k * 128:(k + 1) * 128],
                                            identity=ident[0:16, 0:16])
                return i

            P.op("pe", tr, r=["xsb", "ident"], w=[bkey(3)])
            P.op("dve", lambda layer=layer: nc.vector.tensor_tensor(
                out=hsT, in0=b3bf[:, 0:128].rearrange("p (k t) -> p k t", k=8),
                in1=gT[:, layer, :].unsqueeze(2).to_broadcast([128, 8, 16]), op=ALU.mult),
                r=[bkey(3), "gT"], w=["hsT"])

        sorder = [(0, g_) for g_ in range(4)] + [(1, h_) for h_ in range(4)]
        spre = {0: early_slot}

        def sget(i):
            if i < len(sorder) and i not in spre:
                spre[i] = load_win(*sorder[i])
            return spre.get(i)


        sget(0)
        s_norm(0)
        for g in range(4):
            s = sget(g)
            sget(g + 1)
            wi, wo = win_l0[s], wout[0]
            wk = ("win", s)

            def mmq():
                for p in range(4):
                    for k in range(8):
                        i = nc.tensor.matmul(banks[0][:, p * 16:(p + 1) * 16], lhsT=wi[:, k, p * 128:(p + 1) * 128], rhs=hsT[:, k, :],
                                             start=(k == 0), stop=(k == 7))
                return i

            P.op("pe", mmq, r=[wk, "hsT"], w=[bkey(0)])
            P.op("dve", lambda: nc.vector.tensor_copy(out=QTs, in_=banks[0][:, 0:64].rearrange("p (q t) -> p q t", q=4)),
                 r=[bkey(0)], w=["QTs"])

            def mmkv():
                for j, c0_ in enumerate((512, 768)):
                    for k in range(8):
                        i = nc.tensor.matmul(banks[1][0:16, j * 64:(j + 1) * 64], lhsT=hsT[:, k, :], rhs=wi[:, k, c0_:c0_ + 64],
                                             start=(k == 0), stop=(k == 7))
                return i

            P.op("pe", mmkv, r=[wk, "hsT"], w=[bkey(1)])
            P.op("dve", lambda: nc.vector.tensor_copy(out=knv[0:16, :], in_=banks[1][0:16, 0:128]), r=[bkey(1)], w=["knv"])
            P.dma("sp", lambda g=g: nc.sync.dma_start(out=swa_k_s[:, 127, g * 64:(g + 1) * 64], in_=knv[0:16, 0:64]),
                  r=["knv"], w=["skd"], key="skn")
            P.dma("sp", lambda g=g: nc.sync.dma_start(out=swa_v_s[:, 127, g * 64:(g + 1) * 64], in_=knv[0:16, 64:128]),
                  r=["knv"], w=["svd"], key="skn")
            P.dma("sp", lambda g=g: nc.sync.dma_start(out=Kw[127:128, :, g * 64:(g + 1) * 64],
                                                      in_=swa_k_s[:, 127:128, g * 64:(g + 1) * 64].rearrange("b o d -> o b d")),
                  r=["skd", "svd"], w=["Kw"], key="skn2k")
            P.dma("sp", lambda g=g: nc.sync.dma_start(out=Vw[127:128, :, g * 64:(g + 1) * 64],
                                                      in_=swa_v_s[:, 127:128, g * 64:(g + 1) * 64].rearrange("b o d -> o b d")),
                  r=["skd", "svd"], w=["Vw"], key="skn2v")

            def mmg():
                for k in range(8):
                    i = nc.tensor.matmul(banks[2][0:16, :], lhsT=hsT[:, k, :], rhs=wi[:, k, 832:1344], start=(k == 0), stop=(k == 7))
                return i

            P.op("pe", mmg, r=[wk, "hsT"], w=[bkey(2)])
            P.op("act", lambda: nc.scalar.activation(out=t_s[0:16, :], in_=banks[2][0:16, :], func=AF.Tanh, scale=0.5),
                 r=[bkey(2)], w=["t_s"])
            P.op("dve", lambda: nc.vector.scalar_tensor_tensor(out=us[0:16, :], in0=t_s[0:16, :], scalar=1.0, in1=banks[2][0:16, :],
                                                               op0=ALU.add, op1=ALU.mult), r=[bkey(2), "t_s"], w=["us"])
            P.op("dve", lambda g=g: nc.vector.tensor_copy(out=Vw65[:, :, 0:64], in_=Vw[:, :, g * 64:(g + 1) * 64]),
                 r=["Vw"], w=["Vw65"])
            def S1(b):
                pr = b % 2
                P.op("dve", lambda: nc.vector.tensor_copy(out=Kpad2[pr][:, 64:128], in_=Kw[:, b, g * 64:(g + 1) * 64]),
                     r=["Kw"], w=[("Kpad", pr)])

                def trk():
                    nc.tensor.transpose(out=b3bf[:, 0:128], in_=Kpad2[pr][:, 64:192], identity=ident[:, :])
                    return nc.tensor.transpose(out=b3bf[:, 128:256], in_=Kpad2[pr][:, 0:128], identity=ident[:, :])

                P.op("pe", trk, r=[("Kpad", pr), "ident"], w=[bkey(3)])
                P.op("act", lambda: nc.scalar.copy(out=KTp2[pr], in_=b3bf[:, 0:256].rearrange("p (m j) -> p m j", m=2)),
                     r=[bkey(3)], w=[("KTp", pr)])

            def S2(b):
                pr = b % 2

                def mml():
                    nc.tensor.matmul(banks[4][:, 0:8], lhsT=ident[:, :], rhs=biasS[:, g * 8:(g + 1) * 8], start=True, stop=False)
                    nc.tensor.matmul(banks[4][:, 0:4], lhsT=KTp2[pr][:, 0, :], rhs=QTs[:, :, b], start=False, stop=False)
                    return nc.tensor.matmul(banks[4][:, 4:8], lhsT=KTp2[pr][:, 1, :], rhs=QTs[:, :, b], start=False, stop=True)

                P.op("pe", mml, r=[("KTp", pr), "QTs", "biasS", "ident"], w=[bkey(4)])
                P.op("act", lambda: nc.scalar.activation(out=PTs2[pr], in_=banks[4][:, 0:8], func=AF.Exp, scale=0.125),
                     r=[bkey(4)], w=[("PTs", pr)], tiny=True)

            def S3(b):
                pr = b % 2

                def mmpv():
                    nc.tensor.matmul(banks[5][0:4, 0:65], lhsT=PTs2[pr][:, 0:4], rhs=Vw65[:, b, 0:65], start=True, stop=True)
                    return nc.tensor.matmul(banks[5][0:4, 65:130], lhsT=PTs2[pr][:, 4:8], rhs=Vw65[:, b, 0:65], start=True, stop=True)

                P.op("pe", mmpv, r=[("PTs", pr), "Vw65"], w=[bkey(5)])
                O2 = banks[5][0:4, 0:130].rearrange("p (m d) -> p m d", m=2)
                P.op("dve", lambda: nc.vector.scalar_tensor_tensor(out=sm[0:4, 4:6], in0=O2[:, :, 64], scalar=2.0,
                                                                   in1=es2s[0:4, g, :], op0=ALU.mult, op1=ALU.add),
                     r=[bkey(5), "es2s"], w=["sden"], tiny=True)
                P.op("dve", lambda: nc.vector.reciprocal(out=sm[0:4, 6:8], in_=sm[0:4, 4:6]), r=["sden"], w=["srr"], tiny=True)
                P.op("dve", lambda: nc.vector.tensor_tensor(out=ob2[pr][0:4, :, :], in0=O2[:, :, 0:64],
                                                            in1=sm[0:4, 6:8].unsqueeze(2).to_broadcast([4, 2, 64]), op=ALU.mult),
                     r=[bkey(5), "srr"], w=[("ob", pr)])
                P.dma("sp", lambda: nc.sync.dma_start(out=osamp[g, b, :].rearrange("(p m d) -> p m d", p=4, m=2),
                                                      in_=ob2[pr][0:4, :, :]), r=[("ob", pr)], w=[("osd", pr)], key="osamp%d" % pr)

            for i in range(SB + 2):
                if i < SB:
                    S1(i)
                if 0 <= i - 1 < SB:
                    S2(i - 1)
                if 0 <= i - 2 < SB:
                    S3(i - 2)
            P.dma("sp", lambda g=g: nc.sync.dma_start(out=os_[0:16, :], in_=osamp[g, :, :]), r=[("osd", 0), ("osd", 1)], w=["os_"], key="osamp2")
            P.op("dve", lambda: nc.vector.tensor_tensor(out=ogs[0:16, :], in0=os_[0:16, :], in1=us[0:16, :], op=ALU.mult),
                 r=["os_", "us"], w=["ogs"])

            def trg():
                for f in range(4):
                    i = nc.tensor.transpose(out=b3bf[:, f * 16:(f + 1) * 16], in_=ogs[0:16, f * 128:(f + 1) * 128],
                                            identity=ident[0:16, 0:16])
                return i

            P.op("pe", trg, r=["ogs", "ident"], w=[bkey(3)])
            P.op("act", lambda: nc.scalar.copy(out=gsT, in_=b3bf[:, 0:64].rearrange("p (f t) -> p f t", f=4)), r=[bkey(3)], w=["gsT"])

            def mmy(g=g):
                for n_ in range(2):
                    for f in range(4):
                        i = nc.tensor.matmul(banks[6 + n_][0:16, :], lhsT=gsT[:, f, :], rhs=wo[:, f, n_ * 512:(n_ + 1) * 512],
                                             start=(g == 0 and f == 0), stop=(g == 3 and f == 3))
                return i

            P.op("pe", mmy, r=["gsT", ("wout", 0)], w=[bkey(6), bkey(7)])
            if g + 1 < len(sorder):
                load_wout(*sorder[g + 1])
        P.op("dve", lambda: nc.vector.tensor_tensor(out=xs_t[0:16, 0:512], in0=xs_t[0:16, 0:512], in1=banks[6][0:16, :], op=ALU.add),
             r=[bkey(6), "xs_t"], w=["xs_t"])
        P.op("dve", lambda: nc.vector.tensor_tensor(out=xs_t[0:16, 512:1024], in0=xs_t[0:16, 512:1024], in1=banks[7][0:16, :], op=ALU.add),
             r=[bkey(7), "xs_t"], w=["xs_t"])

        if stage == "S0":
            P.dma("sp", lambda: nc.sync.dma_start(out=y_sample[:, :], in_=xs_t[0:16, :]), r=["xs_t"], w=[], key="sconst")
            P.barrier()
            return
        import os as _os
        _dbg = _os.environ.get("KDBG", "")

        def cut(name):
            if _dbg == name:
                P.dma("sp", lambda: nc.sync.dma_start(out=y_sample[:, :], in_=xs_t[0:16, :]), r=["xs_t"], w=[], key="sconst")
                P.barrier()
                P.muted = True

        s_norm(1)
        it = 0
        for h in range(4):
            s = sget(4 + h)
            sget(4 + h + 1)
            wi, wo = win[s], wout[0]
            wk = ("win", s)
            G1 = HC["g1"][h]

            def mmqk():
                for idx, c0_ in enumerate((0, 128, 256, 384)):
                    for k in range(8):
                        i = nc.tensor.matmul(banks[0][:, idx * 16:(idx + 1) * 16], lhsT=wi[:, k, c0_:c0_ + 128], rhs=hsT[:, k, :],
                                             start=(k == 0), stop=(k == 7))
                return i

            P.op("pe", mmqk, r=[wk, "hsT"], w=[bkey(0)])
            P.op("dve", lambda: nc.vector.tensor_copy(out=qk4, in_=banks[0][:, 0:64].rearrange("p (i t) -> p i t", i=4)),
                 r=[bkey(0)], w=["qk4"], tiny=True)
            for (src0, dst, cc, sc, nm) in ((0, qf, 0, 1, "qf"), (2, kf, 2, 3, "kf")):
                P.op("dve", lambda src0=src0, sc=sc: nc.vector.tensor_scalar(out=tmpr, in0=qk4[:, src0 + 1, :], scalar1=rots[:, sc:sc + 1],
                                                                             scalar2=None, op0=ALU.mult), r=["qk4", "rots"], w=["tmpr"], tiny=True)
                P.op("dve", lambda src0=src0, dst=dst, cc=cc: nc.vector.scalar_tensor_tensor(
                    out=dst[:, 0, :], in0=qk4[:, src0, :], scalar=rots[:, cc:cc + 1], in1=tmpr, op0=ALU.mult, op1=ALU.subtract),
                    r=["qk4", "rots", "tmpr"], w=[nm + "0"], tiny=True)
                P.op("dve", lambda src0=src0, cc=cc: nc.vector.tensor_scalar(out=tmpr, in0=qk4[:, src0 + 1, :], scalar1=rots[:, cc:cc + 1],
                                                                             scalar2=None, op0=ALU.mult), r=["qk4", "rots", nm + "0"], w=["tmpr"], tiny=True)
                P.op("dve", lambda src0=src0, dst=dst, sc=sc: nc.vector.scalar_tensor_tensor(
                    out=dst[:, 1, :], in0=qk4[:, src0, :], scalar=rots[:, sc:sc + 1], in1=tmpr, op0=ALU.mult, op1=ALU.add),
                    r=["qk4", "rots", "tmpr"], w=[nm + "1"], tiny=True)
            cut("c1")
            P.op("dve", lambda: nc.vector.tensor_copy(out=qTs, in_=qf), r=["qf0", "qf1"], w=["qTs"], tiny=True)
            P.op("dve", lambda: nc.vector.tensor_copy(out=kTs, in_=kf), r=["kf0", "kf1"], w=["kTs"], tiny=True)
            P.op("dve", lambda: nc.vector.tensor_tensor(out=prodf, in0=qf, in1=kf, op=ALU.mult), r=["qf0", "qf1", "kf0", "kf1"], w=["prodf"], tiny=True)

            def mmdot():
                nc.tensor.matmul(banks[0][0:16, 128:130], lhsT=prodf[:, 0, :], rhs=onesf[:, 0:2], start=True, stop=False)
                return nc.tensor.matmul(banks[0][0:16, 128:130], lhsT=prodf[:, 1, :], rhs=onesf[:, 0:2], start=False, stop=True)

            P.op("pe", mmdot, r=["prodf", "onesf"], w=[bkey(0)])
            P.op("dve", lambda: nc.vector.tensor_copy(out=sm[0:16, 8:9], in_=banks[0][0:16, 128:129]), r=[bkey(0)], w=["sdot"], tiny=True)

            cut("c2")

            def mmv():
                for k in range(8):
                    i = nc.tensor.matmul(banks[1][0:16, :], lhsT=hsT[:, k, :], rhs=wi[:, k, 512:1024], start=(k == 0), stop=(k == 7))
                return i

            P.op("pe", mmv, r=[wk, "hsT"], w=[bkey(1)])
            P.op("act", lambda: nc.scalar.copy(out=vtokb[0:16, :], in_=banks[1][0:16, :]), r=[bkey(1)], w=["vtokb"])
            P.op("dve", lambda: nc.vector.tensor_scalar(out=o1[0:16, :], in0=banks[1][0:16, :], scalar1=sm[0:16, 8:9], scalar2=None,
                                                        op0=ALU.mult), r=[bkey(1), "sdot", "vtokb"], w=["o1"])

            def mmg1():
                for k in range(8):
                    i = nc.tensor.matmul(banks[2][0:16, :], lhsT=hsT[:, k, :], rhs=wi[:, k, 1024:1536], start=(k == 0), stop=(k == 7))
                return i

            P.op("pe", mmg1, r=[wk, "hsT"], w=[bkey(2)])
            P.op("act", lambda: nc.scalar.activation(out=t_s[0:16, :], in_=banks[2][0:16, :], func=AF.Tanh, scale=0.5),
                 r=[bkey(2)], w=["t_s"])
            P.op("dve", lambda: nc.vector.scalar_tensor_tensor(out=us[0:16, :], in0=t_s[0:16, :], scalar=1.0, in1=banks[2][0:16, :],
                                                               op0=ALU.add, op1=ALU.mult), r=[bkey(2), "t_s"], w=["us"])

            cut("c3")
            def trk1():
                nc.tensor.transpose(out=b3bf[0:16, 0:128], in_=kTs[:, 0, :], identity=ident[:, :])
                return nc.tensor.transpose(out=b3bf[0:16, 128:256], in_=kTs[:, 1, :], identity=ident[:, :])

            P.op("pe", trk1, r=["kTs", "ident"], w=[bkey(3)])
            P.op("act", lambda: nc.scalar.copy(out=ktok[0:16, :], in_=b3bf[0:16, 0:256]), r=[bkey(3)], w=["ktok"])
            cut("c4")
            P.op("dve", lambda: nc.vector.tensor_tensor(out=Zq, in0=qTs.unsqueeze(2).to_broadcast([128, 2, 16, 16]),
                                                        in1=eyeq.unsqueeze(1).to_broadcast([128, 2, 16, 16]), op=ALU.mult),
                 r=["qTs", "eyeq"], w=["Zq"])
            P.op("dve", lambda: nc.vector.tensor_tensor(out=Zk[0:16, :, :], in0=ktok[0:16, :].unsqueeze(1).to_broadcast([16, 16, 256]),
                                                        in1=ident[0:16, 0:16].unsqueeze(2).to_broadcast([16, 16, 256]), op=ALU.mult),
                 r=["ktok", "ident"], w=["Zk"])
            cut("c5")
            for b in range(SB if "norloop" not in _dbg else 0):
                sl = it % 4
                s2 = it % 2
                ub = 2 + 2 * (it % 2)
                it += 1
                P.dma("sp", lambda b=b, h=h, sl=sl: nc.sync.dma_start(out=Rb[sl], in_=state_ret[b, h].rearrange("(c p) v -> p c v", p=128)),
                      w=[("Rb", sl)], key="rin%d" % sl)
                P.op("act", lambda sl=sl, s2=s2: nc.scalar.copy(out=Rbfs[s2], in_=Rb[sl]), r=[("Rb", sl)], w=[("Rbfs", s2)])

                def mmc(b=b, s2=s2):
                    nc.tensor.matmul(banks[1][0:16, :], lhsT=Zq[:, 0, b, :], rhs=Rbfs[s2][:, 0, :], start=(b == 0), stop=False)
                    return nc.tensor.matmul(banks[1][0:16, :], lhsT=Zq[:, 1, b, :], rhs=Rbfs[s2][:, 1, :], start=False, stop=(b == SB - 1))

                P.op("pe", mmc, r=["Zq", ("Rbfs", s2), "o1", "vtokb"], w=[bkey(1)])

                def mmu(b=b, ub=ub):
                    nc.tensor.matmul(banks[ub][:, :], lhsT=Zk[0:16, b, 0:128], rhs=vtokb[0:16, :], start=True, stop=True)
                    return nc.tensor.matmul(banks[ub + 1][:, :], lhsT=Zk[0:16, b, 128:256], rhs=vtokb[0:16, :], start=True, stop=True)

                P.op("pe", mmu, r=["Zk", "vtokb"], w=[bkey(ub), bkey(ub + 1)])

                def upd(sl=sl, ub=ub, G1=G1):
                    nc.vector.scalar_tensor_tensor(out=Rb[sl][:, 0, :], in0=Rb[sl][:, 0, :], scalar=G1, in1=banks[ub][:, :],
                                                   op0=ALU.mult, op1=ALU.add)
                    return nc.vector.scalar_tensor_tensor(out=Rb[sl][:, 1, :], in0=Rb[sl][:, 1, :], scalar=G1, in1=banks[ub + 1][:, :],
                                                          op0=ALU.mult, op1=ALU.add)

                P.op("dve", upd, r=[bkey(ub), bkey(ub + 1), ("Rb", sl), ("Rbfs", s2)], w=[("Rb", sl)])
                P.dma("pool", lambda b=b, h=h, sl=sl: nc.gpsimd.dma_start(out=ret_s[b, h].rearrange("(c p) v -> p c v", p=128), in_=Rb[sl]),
                      r=[("Rb", sl)], w=[], key="rout_s%d" % sl)
            P.op("dve", lambda G1=G1: nc.vector.scalar_tensor_tensor(out=o1[0:16, :], in0=banks[1][0:16, :], scalar=G1, in1=o1[0:16, :],
                                                                     op0=ALU.mult, op1=ALU.add), r=[bkey(1), "o1"], w=["o1"])
            P.op("dve", lambda: nc.vector.bn_stats(out=sm[0:16, 10:16], in_=o1[0:16, :]), r=["o1"], w=["sst6"], tiny=True)
            P.op("dve", lambda: nc.vector.bn_aggr(out=sm[0:16, 2:4], in_=sm[0:16, 10:16]), r=["sst6"], w=["smv"], tiny=True)
            P.op("pool", lambda: nc.gpsimd.tensor_scalar(out=sm[0:16, 9:10], in0=sm[0:16, 3:4], scalar1=float(EPS), scalar2=None, op0=ALU.add),
                 r=["smv"], w=["srs"], tiny=True)
            P.op("pool", lambda: nc.gpsimd.tensor_tensor(out=sm[0:16, 9:10], in0=sm[0:16, 9:10], in1=nhalf[0:16, 0:1], op=ALU.pow),
                 r=["srs", "nhalf"], w=["srs"], tiny=True)
            P.op("dve", lambda: nc.vector.tensor_scalar(out=o1[0:16, :], in0=o1[0:16, :], scalar1=sm[0:16, 2:3], scalar2=sm[0:16, 9:10],
                                                        op0=ALU.subtract, op1=ALU.mult), r=["o1", "smv", "srs"], w=["o1"])
            P.op("dve", lambda: nc.vector.tensor_tensor(out=gts[0:16, :], in0=o1[0:16, :], in1=us[0:16, :], op=ALU.mult),
                 r=["o1", "us"], w=["gts"])

            def trg1():
                for f in range(4):
                    i = nc.tensor.transpose(out=b3bf[:, f * 16:(f + 1) * 16], in_=gts[0:16, f * 128:(f + 1) * 128],
                                            identity=ident[0:16, 0:16])
                return i

            P.op("pe", trg1, r=["gts", "ident"], w=[bkey(3)])
            P.op("act", lambda: nc.scalar.copy(out=gsT, in_=b3bf[:, 0:64].rearrange("p (f t) -> p f t", f=4)), r=[bkey(3)], w=["gsT"])

            def mmy1(h=h):
                for n_ in range(2):
                    for f in range(4):
                        i = nc.tensor.matmul(banks[6 + n_][0:16, :], lhsT=gsT[:, f, :], rhs=wo[:, f, n_ * 512:(n_ + 1) * 512],
                                             start=(h == 0 and f == 0), stop=(h == 3 and f == 3))
                return i

            P.op("pe", mmy1, r=["gsT", ("wout", 0)], w=[bkey(6), bkey(7)])
            if 4 + h + 1 < len(sorder):
                load_wout(*sorder[4 + h + 1])
        P.op("dve", lambda: nc.vector.scalar_tensor_tensor(out=xs_t[0:16, 0:512], in0=banks[6][0:16, :], scalar=0.5, in1=xs_t[0:16, 0:512],
                                                           op0=ALU.mult, op1=ALU.add), r=[bkey(6), "xs_t"], w=["xs_t"])
        P.op("dve", lambda: nc.vector.scalar_tensor_tensor(out=xs_t[0:16, 512:1024], in0=banks[7][0:16, :], scalar=0.5,
                                                           in1=xs_t[0:16, 512:1024], op0=ALU.mult, op1=ALU.add),
             r=[bkey(7), "xs_t"], w=["xs_t"])
        P.op("pool", lambda: nc.gpsimd.memset(sm[0:16, 0:1], 0.0), w=["sm_ss"], tiny=True)
        P.op("act", lambda: nc.scalar.activation(out=xsb[0:16, :], in_=xs_t[0:16, :], func=AF.Square, accum_out=sm[0:16, 0:1]),
             r=["xs_t", "sm_ss"], w=["sm_ss", "xsb"], tiny=True)
        P.op("pool", lambda: nc.gpsimd.tensor_scalar(out=sm[0:16, 1:2], in0=sm[0:16, 0:1], scalar1=1.0 / D, scalar2=float(EPS),
                                                     op0=ALU.mult, op1=ALU.add), r=["sm_ss"], w=["sm_r"], tiny=True)
        P.op("pool", lambda: nc.gpsimd.tensor_tensor(out=sm[0:16, 1:2], in0=sm[0:16, 1:2], in1=nhalf[0:16, 0:1], op=ALU.pow),
             r=["sm_r", "nhalf"], w=["sm_r"], tiny=True)
        P.op("dve", lambda: nc.vector.scalar_tensor_tensor(out=xs_t[0:16, :], in0=xs_t[0:16, :], scalar=sm[0:16, 1:2], in1=gfs[0:16, :],
                                                           op0=ALU.mult, op1=ALU.mult), r=["xs_t", "sm_r", "gfs"], w=["xs_t"])
        P.dma("sp", lambda: nc.sync.dma_start(out=y_sample[:, :], in_=xs_t[0:16, :]), r=["xs_t"], w=[], key="sconst")
        P.barrier()

    order = []
    for seq in range(NSEQ if not stage.startswith("S") else 0):
        order += [(0, g) for g in range(4)] + ([(1, h) for h in range(4)] if stage != "L0" else [])
    loaded = {}

    def ensure_win(idx):
        if idx < len(order) and idx not in loaded:
            loaded[idx] = load_win(*order[idx])

    def wout_cb(idx):
        def cb():
            if idx < len(order):
                load_wout(*order[idx])
        return cb

    if stage != "L0":
        sample_phase()
        P.muted = False
    widx = 0
    if order:
        ensure_win(0)
        load_wout(*order[0])
    for seq in range(NSEQ if not stage.startswith("S") else 0):
        if seq == 0 or stage == "L0":
            for b in range(NB):
                load_x(seq, b)
        phase_a(0)
        if seq > 0:
            P.barrier()
        for pc in range(2):
            for hq in range(4):
                P.dma("sp", lambda pc=pc, hq=hq: nc.sync.dma_start(
                    out=biasT[:, pc, hq * 8:(hq + 1) * 8, :],
                    in_=bass.AP(tensor=biasA, offset=(pc * 32 + hq * 8) * ALEN + 127, ap=[[255, 128], [ALEN, 8], [1, 128]])),
                    r=["biasA"], w=["biasT"], key="biasT")
        for pc in range(2):
            P.op("act", lambda pc=pc: nc.scalar.activation(out=biasT[:, pc, :, :], in_=biasT[:, pc, :, :], func=AF.Exp, scale=0.125),
                 r=["biasT"], w=["biasT"])
        P.op("pool", lambda: nc.gpsimd.memset(V65[:, :, 64:66], 1.0), w=[("V", 0), ("V", 1)])
        for g in range(4):
            ensure_win(widx + g)
            ensure_win(widx + g + 1)
            l0_group(seq, g, loaded[widx + g], wout_cb(widx + g + 1))
        widx += 4
        if stage != "L0":
            phase_a(1)
        P.barrier()
        if stage == "L0":
            for b in range(NB):
                P.dma("sp", lambda b=b, seq=seq: nc.sync.dma_start(out=dbg_x1[seq, b * 128:(b + 1) * 128, :], in_=xres[:, b, :]),
                      r=[("x", b)], w=[], key="x%d" % b)
            for b in range(NB):
                P.dma("sp", lambda b=b, seq=seq: nc.sync.dma_start(out=y_prompt[seq, b * 128:(b + 1) * 128, :], in_=xres[:, b, :]),
                      r=[("x", b)], w=[], key="x%d" % b)
            for h in range(4):
                P.dma("sp", lambda seq=seq, h=h: nc.sync.dma_start(out=ret_p[seq, h].rearrange("(c p) v -> p c v", p=128),
                                                              in_=xres[:, 0, :].rearrange("p (c v) -> p c v", c=2)),
                      r=[("x", 0)], w=[], key="rout")
            P.barrier()
            continue
        PP[:] = [0, 1, 2, 3]
        for h in range(4):
            ensure_win(widx + h)
            ensure_win(widx + h + 1)
            l1_head(seq, h, loaded[widx + h], wout_cb(widx + h + 1))
        widx += 4
        PP[:] = [0, 1]
        P.dma("sp", lambda: nc.sync.dma_start(out=gfin, in_=final_g[0:1, :].to_broadcast([128, D])), r=[], w=["gfin", "R", "Rbf"], key="const")
        final_norm(seq)

    P.emit()
    es.close()
    return nc, P


def _arr_win_attn(W):
    out = np.zeros((4, 128, 8, 1344), np.float32)
    for g in range(4):
        k = W[:, 2048 + g * 64:2048 + (g + 1) * 64]
        cols = np.concatenate([W[:, g * 512:(g + 1) * 512], k, np.zeros((D, 128), np.float32), k,
                               W[:, 2304 + g * 64:2304 + (g + 1) * 64], W[:, 2560 + g * 512:2560 + (g + 1) * 512]], axis=1)
        out[g] = cols.reshape(8, 128, 1344).transpose(1, 0, 2)
    return np.ascontiguousarray(out.reshape(4, 128, 8 * 1344))


def _arr_win_ret(W):
    out = np.zeros((4, 128, 8, 1536), np.float32)
    for h in range(4):
        cols = np.concatenate([W[:, h * 256:(h + 1) * 256], W[:, 1024 + h * 256:1024 + (h + 1) * 256],
                               W[:, 2048 + h * 512:2048 + (h + 1) * 512], W[:, 4096 + h * 512:4096 + (h + 1) * 512]], axis=1)
        out[h] = cols.reshape(8, 128, 1536).transpose(1, 0, 2)
    return np.ascontiguousarray(out.reshape(4, 128, 8 * 1536))


def _arr_wout(W):
    return np.ascontiguousarray(W.reshape(4, 4, 128, D).transpose(0, 2, 1, 3).reshape(4, 128, 4096))


_CACHE = {}


def _get_prog(stage):
    if stage not in _CACHE:
        _CACHE[stage] = build(stage)
    return _CACHE[stage]


def kernel(x_prompt, x_sample, cache_swa_k, cache_swa_v, state_ret, norm_g, final_norm_g, rel_bias,
           w_in_attn, attn_sinks, w_out_attn, w_in_ret, w_out_ret, _stage="full"):
    nc, P = _get_prog(_stage)
    f = lambda a: np.ascontiguousarray(np.asarray(a, dtype=np.float32))
    shared = {
        "w_in_attn_r": _arr_win_attn(f(w_in_attn)[0]), "w_out_attn_r": _arr_wout(f(w_out_attn)[0]),
        "w_in_ret_r": _arr_win_ret(f(w_in_ret)[0]), "w_out_ret_r": _arr_wout(f(w_out_ret)[0]),
        "norm_gT": np.ascontiguousarray(f(norm_g).reshape(2, 8, 128).transpose(0, 2, 1)),
        "final_norm_g": f(final_norm_g).reshape(1, D),
        "rel_bias": f(rel_bias), "attn_sinks": f(attn_sinks),
        "c_ident": HC["ident"], "c_etab": HC["etab"], "c_maskT": HC["maskT"], "c_rot": HC["rot"],
        "c_etab_s": HC["etab_s"], "c_eyeq": HC["eyeq"], "c_rot_s": HC["rot_s"],
    }
    xsm = f(x_sample).reshape(128, D)
    ck = f(cache_swa_k).reshape(128, 128, 256)
    cv = f(cache_swa_v).reshape(128, 128, 256)
    st = f(state_ret).reshape(128, 4, 256, 512)
    xp = f(x_prompt)
    in_maps = []
    for c in range(NCORES):
        m = dict(shared)
        m["x_prompt"] = xp[c * NSEQ:(c + 1) * NSEQ]
        if _stage != "L0":
            m["x_sample"] = xsm[c * SB:(c + 1) * SB]
            m["cache_k"] = ck[c * SB:(c + 1) * SB]
            m["cache_v"] = cv[c * SB:(c + 1) * SB]
            m["state_ret"] = st[c * SB:(c + 1) * SB]
        else:
            for k_ in ("c_etab_s", "c_eyeq", "c_rot_s"):
                m.pop(k_, None)
        in_maps.append(m)
    res = run_bass_kernel_spmd(nc, in_maps, core_ids=list(range(NCORES)))
    R = res.results
    if _stage.startswith("S"):
        return np.concatenate([r["y_sample"] for r in R], axis=0), np.concatenate([r["swa_k_sample"] for r in R], axis=0), np.concatenate([r["swa_v_sample"] for r in R], axis=0), np.concatenate([r["ret_state_sample"] for r in R], axis=0)
    y_prompt = np.concatenate([r["y_prompt"] for r in R], axis=0)
    swa_k = np.concatenate([r["swa_k_prompt"] for r in R], axis=0)[None]
    swa_v = np.concatenate([r["swa_v_prompt"] for r in R], axis=0)[None]
    ret_p = np.concatenate([r["ret_state_prompt"] for r in R], axis=0)[None]
    if _stage == "L0":
        return [y_prompt, None, swa_k, swa_v, ret_p, None, None, None], np.concatenate([r["dbg_x1"] for r in R], axis=0)
    y_sample = np.concatenate([r["y_sample"] for r in R], axis=0).reshape(128, 1, D)
    swa_k_s = np.concatenate([r["swa_k_sample"] for r in R], axis=0).reshape(1, 128, 128, 4, 64)
    swa_v_s = np.concatenate([r["swa_v_sample"] for r in R], axis=0).reshape(1, 128, 128, 4, 64)
    ret_s = np.concatenate([r["ret_state_sample"] for r in R], axis=0)[None]
    return (y_prompt, y_sample, swa_k, swa_v, ret_p, swa_k_s, swa_v_s, ret_s)
```

```python
import math
import types
from contextlib import ExitStack

import numpy as np
import ml_dtypes

import concourse.bass as bass
import concourse.mybir as mybir
from concourse.bass_utils import run_bass_kernel_spmd

F32 = mybir.dt.float32
BF16 = mybir.dt.bfloat16
AF = mybir.ActivationFunctionType
ALU = mybir.AluOpType

NCORES = 8
D = 1024
SEQ = 2048
NB = SEQ // 128
NSEQ = 2
SB = 16
EPS = 1e-6
PAST = 16384
NEG = -1.0e30
ROPE_BASE = 10000.0
ALEN = 130 * 256


def _freeze(fn):
    if fn.__closure__ is None:
        return fn
    cells = []
    for c in fn.__closure__:
        try:
            cells.append(types.CellType(c.cell_contents))
        except ValueError:
            cells.append(c)
    return types.FunctionType(fn.__code__, fn.__globals__, fn.__name__, fn.__defaults__, tuple(cells))


class Prog:
    def __init__(self, nc, es):
        self.nc = nc
        self.es = es
        self.ops = []
        self.muted = False
        self.engs = {"pe": nc.tensor, "dve": nc.vector, "act": nc.scalar, "pool": nc.gpsimd, "sp": nc.sync}

    def op(self, eng, fn, r=(), w=(), tiny=False):
        if self.muted:
            return
        self.ops.append(("c", eng, _freeze(fn), tuple(r), tuple(w), "tiny" if tiny else None))

    def dma(self, q, fn, r=(), w=(), key=None):
        assert key is not None
        if self.muted:
            return
        self.ops.append(("d", q, _freeze(fn), tuple(r), tuple(w), key))

    def barrier(self):
        if self.muted:
            return
        self.ops.append(("b", None, None, (), (), None))

    def emit(self):
        ops = self.ops
        n = len(ops)
        tl = [None] * n
        for i, o in enumerate(ops):
            if o[0] == "c":
                tl[i] = o[1]
            elif o[0] == "d":
                tl[i] = "dma:" + o[5]
        last_w = {}
        readers = {}
        last_on_tl = {}
        bar_deps = {}
        need = [None] * n
        signaling = [False] * n
        for i, o in enumerate(ops):
            kind, eng, fn, rs, ws, key = o
            if kind == "b":
                bar_deps = dict(last_on_tl)
                continue
            deps = {}

            def add(j):
                t = tl[j]
                if j > deps.get(t, -1):
                    deps[t] = j

            for b in rs:
                if b in last_w:
                    add(last_w[b])
            for b in ws:
                if b in last_w:
                    add(last_w[b])
                for j in readers.get(b, {}).values():
                    add(j)
            for j in bar_deps.values():
                add(j)
            nd = {}
            for t, j in deps.items():
                if kind == "c" and t == eng and ops[j][5] != "tiny":
                    continue
                nd[t] = j
                if ops[j][0] == "c":
                    signaling[j] = True
            need[i] = nd
            for b in rs:
                readers.setdefault(b, {})[tl[i]] = i
            for b in ws:
                last_w[b] = i
                readers[b] = {}
            last_on_tl[tl[i]] = i
        val = [0] * n
        cnt = {}
        for i, o in enumerate(ops):
            if o[0] == "c":
                if signaling[i]:
                    cnt[o[1]] = cnt.get(o[1], 0) + 1
                    val[i] = cnt[o[1]]
            elif o[0] == "d":
                cnt[tl[i]] = cnt.get(tl[i], 0) + 16
                val[i] = cnt[tl[i]]
        sems = {}
        for t in cnt:
            sems[t] = self.es.enter_context(self.nc.semaphore("s_" + t.replace(":", "_")))
        self.nsem = len(sems)
        waited = {e: {} for e in self.engs}
        for i, o in enumerate(ops):
            kind, eng, fn, rs, ws, key = o
            if kind == "b":
                continue
            e = self.engs[eng]
            for t, j in need[i].items():
                v = val[j]
                if waited[eng].get(t, 0) >= v:
                    continue
                e.wait_ge(sems[t], v)
                waited[eng][t] = v
            inst = fn()
            if kind == "d":
                inst.then_inc(sems[tl[i]], 16)
            elif signaling[i]:
                inst.then_inc(sems[eng], 1)
        sp = self.nc.sync
        for t, c in cnt.items():
            if waited["sp"].get(t, 0) < c:
                sp.wait_ge(sems[t], c)
        self.counts = cnt
        self.vals = val
        self.sig = signaling
        self.need = need


def _t5_bucket_np(rel):
    n = np.maximum(rel, 0)
    nf = np.maximum(n, 1).astype(np.float32)
    large = 16 + (np.log(nf / np.float32(16)) / np.float32(math.log(128 / 16)) * np.float32(16)).astype(np.int32)
    large = np.minimum(large, 31)
    return np.where(n < 16, n, large)


def _host_consts():
    c = {}
    c["ident"] = np.eye(128, dtype=np.float32).astype(ml_dtypes.bfloat16)
    E = np.zeros((33, 2, 256), np.float32)
    m = np.arange(256)
    rel_cur = m - 127
    ok_cur = (rel_cur >= 0) & (rel_cur <= 127)
    bc = _t5_bucket_np(rel_cur)
    rel_prev = m + 1
    ok_prev = m <= 126
    bp = _t5_bucket_np(rel_prev)
    for mm in range(256):
        if ok_prev[mm]:
            E[bp[mm], 0, mm] = 1.0
        else:
            E[32, 0, mm] = 1.0
        if ok_cur[mm]:
            E[bc[mm], 1, mm] = 1.0
        else:
            E[32, 1, mm] = 1.0
    c["etab"] = E.astype(ml_dtypes.bfloat16)
    Es = np.zeros((32, 128), np.float32)
    bs = _t5_bucket_np(127 - np.arange(128))
    Es[bs, np.arange(128)] = 1.0
    c["etab_s"] = Es.astype(ml_dtypes.bfloat16)
    jj = np.arange(128)[:, None]
    ii = np.arange(128)[None, :]
    c["maskT"] = (ii >= jj).astype(np.float32)
    half = 128
    inv = (np.float32(ROPE_BASE) ** (-np.arange(half, dtype=np.float32) / np.float32(half))).astype(np.float32)
    pos = np.arange(SEQ, dtype=np.float32)
    ang = pos[None, :] * inv[:, None]
    cos = np.cos(ang).astype(np.float32)
    sin = np.sin(ang).astype(np.float32)
    lg = np.log1p(-np.exp2(-5.0 - np.arange(4, dtype=np.float64)))
    tin = (np.arange(SEQ) % 128).astype(np.float64)
    rot = np.zeros((4, 4, 4, 128, 512), np.float32)
    for h in range(4):
        fq = np.exp((tin + 1.0) * lg[h])
        fk = np.exp(-(tin + 1.0) * lg[h]) / 16.0
        tabs = [cos * fq[None, :], sin * fq[None, :], cos * fk[None, :], sin * fk[None, :]]
        for t in range(4):
            rot[h, :, t] = tabs[t].astype(np.float32).reshape(128, 4, 512).transpose(1, 0, 2)
    c["rot"] = rot
    c["g128"] = [float(np.exp(128.0 * lg[h])) for h in range(4)]
    c["g1"] = [float(np.exp(lg[h])) for h in range(4)]
    angs = np.float32(PAST) * inv
    rs = np.stack([np.cos(angs), np.sin(angs)], axis=1).astype(np.float32)
    c["rot_s"] = np.concatenate([rs, rs / np.float32(16.0)], axis=1).astype(np.float32)
    c["eyeq"] = np.tile(np.eye(16, dtype=np.float32).reshape(1, 256), (128, 1)).astype(ml_dtypes.bfloat16)
    return c


HC = _host_consts()


def build(stage="full"):
    nc = bass.Bass("TRN2", target_bir_lowering=False)
    es = ExitStack()
    P = Prog(nc, es)

    def din(name, shape, dt=F32):
        return nc.dram_tensor(name, list(shape), dt, kind="ExternalInput").ap()

    def dout(name, shape, dt=F32):
        return nc.dram_tensor(name, list(shape), dt, kind="ExternalOutput").ap()

    x_prompt = din("x_prompt", [NSEQ, SEQ, D])
    w_in_attn = din("w_in_attn_r", [4, 128, 8 * 1344])
    w_out_attn = din("w_out_attn_r", [4, 128, 4096])
    w_in_ret = din("w_in_ret_r", [4, 128, 8 * 1536])
    w_out_ret = din("w_out_ret_r", [4, 128, 4096])
    norm_gT = din("norm_gT", [2, 128, 8])
    final_g = din("final_norm_g", [1, D])
    rel_bias = din("rel_bias", [32, 32])
    sinks = din("attn_sinks", [1, 32])
    c_ident = din("c_ident", [128, 128], BF16)
    c_etab = din("c_etab", [33, 2, 256], BF16)
    c_maskT = din("c_maskT", [128, 128])
    c_rot = din("c_rot", [4, 4, 4, 128, 512])

    if stage == "L0":
        din = lambda name, shape, dt=F32: None
        dout_real = dout
        dout = lambda name, shape, dt=F32: None
    x_sample = din("x_sample", [SB, D])
    cache_k = din("cache_k", [SB, 128, 256])
    cache_v = din("cache_v", [SB, 128, 256])
    state_ret = din("state_ret", [SB, 4, 256, 512])
    c_etab_s = din("c_etab_s", [32, 128], BF16)
    c_eyeq = din("c_eyeq", [128, 256], BF16)
    c_rot_s = din("c_rot_s", [128, 4])
    y_sample = dout("y_sample", [SB, D])
    swa_k_s = dout("swa_k_sample", [SB, 128, 256])
    swa_v_s = dout("swa_v_sample", [SB, 128, 256])
    ret_s = dout("ret_state_sample", [SB, 4, 256, 512])
    osamp = nc.dram_tensor("osamp", [4, SB, 512], F32, kind="Internal").ap()
    if stage == "L0":
        dout = dout_real
    y_prompt = dout("y_prompt", [NSEQ, SEQ, D])
    swa_k_p = dout("swa_k_prompt", [NSEQ, 128, 4, 64])
    swa_v_p = dout("swa_v_prompt", [NSEQ, 128, 4, 64])
    ret_p = dout("ret_state_prompt", [NSEQ, 4, 256, 512])
    dbg_x1 = dout("dbg_x1", [NSEQ, SEQ, D]) if stage == "L0" else None

    biasA = nc.dram_tensor("biasA", [2, 32, ALEN], BF16, kind="Internal")

    def sb(name, shape, dt=F32):
        return es.enter_context(nc.sbuf_tensor(name, list(shape), dt))

    def ps(name, shape=(128, 512), dt=F32):
        return es.enter_context(nc.psum_tensor(name, list(shape), dt))

    xres = sb("xres", [128, NB, D])
    hT = sb("hT", [128, 8, SEQ], BF16)
    win = [sb("win%d" % i, [128, 8, 1536], BF16) for i in range(2)]
    wout = [sb("wout0", [128, 4, D], BF16)]
    win_l0 = [w_[:, :, :].rearrange("p k c -> p (k c)")[:, 0:8 * 1344].rearrange("p (k c) -> p k c", k=8) for w_ in win]
    lscr = sb("lscr", [128, 4096])
    work = sb("work", [128, 37 * 256])
    ident = sb("ident", [128, 128], BF16)
    maskT = sb("maskT", [128, 128])
    gT = sb("gT", [128, 2, 8])
    es2 = sb("es2", [128, 32])
    ss = sb("ss", [128, NB])
    rstd = sb("rstd", [128, NB])
    small = sb("small", [128, 64])
    nhalf = sb("nhalf", [128, 2])

    banks = [ps("bank%d" % i) for i in range(8)]

    biasT = lscr[:, :].bitcast(BF16).rearrange("p (c h i) -> p c h i", c=2, h=32)
    rott = [lscr[:, s * 2048:(s + 1) * 2048].rearrange("p (t c) -> p t c", t=4) for s in range(2)]

    class Carver:
        def __init__(self):
            self.off = 0

        def f32(self, n):
            a = work[:, self.off:self.off + n]
            self.off += n
            assert self.off <= 37 * 256, self.off
            return a

        def bf(self, n):
            assert n % 2 == 0
            a = work[:, self.off:self.off + n // 2].bitcast(BF16)
            self.off += n // 2
            assert self.off <= 37 * 256, self.off
            return a

    cA = Carver()
    xs_bf = [cA.bf(1024), cA.bf(1024)]
    offA = cA.off
    c0 = Carver()
    c0.off = offA
    QT = [c0.bf(2048).rearrange("p (q t) -> p q t", q=4) for _ in range(2)]
    KTA = c0.bf(1024)
    KTB = c0.bf(1024)
    V65 = c0.bf(8 * 66).rearrange("p (b d) -> p b d", b=8)
    PT = [[[c0.bf(512).rearrange("p (q t) -> p q t", q=4) for _ in range(2)] for _ in range(2)] for _ in range(2)]
    t0 = c0.f32(512)
    u0 = [c0.f32(512), c0.f32(512)]
    o2 = c0.f32(512)
    og = c0.bf(512)
    ogT = c0.bf(512).rearrange("p (f t) -> p f t", f=4)
    kvout = c0.f32(128)
    c1 = Carver()
    c1.off = offA
    qT = [c1.bf(1024).rearrange("p (c t) -> p c t", c=2) for _ in range(2)]
    kT = [c1.bf(1024).rearrange("p (c t) -> p c t", c=2) for _ in range(2)]
    ra = c1.f32(512)
    rb = c1.f32(512)
    vb = [c1.bf(512), c1.bf(512)]
    t1 = c1.f32(512)
    u1 = [c1.f32(512), c1.f32(512)]
    sT = [c1.bf(128), c1.bf(128)]
    sraw = c1.f32(128)
    kt = [c1.bf(256), c1.bf(256)]
    Rflat = c1.f32(1024)
    Rst = Rflat.rearrange("p (c v) -> p c v", c=2)
    Rbf = c1.bf(1024).rearrange("p (c v) -> p c v", c=2)
    on = c1.f32(512)
    gtd = c1.bf(512)
    gTt = c1.bf(512).rearrange("p (f t) -> p f t", f=4)
    gfin = Rflat

    PP = [0, 1]
    BL = [[2, 3], [4, 5]]
    BO = 6
    BX = 7
    bkey = lambda i: ("ps", i)
    ppc = [0]

    def next_pp():
        b = PP[ppc[0] % len(PP)]
        ppc[0] += 1
        return b

    bx_bf = banks[BX][:, 0:256].bitcast(BF16)
    pT8 = banks[BX][:, :].bitcast(BF16).rearrange("p (k t) -> p k t", k=8)

    wslot = [0]

    def load_win(layer, g):
        s = wslot[0] % 2
        wslot[0] += 1
        flat = win[s][:, :, :].rearrange("p k c -> p (k c)")
        if layer == 0:
            n, src = 8 * 1344, w_in_attn
        else:
            n, src = 8 * 1536, w_in_ret
        P.dma("pool", lambda: nc.gpsimd.dma_start(out=flat[:, 0:n].rearrange("p (a b) -> p a b", a=6),
                                                  in_=src[g].rearrange("p (a b) -> p a b", a=6)),
              w=[("win", s)], key="win%d" % s)
        return s

    def load_wout(layer, g):
        wsrc = w_out_attn if layer == 0 else w_out_ret
        P.dma("pool", lambda: nc.gpsimd.dma_start(out=wout[0][:, :, :].rearrange("p f c -> p (f c)").rearrange("p (a b) -> p a b", a=2),
                                                  in_=wsrc[g].rearrange("p (a b) -> p a b", a=2)),
              w=[("wout", 0)], key="wout0")

    early_slot = None
    if stage != "L0":
        early_slot = load_win(0, 0)
        load_wout(0, 0)
    P.op("pool", lambda: nc.gpsimd.memset(nhalf[:, :], -0.5), w=["nhalf"])
    P.dma("sp", lambda: nc.sync.dma_start(out=ident[:, :], in_=c_ident[:, :]), w=["ident"], key="const")
    P.dma("sp", lambda: nc.sync.dma_start(out=maskT[:, :], in_=c_maskT[:, :]), w=["maskT"], key="const")
    P.dma("sp", lambda: nc.sync.dma_start(out=gT[:, :, :], in_=norm_gT.rearrange("l p k -> p l k")), w=["gT"], key="const")
    P.dma("sp", lambda: nc.sync.dma_start(out=es2[:, :], in_=sinks[0:1, :].to_broadcast([128, 32])), w=["es2"], key="const")
    rb33 = work[0:33, 0:32]
    rb33b = work[0:33, 32:48].bitcast(BF16)
    etab = work[0:33, 64:320].bitcast(BF16).rearrange("p (c m) -> p c m", c=2)
    tsb = work[0:32, 320:576].bitcast(BF16).rearrange("p (c m) -> p c m", c=2)
    P.op("dve", lambda: nc.vector.memset(work[0:64, 0:32], NEG), w=["rb33"], tiny=True)
    P.dma("sp", lambda: nc.sync.dma_start(out=work[0:32, 0:32], in_=rel_bias[:, :]), r=[], w=["rb33"], key="const")
    P.dma("sp", lambda: nc.sync.dma_start(out=etab, in_=c_etab[:, :, :]), w=["etab"], key="const")
    P.barrier()
    P.op("act", lambda: nc.scalar.activation(out=es2[:, :], in_=es2[:, :], func=AF.Exp), r=["es2"], w=["es2"], tiny=True)
    P.op("dve", lambda: nc.vector.tensor_scalar(out=es2[:, :], in0=es2[:, :], scalar1=2.0, scalar2=None, op0=ALU.mult),
         r=["es2"], w=["es2"], tiny=True)
    P.op("dve", lambda: nc.vector.tensor_scalar(out=rb33b, in0=rb33, scalar1=8.0, scalar2=None, op0=ALU.mult),
         r=["rb33"], w=["rb33b"], tiny=True)

    def mk_tp():
        nc.tensor.matmul(banks[0][0:32, 0:256], lhsT=rb33b, rhs=etab[:, 0, :], start=True, stop=True)
        return nc.tensor.matmul(banks[0][0:32, 256:512], lhsT=rb33b, rhs=etab[:, 1, :], start=True, stop=True)

    P.op("pe", mk_tp, r=["rb33b", "etab"], w=[bkey(0)])
    P.op("dve", lambda: nc.vector.tensor_copy(out=tsb, in_=banks[0][0:32, :].rearrange("p (c m) -> p c m", c=2)),
         r=[bkey(0)], w=["tsb"])
    def biasA_dmas():
        for pc in range(2):
            P.dma("sp", lambda pc=pc: nc.sync.dma_start(
                out=biasA.ap()[pc].rearrange("h (r m) -> h r m", m=256),
                in_=tsb[:, pc, :].unsqueeze(1).to_broadcast([32, 130, 256])), r=["tsb"], w=["biasA"], key="biasA")

    if stage == "L0":
        biasA_dmas()
    P.barrier()

    def phase_a(layer):
        P.op("pool", lambda: nc.gpsimd.memset(ss[:, :], 0.0), w=["ss"] + [("ss", b) for b in range(NB)])
        for b in range(NB):
            s = b % 2
            P.op("act", lambda b=b, s=s: nc.scalar.activation(out=xs_bf[s], in_=xres[:, b, :], func=AF.Square,
                                                               accum_out=ss[:, b:b + 1]),
                 r=[("x", b), "ss"], w=[("xs", s), ("ss", b)])
            P.op("pool", lambda b=b: nc.gpsimd.tensor_scalar(out=rstd[:, b:b + 1], in0=ss[:, b:b + 1], scalar1=1.0 / D,
                                                              scalar2=float(EPS), op0=ALU.mult, op1=ALU.add),
                 r=[("ss", b)], w=[("rstd", b)], tiny=True)
            P.op("pool", lambda b=b: nc.gpsimd.tensor_tensor(out=rstd[:, b:b + 1], in0=rstd[:, b:b + 1], in1=nhalf[:, 0:1],
                                                              op=ALU.pow),
                 r=[("rstd", b), "nhalf"], w=[("rstd", b)], tiny=True)
            P.op("dve", lambda b=b, s=s: nc.vector.tensor_scalar(out=xs_bf[s], in0=xres[:, b, :],
                                                                  scalar1=rstd[:, b:b + 1], scalar2=None,
                                                                  op0=ALU.mult),
                 r=[("x", b), ("rstd", b)], w=[("xs", s)])

            def tr(b=b, s=s):
                for k in range(8):
                    i = nc.tensor.transpose(out=pT8[:, k, :], in_=xs_bf[s][:, k * 128:(k + 1) * 128], identity=ident[:, :])
                return i

            P.op("pe", tr, r=[("xs", s), "ident"], w=[bkey(BX)])
            P.op("dve", lambda b=b, layer=layer: nc.vector.tensor_tensor(
                out=hT[:, :, b * 128:(b + 1) * 128], in0=pT8,
                in1=gT[:, layer, :].unsqueeze(2).to_broadcast([128, 8, 128]), op=ALU.mult),
                r=[bkey(BX), "gT"], w=[("hT", b)])

    def l0_group(seq, g, s, last_group_cb):
        wi, wo = win_l0[s], wout[0]
        wk = ("win", s)
        O7 = banks[BO][:, 0:455].rearrange("p (h d) -> p h d", h=7)
        den = small[:, 0:8]
        rr = small[:, 8:16]
        o2v = o2.rearrange("p (h d) -> p h d", h=8)

        def proj(sbi):
            ring = sbi % 2
            QTc = QT[sbi % 2]
            toks = slice(sbi * 512, (sbi + 1) * 512)
            hkeys = [("hT", sbi * 4 + i) for i in range(4)]
            for p in range(4):
                bk = next_pp()

                def mmq(p=p, bk=bk):
                    for k in range(8):
                        i = nc.tensor.matmul(banks[bk][:, :], lhsT=wi[:, k, p * 128:(p + 1) * 128], rhs=hT[:, k, toks],
                                             start=(k == 0), stop=(k == 7))
                    return i

                P.op("pe", mmq, r=[wk] + hkeys, w=[bkey(bk)])
                P.op("dve", lambda p=p, bk=bk: nc.vector.tensor_copy(out=QTc[:, p, :], in_=banks[bk][:, :]),
                     r=[bkey(bk)], w=[("QT", sbi % 2)])
            for which, (c0_, dst) in enumerate(((512, KTA), (640, KTB))):
                bk = next_pp()

                def mmk(c0_=c0_, bk=bk):
                    for k in range(8):
                        i = nc.tensor.matmul(banks[bk][:, :], lhsT=wi[:, k, c0_:c0_ + 128], rhs=hT[:, k, toks],
                                             start=(k == 0), stop=(k == 7))
                    return i

                P.op("pe", mmk, r=[wk] + hkeys, w=[bkey(bk)])
                P.op("act", lambda dst=dst, bk=bk: nc.scalar.copy(out=dst[:, ring * 512:(ring + 1) * 512], in_=banks[bk][:, :]),
                     r=[bkey(bk)], w=[("KT", which, ring)])
            bk = next_pp()

            def mmv(bk=bk):
                for blk in range(4):
                    b = sbi * 4 + blk
                    for k in range(8):
                        i = nc.tensor.matmul(banks[bk][:, blk * 64:(blk + 1) * 64], lhsT=hT[:, k, b * 128:(b + 1) * 128],
                                             rhs=wi[:, k, 768:832], start=(k == 0), stop=(k == 7))
                return i

            P.op("pe", mmv, r=[wk] + hkeys, w=[bkey(bk)])
            P.op("dve", lambda bk=bk: nc.vector.tensor_copy(
                out=V65[:, ring * 4:(ring + 1) * 4, 0:64], in_=banks[bk][:, 0:256].rearrange("p (b d) -> p b d", b=4)),
                r=[bkey(bk)], w=[("V", ring)])
            if sbi == 3:
                P.op("dve", lambda bk=bk: nc.vector.tensor_copy(out=kvout[:, 64:128], in_=banks[bk][:, 192:256]),
                     r=[bkey(bk)], w=["vout"])
                P.dma("sp", lambda: nc.sync.dma_start(out=swa_v_p[seq, :, g, :], in_=kvout[:, 64:128]), r=["vout"], w=[], key="vout")
                bk2 = next_pp()

                def mmko(bk2=bk2):
                    for k in range(8):
                        i = nc.tensor.matmul(banks[bk2][:, 0:64], lhsT=hT[:, k, 15 * 128:16 * 128], rhs=wi[:, k, 512:576],
                                             start=(k == 0), stop=(k == 7))
                    return i

                P.op("pe", mmko, r=[wk, ("hT", 15)], w=[bkey(bk2)])
                P.op("dve", lambda bk2=bk2: nc.vector.tensor_copy(out=kvout[:, 0:64], in_=banks[bk2][:, 0:64]),
                     r=[bkey(bk2)], w=["kout"])
                P.dma("sp", lambda: nc.sync.dma_start(out=swa_k_p[seq, :, g, :], in_=kvout[:, 0:64]), r=["kout"], w=[], key="kout")

        def stA(b):
            sbi, blk = b // 4, b % 4
            par = b % 2
            QTc = QT[sbi % 2]
            rb_cur = (sbi % 2) * 4 + blk
            rb_prev = (rb_cur - 1) % 8
            pcs = [1] if b == 0 else [0, 1]
            bg = next_pp()

            def mmg():
                for k in range(8):
                    i = nc.tensor.matmul(banks[bg][:, :], lhsT=hT[:, k, b * 128:(b + 1) * 128], rhs=wi[:, k, 832:1344],
                                         start=(k == 0), stop=(k == 7))
                return i

            P.op("pe", mmg, r=[wk, ("hT", b)], w=[bkey(bg)])
            P.op("act", lambda: nc.scalar.activation(out=t0, in_=banks[bg][:, :], func=AF.Tanh, scale=0.5), r=[bkey(bg)], w=["t0"])
            P.op("dve", lambda: nc.vector.scalar_tensor_tensor(out=u0[par], in0=t0, scalar=1.0, in1=banks[bg][:, :],
                                                               op0=ALU.add, op1=ALU.mult), r=[bkey(bg), "t0"], w=[("u0", par)])
            for m in range(2):
                KT = KTA if m == 0 else KTB
                for pc in pcs:
                    bl = BL[m][pc]
                    rbk = rb_prev if pc == 0 else rb_cur

                    def mml(bl=bl, KT=KT, rbk=rbk, pc=pc, m=m):
                        nc.tensor.matmul(banks[bl][:, :], lhsT=KT[:, rbk * 128:(rbk + 1) * 128],
                                         rhs=QTc[:, :, blk * 128:(blk + 1) * 128], start=True, stop=False)
                        return nc.tensor.matmul(banks[bl][:, :], lhsT=ident[:, :],
                                                rhs=biasT[:, pc, g * 8 + m:g * 8 + 8:2, :], start=False, stop=True)

                    P.op("pe", mml, r=[("QT", sbi % 2), ("KT", m, rbk // 4), "ident", "biasT"], w=[bkey(bl)])
                    P.op("act", lambda bl=bl, m=m, pc=pc: nc.scalar.activation(
                        out=PT[par][m][pc], in_=banks[bl][:, :].rearrange("p (q t) -> p q t", q=4),
                        func=AF.Exp, scale=0.125), r=[bkey(bl)], w=[("PT", par, m, pc)])

        def stB(b):
            sbi, blk = b // 4, b % 4
            par = b % 2
            rb_cur = (sbi % 2) * 4 + blk
            rb_prev = (rb_cur - 1) % 8
            pcs = [1] if b == 0 else [0, 1]

            def mmpv():
                for hl in range(8):
                    p, m = hl // 2, hl % 2
                    out = banks[BO][:, hl * 65:(hl + 1) * 65] if hl < 7 else banks[BX][:, 256:321]
                    for n_, pc in enumerate(pcs):
                        rbk = rb_prev if pc == 0 else rb_cur
                        i = nc.tensor.matmul(out, lhsT=PT[par][m][pc][:, p, :], rhs=V65[:, rbk, 0:65],
                                             start=(n_ == 0), stop=(n_ == len(pcs) - 1))
                return i

            P.op("pe", mmpv, r=[("PT", par, 0, 0), ("PT", par, 0, 1), ("PT", par, 1, 0), ("PT", par, 1, 1), ("V", 0), ("V", 1)],
                 w=[bkey(BO), bkey(BX)])
            P.op("dve", lambda: nc.vector.scalar_tensor_tensor(
                out=den[:, 0:7], in0=O7[:, :, 64], scalar=2.0, in1=es2[:, g * 8:g * 8 + 7], op0=ALU.mult, op1=ALU.add),
                r=[bkey(BO), "es2"], w=["den7"], tiny=True)
            P.op("dve", lambda: nc.vector.scalar_tensor_tensor(
                out=den[:, 7:8], in0=banks[BX][:, 320:321], scalar=2.0, in1=es2[:, g * 8 + 7:g * 8 + 8],
                op0=ALU.mult, op1=ALU.add), r=[bkey(BX), "es2"], w=["den1"], tiny=True)
            P.op("dve", lambda: nc.vector.reciprocal(out=rr, in_=den), r=["den7", "den1"], w=["rr"], tiny=True)
            P.op("dve", lambda: nc.vector.tensor_tensor(out=o2v[:, 0:7, :], in0=O7[:, :, 0:64],
                                                        in1=rr[:, 0:7].unsqueeze(2).to_broadcast([128, 7, 64]), op=ALU.mult),
                 r=[bkey(BO), "rr"], w=["o2a"])
            P.op("dve", lambda: nc.vector.tensor_scalar(out=o2[:, 448:512], in0=banks[BX][:, 256:320], scalar1=rr[:, 7:8],
                                                        scalar2=None, op0=ALU.mult), r=[bkey(BX), "rr"], w=["o2b"])
            P.op("dve", lambda: nc.vector.tensor_tensor(out=og, in0=o2, in1=u0[par], op=ALU.mult),
                 r=["o2a", "o2b", ("u0", par)], w=["og"])

        def stT(b):
            def tro():
                for f in range(4):
                    i = nc.tensor.transpose(out=bx_bf[:, f * 128:(f + 1) * 128], in_=og[:, f * 128:(f + 1) * 128],
                                            identity=ident[:, :])
                return i

            P.op("pe", tro, r=["og", "ident"], w=[bkey(BX)])
            P.op("dve", lambda: nc.vector.tensor_copy(out=ogT, in_=bx_bf.rearrange("p (f t) -> p f t", f=4)),
                 r=[bkey(BX)], w=["ogT"])

        def stY(b):
            for n_ in range(2):
                by = next_pp()

                def mmy(by=by, n_=n_):
                    for f in range(4):
                        i = nc.tensor.matmul(banks[by][:, :], lhsT=ogT[:, f, :], rhs=wo[:, f, n_ * 512:(n_ + 1) * 512],
                                             start=(f == 0), stop=(f == 3))
                    return i

                P.op("pe", mmy, r=["ogT", ("wout", 0)], w=[bkey(by)])
                P.op("dve", lambda by=by, n_=n_: nc.vector.tensor_tensor(
                    out=xres[:, b, n_ * 512:(n_ + 1) * 512], in0=xres[:, b, n_ * 512:(n_ + 1) * 512],
                    in1=banks[by][:, :], op=ALU.add), r=[bkey(by), ("x", b)], w=[("x", b)])

        proj(0)
        for i in range(NB + 2):
            if 0 <= i - 2 < NB:
                stT(i - 2)
            if i < NB:
                stA(i)
            if 0 <= i - 1 < NB:
                stB(i - 1)
            if 0 <= i - 2 < NB:
                stY(i - 2)
            if i % 4 == 1 and i // 4 + 1 < 4:
                proj(i // 4 + 1)
        last_group_cb()

    def l1_head(seq, h, s, last_cb):
        wi, wo = win[s], wout[0]
        wk = ("win", s)
        G = HC["g128"][h]
        P.op("pool", lambda: nc.gpsimd.memset(Rst, 0.0), w=["R"])
        P.op("pool", lambda: nc.gpsimd.memset(Rbf, 0.0), w=["Rbf"])
        st6 = small[:, 16:22]
        mv = small[:, 22:24]
        rs_ = small[:, 24:25]
        nb_ = small[:, 25:26]
        b3v = banks[BX][:, 256:384].bitcast(BF16)
        kxa = kxb = bkey(BX)

        def proj(sbi, parts=(0, 1)):
            rs = (h * 4 + sbi) % 2
            par = sbi % 2
            toks = slice(sbi * 512, (sbi + 1) * 512)
            hkeys = [("hT", sbi * 4 + i) for i in range(4)]
            if 0 in parts:
                P.dma("sp", lambda: nc.sync.dma_start(out=rott[rs], in_=c_rot[h, sbi].rearrange("t j c -> j t c")),
                      w=[("rot", rs)], key="rot%d" % rs)
            for which, (c0_, dstT) in enumerate(((0, qT[par]), (256, kT[par]))):
                if which not in parts:
                    continue
                b1 = next_pp()
                b2 = next_pp()

                def mmqk(c0_=c0_, b1=b1, b2=b2):
                    for dc, bk in ((0, b1), (1, b2)):
                        for k in range(8):
                            i = nc.tensor.matmul(banks[bk][:, :], lhsT=wi[:, k, c0_ + dc * 128:c0_ + (dc + 1) * 128],
                                                 rhs=hT[:, k, toks], start=(k == 0), stop=(k == 7))
                    return i

                P.op("pe", mmqk, r=[wk] + hkeys, w=[bkey(b1), bkey(b2)])
                C = rott[rs][:, 2 * which, :]
                S = rott[rs][:, 2 * which + 1, :]

                def rot(b1=b1, b2=b2, C=C, S=S, dstT=dstT):
                    nc.vector.tensor_tensor(out=ra, in0=banks[b1][:, :], in1=C, op=ALU.mult)
                    nc.vector.tensor_tensor(out=rb, in0=banks[b2][:, :], in1=S, op=ALU.mult)
                    nc.vector.tensor_tensor(out=dstT[:, 0, :], in0=ra, in1=rb, op=ALU.subtract)
                    nc.vector.tensor_tensor(out=ra, in0=banks[b1][:, :], in1=S, op=ALU.mult)
                    nc.vector.tensor_tensor(out=rb, in0=banks[b2][:, :], in1=C, op=ALU.mult)
                    return nc.vector.tensor_tensor(out=dstT[:, 1, :], in0=ra, in1=rb, op=ALU.add)

                P.op("dve", rot, r=[bkey(b1), bkey(b2), ("rot", rs)], w=[("qk", which, par), "rab"])

        def stA(b):
            sbi, blk = b // 4, b % 4
            par, sp_ = b % 2, sbi % 2
            cs = slice(blk * 128, (blk + 1) * 128)

            bv = next_pp()

            def mmv():
                for k in range(8):
                    i = nc.tensor.matmul(banks[bv][:, :], lhsT=hT[:, k, b * 128:(b + 1) * 128], rhs=wi[:, k, 512:1024],
                                         start=(k == 0), stop=(k == 7))
                return i

            P.op("pe", mmv, r=[wk, ("hT", b)], w=[bkey(bv)])
            P.op("act", lambda: nc.scalar.copy(out=vb[par], in_=banks[bv][:, :]), r=[bkey(bv)], w=[("v", par)])
            bg = next_pp()

            def mmg():
                for k in range(8):
                    i = nc.tensor.matmul(banks[bg][:, :], lhsT=hT[:, k, b * 128:(b + 1) * 128], rhs=wi[:, k, 1024:1536],
                                         start=(k == 0), stop=(k == 7))
                return i

            P.op("pe", mmg, r=[wk, ("hT", b)], w=[bkey(bg)])
            P.op("act", lambda: nc.scalar.activation(out=t1, in_=banks[bg][:, :], func=AF.Tanh, scale=0.5), r=[bkey(bg)], w=["t1"])
            P.op("dve", lambda: nc.vector.scalar_tensor_tensor(out=u1[par], in0=t1, scalar=1.0, in1=banks[bg][:, :],
                                                               op0=ALU.add, op1=ALU.mult), r=[bkey(bg), "t1"], w=[("u1", par)])

            def mms():
                nc.tensor.matmul(banks[BX][:, 384:512], lhsT=kT[sp_][:, 0, cs], rhs=qT[sp_][:, 0, cs], start=True, stop=False)
                return nc.tensor.matmul(banks[BX][:, 384:512], lhsT=kT[sp_][:, 1, cs], rhs=qT[sp_][:, 1, cs], start=False, stop=True)

            P.op("pe", mms, r=[("qk", 0, sp_), ("qk", 1, sp_)], w=[kxb])
            P.op("act", lambda: nc.scalar.copy(out=sraw, in_=banks[BX][:, 384:512]), r=[kxb], w=["sraw"])
            P.op("pool", lambda: nc.gpsimd.tensor_tensor(out=sT[par], in0=sraw, in1=maskT[:, :], op=ALU.mult),
                 r=["sraw", "maskT"], w=[("sT", par)])

            def trk():
                nc.tensor.transpose(out=b3v[:, 0:128], in_=kT[sp_][:, 0, cs], identity=ident[:, :])
                return nc.tensor.transpose(out=b3v[:, 128:256], in_=kT[sp_][:, 1, cs], identity=ident[:, :])

            P.op("pe", trk, r=[("qk", 1, sp_), "ident"], w=[kxb])
            P.op("act", lambda: nc.scalar.activation(out=kt[par], in_=b3v, func=AF.Identity, scale=G), r=[kxb], w=[("kt", par)])

        def stB(b):
            sbi, blk = b // 4, b % 4
            par, sp_ = b % 2, sbi % 2
            cs = slice(blk * 128, (blk + 1) * 128)

            def mmo():
                nc.tensor.matmul(banks[4][:, :], lhsT=sT[par], rhs=vb[par], start=True, stop=False)
                nc.tensor.matmul(banks[4][:, :], lhsT=qT[sp_][:, 0, cs], rhs=Rbf[:, 0, :], start=False, stop=False)
                return nc.tensor.matmul(banks[4][:, :], lhsT=qT[sp_][:, 1, cs], rhs=Rbf[:, 1, :], start=False, stop=True)

            P.op("pe", mmo, r=[("sT", par), ("v", par), ("qk", 0, sp_), "Rbf"], w=[bkey(4)])

            def mmu():
                nc.tensor.matmul(banks[5][:, :], lhsT=kt[par][:, 0:128], rhs=vb[par], start=True, stop=True)
                return nc.tensor.matmul(banks[6][:, :], lhsT=kt[par][:, 128:256], rhs=vb[par], start=True, stop=True)

            P.op("pe", mmu, r=[("kt", par), ("v", par)], w=[bkey(5), bkey(6)])

            def upd():
                nc.vector.scalar_tensor_tensor(out=Rst[:, 0, :], in0=Rst[:, 0, :], scalar=G, in1=banks[5][:, :],
                                               op0=ALU.mult, op1=ALU.add)
                return nc.vector.scalar_tensor_tensor(out=Rst[:, 1, :], in0=Rst[:, 1, :], scalar=G, in1=banks[6][:, :],
                                                      op0=ALU.mult, op1=ALU.add)

            P.op("dve", upd, r=[bkey(5), bkey(6), "R"], w=["R"])
            if b < NB - 1:
                P.op("act", lambda: nc.scalar.copy(out=Rbf, in_=Rst), r=["R"], w=["Rbf"])
            else:
                P.dma("sp", lambda: nc.sync.dma_start(out=ret_p[seq, h].rearrange("(c p) v -> p c v", p=128), in_=Rst),
                      r=["R"], w=[], key="rout")
            P.op("dve", lambda: nc.vector.bn_stats(out=st6, in_=banks[4][:, :]), r=[bkey(4)], w=["st6"], tiny=True)
            P.op("dve", lambda: nc.vector.bn_aggr(out=mv, in_=st6), r=["st6"], w=["mv"], tiny=True)
            P.op("pool", lambda: nc.gpsimd.tensor_scalar(out=rs_, in0=mv[:, 1:2], scalar1=4.0, scalar2=float(4.0 * EPS),
                                                         op0=ALU.mult, op1=ALU.add), r=["mv"], w=["rs_"], tiny=True)
            P.op("pool", lambda: nc.gpsimd.tensor_tensor(out=rs_, in0=rs_, in1=nhalf[:, 0:1], op=ALU.pow),
                 r=["rs_", "nhalf"], w=["rs_"], tiny=True)
            P.op("pool", lambda: nc.gpsimd.tensor_scalar(out=nb_, in0=mv[:, 0:1], scalar1=rs_, scalar2=-1.0,
                                                         op0=ALU.mult, op1=ALU.mult), r=["mv", "rs_"], w=["nb_"], tiny=True)

        def stB2(b):
            par = b % 2
            P.op("act", lambda: nc.scalar.activation(out=on, in_=banks[4][:, :], func=AF.Identity, scale=rs_, bias=nb_),
                 r=[bkey(4), "rs_", "nb_"], w=["on"])
            P.op("dve", lambda: nc.vector.tensor_tensor(out=gtd, in0=on, in1=u1[par], op=ALU.mult),
                 r=["on", ("u1", par)], w=["gtd"])

        def stT(b):
            def trg():
                for f in range(4):
                    i = nc.tensor.transpose(out=bx_bf[:, f * 128:(f + 1) * 128], in_=gtd[:, f * 128:(f + 1) * 128],
                                            identity=ident[:, :])
                return i

            P.op("pe", trg, r=["gtd", "ident"], w=[kxa])
            P.op("act", lambda: nc.scalar.copy(out=gTt, in_=bx_bf.rearrange("p (f t) -> p f t", f=4)), r=[kxa], w=["gTt"])

        def stY(b):
            for n_ in range(2):
                by = next_pp()

                def mmy(by=by, n_=n_):
                    for f in range(4):
                        i = nc.tensor.matmul(banks[by][:, :], lhsT=gTt[:, f, :], rhs=wo[:, f, n_ * 512:(n_ + 1) * 512],
                                             start=(f == 0), stop=(f == 3))
                    return i

                P.op("pe", mmy, r=["gTt", ("wout", 0)], w=[bkey(by)])
                P.op("dve", lambda by=by, n_=n_: nc.vector.tensor_tensor(
                    out=xres[:, b, n_ * 512:(n_ + 1) * 512], in0=xres[:, b, n_ * 512:(n_ + 1) * 512],
                    in1=banks[by][:, :], op=ALU.add), r=[bkey(by), ("x", b)], w=[("x", b)])

        proj(0)
        for i in range(NB + 3):
            if 0 <= i - 3 < NB:
                stT(i - 3)
            if 0 <= i - 2 < NB:
                stB2(i - 2)
            if i < NB:
                stA(i)
            if 0 <= i - 1 < NB:
                stB(i - 1)
            if 0 <= i - 3 < NB:
                stY(i - 3)
            if i % 4 == 1 and i // 4 + 1 < 4:
                proj(i // 4 + 1, parts=(0,))
            if i % 4 == 2 and i // 4 + 1 < 4:
                proj(i // 4 + 1, parts=(1,))
        last_cb()

    def load_x(seq, b, q="sp"):
        eng = nc.sync if q == "sp" else nc.gpsimd
        P.dma(q, lambda: eng.dma_start(out=xres[:, b, :], in_=x_prompt[seq, b * 128:(b + 1) * 128, :]),
              w=[("x", b)], key="x%d" % b)

    def final_norm(seq):
        P.op("pool", lambda: nc.gpsimd.memset(ss[:, :], 0.0), w=["ss"] + [("ss", b) for b in range(NB)])
        for b in range(NB):
            P.op("act", lambda b=b: nc.scalar.activation(out=xs_bf[0], in_=xres[:, b, :], func=AF.Square, accum_out=ss[:, b:b + 1]),
                 r=[("x", b), "ss"], w=[("xs", 0), ("ss", b)])
            P.op("pool", lambda b=b: nc.gpsimd.tensor_scalar(out=rstd[:, b:b + 1], in0=ss[:, b:b + 1], scalar1=1.0 / D,
                                                              scalar2=float(EPS), op0=ALU.mult, op1=ALU.add),
                 r=[("ss", b)], w=[("rstd", b)], tiny=True)
            P.op("pool", lambda b=b: nc.gpsimd.tensor_tensor(out=rstd[:, b:b + 1], in0=rstd[:, b:b + 1], in1=nhalf[:, 0:1],
                                                              op=ALU.pow),
                 r=[("rstd", b), "nhalf"], w=[("rstd", b)], tiny=True)
            P.op("dve", lambda b=b: nc.vector.scalar_tensor_tensor(out=xres[:, b, :], in0=xres[:, b, :], scalar=rstd[:, b:b + 1],
                                                                   in1=gfin, op0=ALU.mult, op1=ALU.mult),
                 r=[("x", b), ("rstd", b), "gfin"], w=[("x", b)])
            P.dma("sp", lambda b=b: nc.sync.dma_start(out=y_prompt[seq, b * 128:(b + 1) * 128, :], in_=xres[:, b, :]),
                  r=[("x", b)], w=[], key="x%d" % b)
        if seq + 1 < NSEQ:
            for b in range(NB):
                load_x(seq + 1, b, q="pool")


    def sample_phase():
        hw = hT[:, :, :].rearrange("p k t -> p (k t)").bitcast(F32)
        off = [0]

        def f32(n):
            a = hw[:, off[0]:off[0] + n]
            off[0] += n
            assert off[0] <= 8192
            return a

        def bf(n):
            a = hw[:, off[0]:off[0] + n // 2].bitcast(BF16)
            off[0] += n // 2
            assert off[0] <= 8192
            return a

        xs_t = f32(1024)
        xsb = bf(1024)
        hsT = bf(128).rearrange("p (k t) -> p k t", k=8)
        us = f32(512)
        os_ = f32(512)
        t_s = f32(512)
        ogs = bf(512)
        gsT = bf(64).rearrange("p (f t) -> p f t", f=4)
        QTs2 = [bf(64).rearrange("p (q t) -> p q t", q=4) for _ in range(2)]
        us2 = [us, f32(512)]
        knv = f32(128)
        Kpad2 = [bf(192), bf(192)]
        KTp2 = [bf(256).rearrange("p (m j) -> p m j", m=2) for _ in range(2)]
        PTs2 = [bf(8), bf(8)]
        biasS = bf(32)
        etS = bf(128)
        es2s = f32(8).rearrange("p (k m) -> p k m", k=4)
        sm = f32(16)
        ob2 = [f32(128).rearrange("p (m d) -> p m d", m=2) for _ in range(2)]
        qk4 = f32(64).rearrange("p (i t) -> p i t", i=4)
        qf = f32(32).rearrange("p (c t) -> p c t", c=2)
        kf = f32(32).rearrange("p (c t) -> p c t", c=2)
        tmpr = f32(16)
        prodf = bf(32).rearrange("p (c t) -> p c t", c=2)
        qTs = bf(32).rearrange("p (c t) -> p c t", c=2)
        kTs = bf(32).rearrange("p (c t) -> p c t", c=2)
        ktok = bf(256)
        vtokb = bf(512)
        o1 = f32(512)
        gts = bf(512)
        Zq = bf(512).rearrange("p (c s t) -> p c s t", c=2, s=16)
        Zk = work[:, 1024:3072].bitcast(BF16).rearrange("p (s d) -> p s d", s=16)
        eyeq = bf(256).rearrange("p (s t) -> p s t", s=16)
        rots = f32(4)
        onesf = bf(2)
        gfs = work[:, 3072:4096]
        Kw = xres[:, 0:4, :].rearrange("p a (b c) -> p (a b) c", c=256)
        Vw = xres[:, 4:8, :].rearrange("p a (b c) -> p (a b) c", c=256)
        Vw65_2 = [xres[:, r_, :].bitcast(BF16)[:, 0:16 * 66].rearrange("p (b d) -> p b d", b=16) for r_ in (8, 14)]
        Rb = [xres[:, 9 + i, :].rearrange("p (c v) -> p c v", c=2) for i in range(4)]
        Rbfs = [xres[:, 13, :].bitcast(BF16)[:, i * 1024:(i + 1) * 1024].rearrange("p (c v) -> p c v", c=2) for i in range(2)]
        b3bf = banks[3][:, :].bitcast(BF16)
        b0bf = banks[0][:, :].bitcast(BF16)

        P.dma("sp", lambda: nc.sync.dma_start(out=xs_t[0:16, :], in_=x_sample[:, :]), w=["xs_t"], key="sconst")
        for pi, (p0, p1) in enumerate(((0, 112), (112, 127))):
            P.dma("sp", lambda p0=p0, p1=p1: nc.sync.dma_start(out=Kw[p0:p1, :, :],
                                                                in_=cache_k[:, 1 + p0:1 + p1, :].rearrange("b j c -> j b c")),
                  w=[("Kwl", pi)], key="sconst")
            P.dma("sp", lambda p0=p0, p1=p1: nc.sync.dma_start(out=Vw[p0:p1, :, :],
                                                                in_=cache_v[:, 1 + p0:1 + p1, :].rearrange("b j c -> j b c")),
                  w=[("Vwl", pi)], key="sconst")
        P.dma("sp", lambda: nc.sync.dma_start(out=eyeq, in_=c_eyeq.rearrange("p (s t) -> p s t", s=16)), w=["eyeq"], key="sconst")
        P.dma("sp", lambda: nc.sync.dma_start(out=rots, in_=c_rot_s[:, :]), w=["rots"], key="sconst")
        P.dma("sp", lambda: nc.sync.dma_start(out=etS[0:32, :], in_=c_etab_s[:, :]), w=["etS"], key="sconst")
        P.dma("sp", lambda: nc.sync.dma_start(out=es2s[0:4, :, :],
                                              in_=sinks[0].rearrange("(k p m) -> p k m", k=4, p=4, m=2)),
              w=["es2s"], key="sconst")
        P.dma("sp", lambda: nc.sync.dma_start(out=gfs[0:16, :], in_=final_g[0:1, :].to_broadcast([16, D])), w=["gfs"], key="sconst")
        P.op("pool", lambda: nc.gpsimd.memset(onesf, 1.0), w=["onesf"], tiny=True)
        P.op("pool", lambda: nc.gpsimd.memset(Kpad2[0], 0.0), w=[("Kpad", 0)])
        P.op("pool", lambda: nc.gpsimd.memset(Kpad2[1], 0.0), w=[("Kpad", 1)])
        P.op("pool", lambda: nc.gpsimd.memset(Vw65_2[0][:, :, 64:66], 1.0), w=[("Vw65", 0)], tiny=True)
        P.op("pool", lambda: nc.gpsimd.memset(Vw65_2[1][:, :, 64:66], 1.0), w=[("Vw65", 1)], tiny=True)
        P.barrier()
        P.dma("sp", lambda: nc.sync.dma_start(out=swa_k_s[:, 0:127, :], in_=cache_k[:, 1:128, :]), w=[], key="cshift")
        P.dma("sp", lambda: nc.sync.dma_start(out=swa_v_s[:, 0:127, :], in_=cache_v[:, 1:128, :]), w=[], key="cshift")
        biasA_dmas()
        P.op("act", lambda: nc.scalar.activation(out=es2s[0:4, :, :], in_=es2s[0:4, :, :], func=AF.Exp), r=["es2s"], w=["es2s"], tiny=True)
        P.op("dve", lambda: nc.vector.tensor_scalar(out=es2s[0:4, :, :], in0=es2s[0:4, :, :], scalar1=2.0, scalar2=None, op0=ALU.mult),
             r=["es2s"], w=["es2s"], tiny=True)
        P.op("pe", lambda: nc.tensor.matmul(banks[0][:, 0:32], lhsT=etS[0:32, :],
                                            rhs=rb33b[0:32, :].rearrange("b (k p m) -> b k m p", k=4, p=4, m=2),
                                            start=True, stop=True), r=["etS", "rb33b"], w=[bkey(0)])
        P.op("dve", lambda: nc.vector.tensor_copy(out=biasS, in_=banks[0][:, 0:32]), r=[bkey(0)], w=["biasS"], tiny=True)

        def s_norm(layer):
            P.op("pool", lambda: nc.gpsimd.memset(sm[0:16, 0:1], 0.0), w=["sm_ss"], tiny=True)
            P.op("act", lambda: nc.scalar.activation(out=xsb[0:16, :], in_=xs_t[0:16, :], func=AF.Square, accum_out=sm[0:16, 0:1]),
                 r=["xs_t", "sm_ss"], w=["sm_ss", "xsb"], tiny=True)
            P.op("pool", lambda: nc.gpsimd.tensor_scalar(out=sm[0:16, 1:2], in0=sm[0:16, 0:1], scalar1=1.0 / D, scalar2=float(EPS),
                                                         op0=ALU.mult, op1=ALU.add), r=["sm_ss"], w=["sm_r"], tiny=True)
            P.op("pool", lambda: nc.gpsimd.tensor_tensor(out=sm[0:16, 1:2], in0=sm[0:16, 1:2], in1=nhalf[0:16, 0:1], op=ALU.pow),
                 r=["sm_r", "nhalf"], w=["sm_r"], tiny=True)
            P.op("dve", lambda: nc.vector.tensor_scalar(out=xsb[0:16, :], in0=xs_t[0:16, :], scalar1=sm[0:16, 1:2], scalar2=None,
                                                        op0=ALU.mult), r=["xs_t", "sm_r"], w=["xsb"])

            def tr():
                for k in range(8):
                    i = nc.tensor.transpose(out=b3bf[:, k * 16:(k + 1) * 16], in_=xsb[0:16, k * 128:(k + 1) * 128],
                                            identity=ident[0:16, 0:16])
                return i

            P.op("pe", tr, r=["xsb", "ident"], w=[bkey(3)])
            P.op("dve", lambda layer=layer: nc.vector.tensor_tensor(
                out=hsT, in0=b3bf[:, 0:128].rearrange("p (k t) -> p k t", k=8),
                in1=gT[:, layer, :].unsqueeze(2).to_broadcast([128, 8, 16]), op=ALU.mult),
                r=[bkey(3), "gT"], w=["hsT"])

        sorder = [(0, g_) for g_ in range(4)] + [(1, h_) for h_ in range(4)]
        spre = {0: early_slot}

        def sget(i):
            if i < len(sorder) and i not in spre:
                spre[i] = load_win(*sorder[i])
            return spre.get(i)


        sget(0)
        s_norm(0)
        def front(g):
            s = sget(g)
            gp = g % 2
            wi = win_l0[s]
            wk = ("win", s)

            def mmq():
                for p in range(4):
                    for k in range(8):
                        i = nc.tensor.matmul(banks[0][:, p * 16:(p + 1) * 16], lhsT=wi[:, k, p * 128:(p + 1) * 128], rhs=hsT[:, k, :],
                                             start=(k == 0), stop=(k == 7))
                return i

            P.op("pe", mmq, r=[wk, "hsT"], w=[bkey(0)])
            P.op("dve", lambda: nc.vector.tensor_copy(out=QTs2[gp], in_=banks[0][:, 0:64].rearrange("p (q t) -> p q t", q=4)),
                 r=[bkey(0)], w=[("QTs", gp)])

            def mmkv():
                for j, c0_ in enumerate((512, 768)):
                    for k in range(8):
                        i = nc.tensor.matmul(banks[1][0:16, j * 64:(j + 1) * 64], lhsT=hsT[:, k, :], rhs=wi[:, k, c0_:c0_ + 64],
                                             start=(k == 0), stop=(k == 7))
                return i

            P.op("pe", mmkv, r=[wk, "hsT"], w=[bkey(1)])
            P.op("dve", lambda: nc.vector.tensor_copy(out=knv[0:16, :], in_=banks[1][0:16, 0:128]), r=[bkey(1)], w=["knv"])
            P.dma("sp", lambda g=g: nc.sync.dma_start(out=swa_k_s[:, 127, g * 64:(g + 1) * 64], in_=knv[0:16, 0:64]),
                  r=["knv"], w=["skd"], key="skn")
            P.dma("sp", lambda g=g: nc.sync.dma_start(out=swa_v_s[:, 127, g * 64:(g + 1) * 64], in_=knv[0:16, 64:128]),
                  r=["knv"], w=["svd"], key="skn")
            P.dma("sp", lambda g=g: nc.sync.dma_start(out=Kw[127:128, :, g * 64:(g + 1) * 64],
                                                      in_=swa_k_s[:, 127:128, g * 64:(g + 1) * 64].rearrange("b o d -> o b d")),
                  r=["skd", "svd"], w=[("Kw", g)], key="skn2k")
            P.dma("sp", lambda g=g: nc.sync.dma_start(out=Vw[127:128, :, g * 64:(g + 1) * 64],
                                                      in_=swa_v_s[:, 127:128, g * 64:(g + 1) * 64].rearrange("b o d -> o b d")),
                  r=["skd", "svd"], w=[("Vw", g)], key="skn2v")

            def mmg():
                for k in range(8):
                    i = nc.tensor.matmul(banks[2][0:16, :], lhsT=hsT[:, k, :], rhs=wi[:, k, 832:1344], start=(k == 0), stop=(k == 7))
                return i

            P.op("pe", mmg, r=[wk, "hsT"], w=[bkey(2)])
            P.op("act", lambda: nc.scalar.activation(out=t_s[0:16, :], in_=banks[2][0:16, :], func=AF.Tanh, scale=0.5),
                 r=[bkey(2)], w=["t_s"])
            P.op("dve", lambda: nc.vector.scalar_tensor_tensor(out=us2[gp][0:16, :], in0=t_s[0:16, :], scalar=1.0, in1=banks[2][0:16, :],
                                                               op0=ALU.add, op1=ALU.mult), r=[bkey(2), "t_s"], w=[("us", gp)])
            P.op("dve", lambda g=g: nc.vector.tensor_copy(out=Vw65_2[gp][:, :, 0:64], in_=Vw[:, :, g * 64:(g + 1) * 64]),
                 r=[("Vw", g)], w=[("Vw65", gp)])

        front(0)
        for g in range(4):
            s = sget(g)
            sget(g + 1)
            gp = g % 2
            wo = wout[0]
            def S1(b):
                pr = b % 2
                P.op("dve", lambda: nc.vector.tensor_copy(out=Kpad2[pr][:, 64:128], in_=Kw[:, b, g * 64:(g + 1) * 64]),
                     r=[("Kw", g)], w=[("Kpad", pr)])

                def trk():
                    nc.tensor.transpose(out=b3bf[:, 0:128], in_=Kpad2[pr][:, 64:192], identity=ident[:, :])
                    return nc.tensor.transpose(out=b3bf[:, 128:256], in_=Kpad2[pr][:, 0:128], identity=ident[:, :])

                P.op("pe", trk, r=[("Kpad", pr), "ident"], w=[bkey(3)])
                P.op("act", lambda: nc.scalar.copy(out=KTp2[pr], in_=b3bf[:, 0:256].rearrange("p (m j) -> p m j", m=2)),
                     r=[bkey(3)], w=[("KTp", pr)])

            def S2(b):
                pr = b % 2

                def mml():
                    nc.tensor.matmul(banks[4][:, 0:8], lhsT=ident[:, :], rhs=biasS[:, g * 8:(g + 1) * 8], start=True, stop=False)
                    nc.tensor.matmul(banks[4][:, 0:4], lhsT=KTp2[pr][:, 0, :], rhs=QTs2[gp][:, :, b], start=False, stop=False)
                    return nc.tensor.matmul(banks[4][:, 4:8], lhsT=KTp2[pr][:, 1, :], rhs=QTs2[gp][:, :, b], start=False, stop=True)

                P.op("pe", mml, r=[("KTp", pr), ("QTs", gp), "biasS", "ident"], w=[bkey(4)])
                P.op("act", lambda: nc.scalar.activation(out=PTs2[pr], in_=banks[4][:, 0:8], func=AF.Exp, scale=0.125),
                     r=[bkey(4)], w=[("PTs", pr)], tiny=True)

            def S3(b):
                pr = b % 2

                def mmpv():
                    nc.tensor.matmul(banks[5][0:4, 0:65], lhsT=PTs2[pr][:, 0:4], rhs=Vw65_2[gp][:, b, 0:65], start=True, stop=True)
                    return nc.tensor.matmul(banks[5][0:4, 65:130], lhsT=PTs2[pr][:, 4:8], rhs=Vw65_2[gp][:, b, 0:65], start=True, stop=True)

                P.op("pe", mmpv, r=[("PTs", pr), ("Vw65", gp)], w=[bkey(5)])
                O2 = banks[5][0:4, 0:130].rearrange("p (m d) -> p m d", m=2)
                P.op("dve", lambda: nc.vector.scalar_tensor_tensor(out=sm[0:4, 4:6], in0=O2[:, :, 64], scalar=2.0,
                                                                   in1=es2s[0:4, g, :], op0=ALU.mult, op1=ALU.add),
                     r=[bkey(5), "es2s"], w=["sden"], tiny=True)
                P.op("dve", lambda: nc.vector.reciprocal(out=sm[0:4, 6:8], in_=sm[0:4, 4:6]), r=["sden"], w=["srr"], tiny=True)
                P.op("dve", lambda: nc.vector.tensor_tensor(out=ob2[pr][0:4, :, :], in0=O2[:, :, 0:64],
                                                            in1=sm[0:4, 6:8].unsqueeze(2).to_broadcast([4, 2, 64]), op=ALU.mult),
                     r=[bkey(5), "srr"], w=[("ob", pr)])
                P.dma("sp", lambda: nc.sync.dma_start(out=osamp[g, b, :].rearrange("(p m d) -> p m d", p=4, m=2),
                                                      in_=ob2[pr][0:4, :, :]), r=[("ob", pr)], w=[("osd", pr)], key="osamp%d" % pr)

            for i in range(SB + 2):
                if i < SB:
                    S1(i)
                if 0 <= i - 1 < SB:
                    S2(i - 1)
                if 0 <= i - 2 < SB:
                    S3(i - 2)
                if i == 8 and g + 1 < 4:
                    front(g + 1)
            P.dma("sp", lambda g=g: nc.sync.dma_start(out=os_[0:16, :], in_=osamp[g, :, :]), r=[("osd", 0), ("osd", 1)], w=["os_"], key="osamp2")
            P.op("dve", lambda: nc.vector.tensor_tensor(out=ogs[0:16, :], in0=os_[0:16, :], in1=us2[gp][0:16, :], op=ALU.mult),
                 r=["os_", ("us", gp)], w=["ogs"])

            def trg():
                for f in range(4):
                    i = nc.tensor.transpose(out=b3bf[:, f * 16:(f + 1) * 16], in_=ogs[0:16, f * 128:(f + 1) * 128],
                                            identity=ident[0:16, 0:16])
                return i

            P.op("pe", trg, r=["ogs", "ident"], w=[bkey(3)])
            P.op("act", lambda: nc.scalar.copy(out=gsT, in_=b3bf[:, 0:64].rearrange("p (f t) -> p f t", f=4)), r=[bkey(3)], w=["gsT"])

            def mmy(g=g):
                for n_ in range(2):
                    for f in range(4):
                        i = nc.tensor.matmul(banks[6 + n_][0:16, :], lhsT=gsT[:, f, :], rhs=wo[:, f, n_ * 512:(n_ + 1) * 512],
                                             start=(g == 0 and f == 0), stop=(g == 3 and f == 3))
                return i

            P.op("pe", mmy, r=["gsT", ("wout", 0)], w=[bkey(6), bkey(7)])
            if g + 1 < len(sorder):
                load_wout(*sorder[g + 1])
        P.op("dve", lambda: nc.vector.tensor_tensor(out=xs_t[0:16, 0:512], in0=xs_t[0:16, 0:512], in1=banks[6][0:16, :], op=ALU.add),
             r=[bkey(6), "xs_t"], w=["xs_t"])
        P.op("dve", lambda: nc.vector.tensor_tensor(out=xs_t[0:16, 512:1024], in0=xs_t[0:16, 512:1024], in1=banks[7][0:16, :], op=ALU.add),
             r=[bkey(7), "xs_t"], w=["xs_t"])

        if stage == "S0":
            P.dma("sp", lambda: nc.sync.dma_start(out=y_sample[:, :], in_=xs_t[0:16, :]), r=["xs_t"], w=[], key="sconst")
            P.barrier()
            return
        import os as _os
        _dbg = _os.environ.get("KDBG", "")

        def cut(name):
            if _dbg == name:
                P.dma("sp", lambda: nc.sync.dma_start(out=y_sample[:, :], in_=xs_t[0:16, :]), r=["xs_t"], w=[], key="sconst")
                P.barrier()
                P.muted = True

        s_norm(1)
        it = 0
        for h in range(4):
            s = sget(4 + h)
            sget(4 + h + 1)
            wi, wo = win[s], wout[0]
            wk = ("win", s)
            G1 = HC["g1"][h]

            def mmqk():
                for idx, c0_ in enumerate((0, 128, 256, 384)):
                    for k in range(8):
                        i = nc.tensor.matmul(banks[0][:, idx * 16:(idx + 1) * 16], lhsT=wi[:, k, c0_:c0_ + 128], rhs=hsT[:, k, :],
                                             start=(k == 0), stop=(k == 7))
                return i

            P.op("pe", mmqk, r=[wk, "hsT"], w=[bkey(0)])
            P.op("dve", lambda: nc.vector.tensor_copy(out=qk4, in_=banks[0][:, 0:64].rearrange("p (i t) -> p i t", i=4)),
                 r=[bkey(0)], w=["qk4"], tiny=True)
            for (src0, dst, cc, sc, nm) in ((0, qf, 0, 1, "qf"), (2, kf, 2, 3, "kf")):
                P.op("dve", lambda src0=src0, sc=sc: nc.vector.tensor_scalar(out=tmpr, in0=qk4[:, src0 + 1, :], scalar1=rots[:, sc:sc + 1],
                                                                             scalar2=None, op0=ALU.mult), r=["qk4", "rots"], w=["tmpr"], tiny=True)
                P.op("dve", lambda src0=src0, dst=dst, cc=cc: nc.vector.scalar_tensor_tensor(
                    out=dst[:, 0, :], in0=qk4[:, src0, :], scalar=rots[:, cc:cc + 1], in1=tmpr, op0=ALU.mult, op1=ALU.subtract),
                    r=["qk4", "rots", "tmpr"], w=[nm + "0"], tiny=True)
                P.op("dve", lambda src0=src0, cc=cc: nc.vector.tensor_scalar(out=tmpr, in0=qk4[:, src0 + 1, :], scalar1=rots[:, cc:cc + 1],
                                                                             scalar2=None, op0=ALU.mult), r=["qk4", "rots", nm + "0"], w=["tmpr"], tiny=True)
                P.op("dve", lambda src0=src0, dst=dst, sc=sc: nc.vector.scalar_tensor_tensor(
                    out=dst[:, 1, :], in0=qk4[:, src0, :], scalar=rots[:, sc:sc + 1], in1=tmpr, op0=ALU.mult, op1=ALU.add),
                    r=["qk4", "rots", "tmpr"], w=[nm + "1"], tiny=True)
            cut("c1")
            P.op("dve", lambda: nc.vector.tensor_copy(out=qTs, in_=qf), r=["qf0", "qf1"], w=["qTs"], tiny=True)
            P.op("dve", lambda: nc.vector.tensor_copy(out=kTs, in_=kf), r=["kf0", "kf1"], w=["kTs"], tiny=True)
            P.op("dve", lambda: nc.vector.tensor_tensor(out=prodf, in0=qf, in1=kf, op=ALU.mult), r=["qf0", "qf1", "kf0", "kf1"], w=["prodf"], tiny=True)

            def mmdot():
                nc.tensor.matmul(banks[0][0:16, 128:130], lhsT=prodf[:, 0, :], rhs=onesf[:, 0:2], start=True, stop=False)
                return nc.tensor.matmul(banks[0][0:16, 128:130], lhsT=prodf[:, 1, :], rhs=onesf[:, 0:2], start=False, stop=True)

            P.op("pe", mmdot, r=["prodf", "onesf"], w=[bkey(0)])
            P.op("dve", lambda: nc.vector.tensor_copy(out=sm[0:16, 8:9], in_=banks[0][0:16, 128:129]), r=[bkey(0)], w=["sdot"], tiny=True)

            cut("c2")

            def mmv():
                for k in range(8):
                    i = nc.tensor.matmul(banks[1][0:16, :], lhsT=hsT[:, k, :], rhs=wi[:, k, 512:1024], start=(k == 0), stop=(k == 7))
                return i

            P.op("pe", mmv, r=[wk, "hsT"], w=[bkey(1)])
            P.op("act", lambda: nc.scalar.copy(out=vtokb[0:16, :], in_=banks[1][0:16, :]), r=[bkey(1)], w=["vtokb"])
            P.op("dve", lambda: nc.vector.tensor_scalar(out=o1[0:16, :], in0=banks[1][0:16, :], scalar1=sm[0:16, 8:9], scalar2=None,
                                                        op0=ALU.mult), r=[bkey(1), "sdot", "vtokb"], w=["o1"])

            def mmg1():
                for k in range(8):
                    i = nc.tensor.matmul(banks[2][0:16, :], lhsT=hsT[:, k, :], rhs=wi[:, k, 1024:1536], start=(k == 0), stop=(k == 7))
                return i

            P.op("pe", mmg1, r=[wk, "hsT"], w=[bkey(2)])
            P.op("act", lambda: nc.scalar.activation(out=t_s[0:16, :], in_=banks[2][0:16, :], func=AF.Tanh, scale=0.5),
                 r=[bkey(2)], w=["t_s"])
            P.op("dve", lambda: nc.vector.scalar_tensor_tensor(out=us[0:16, :], in0=t_s[0:16, :], scalar=1.0, in1=banks[2][0:16, :],
                                                               op0=ALU.add, op1=ALU.mult), r=[bkey(2), "t_s"], w=["us"])

            cut("c3")
            def trk1():
                nc.tensor.transpose(out=b3bf[0:16, 0:128], in_=kTs[:, 0, :], identity=ident[:, :])
                return nc.tensor.transpose(out=b3bf[0:16, 128:256], in_=kTs[:, 1, :], identity=ident[:, :])

            P.op("pe", trk1, r=["kTs", "ident"], w=[bkey(3)])
            P.op("act", lambda: nc.scalar.copy(out=ktok[0:16, :], in_=b3bf[0:16, 0:256]), r=[bkey(3)], w=["ktok"])
            cut("c4")
            P.op("dve", lambda: nc.vector.tensor_tensor(out=Zq, in0=qTs.unsqueeze(2).to_broadcast([128, 2, 16, 16]),
                                                        in1=eyeq.unsqueeze(1).to_broadcast([128, 2, 16, 16]), op=ALU.mult),
                 r=["qTs", "eyeq"], w=["Zq"])
            P.op("dve", lambda: nc.vector.tensor_tensor(out=Zk[0:16, :, :], in0=ktok[0:16, :].unsqueeze(1).to_broadcast([16, 16, 256]),
                                                        in1=ident[0:16, 0:16].unsqueeze(2).to_broadcast([16, 16, 256]), op=ALU.mult),
                 r=["ktok", "ident"], w=["Zk"])
            cut("c5")
            for b in range(SB if "norloop" not in _dbg else 0):
                sl = it % 4
                s2 = it % 2
                ub = 2 + 2 * (it % 2)
                it += 1
                P.dma("sp", lambda b=b, h=h, sl=sl: nc.sync.dma_start(out=Rb[sl], in_=state_ret[b, h].rearrange("(c p) v -> p c v", p=128)),
                      w=[("Rb", sl)], key="rin%d" % sl)
                P.op("act", lambda sl=sl, s2=s2: nc.scalar.copy(out=Rbfs[s2], in_=Rb[sl]), r=[("Rb", sl)], w=[("Rbfs", s2)])

                def mmc(b=b, s2=s2):
                    nc.tensor.matmul(banks[1][0:16, :], lhsT=Zq[:, 0, b, :], rhs=Rbfs[s2][:, 0, :], start=(b == 0), stop=False)
                    return nc.tensor.matmul(banks[1][0:16, :], lhsT=Zq[:, 1, b, :], rhs=Rbfs[s2][:, 1, :], start=False, stop=(b == SB - 1))

                P.op("pe", mmc, r=["Zq", ("Rbfs", s2), "o1", "vtokb"], w=[bkey(1)])

                def mmu(b=b, ub=ub):
                    nc.tensor.matmul(banks[ub][:, :], lhsT=Zk[0:16, b, 0:128], rhs=vtokb[0:16, :], start=True, stop=True)
                    return nc.tensor.matmul(banks[ub + 1][:, :], lhsT=Zk[0:16, b, 128:256], rhs=vtokb[0:16, :], start=True, stop=True)

                P.op("pe", mmu, r=["Zk", "vtokb"], w=[bkey(ub), bkey(ub + 1)])

                def upd(sl=sl, ub=ub, G1=G1):
                    nc.vector.scalar_tensor_tensor(out=Rb[sl][:, 0, :], in0=Rb[sl][:, 0, :], scalar=G1, in1=banks[ub][:, :],
                                                   op0=ALU.mult, op1=ALU.add)
                    return nc.vector.scalar_tensor_tensor(out=Rb[sl][:, 1, :], in0=Rb[sl][:, 1, :], scalar=G1, in1=banks[ub + 1][:, :],
                                                          op0=ALU.mult, op1=ALU.add)

                P.op("dve", upd, r=[bkey(ub), bkey(ub + 1), ("Rb", sl), ("Rbfs", s2)], w=[("Rb", sl)])
                P.dma("pool", lambda b=b, h=h, sl=sl: nc.gpsimd.dma_start(out=ret_s[b, h].rearrange("(c p) v -> p c v", p=128), in_=Rb[sl]),
                      r=[("Rb", sl)], w=[], key="rout_s%d" % sl)
            P.op("dve", lambda G1=G1: nc.vector.scalar_tensor_tensor(out=o1[0:16, :], in0=banks[1][0:16, :], scalar=G1, in1=o1[0:16, :],
                                                                     op0=ALU.mult, op1=ALU.add), r=[bkey(1), "o1"], w=["o1"])
            P.op("dve", lambda: nc.vector.bn_stats(out=sm[0:16, 10:16], in_=o1[0:16, :]), r=["o1"], w=["sst6"], tiny=True)
            P.op("dve", lambda: nc.vector.bn_aggr(out=sm[0:16, 2:4], in_=sm[0:16, 10:16]), r=["sst6"], w=["smv"], tiny=True)
            P.op("pool", lambda: nc.gpsimd.tensor_scalar(out=sm[0:16, 9:10], in0=sm[0:16, 3:4], scalar1=float(EPS), scalar2=None, op0=ALU.add),
                 r=["smv"], w=["srs"], tiny=True)
            P.op("pool", lambda: nc.gpsimd.tensor_tensor(out=sm[0:16, 9:10], in0=sm[0:16, 9:10], in1=nhalf[0:16, 0:1], op=ALU.pow),
                 r=["srs", "nhalf"], w=["srs"], tiny=True)
            P.op("dve", lambda: nc.vector.tensor_scalar(out=o1[0:16, :], in0=o1[0:16, :], scalar1=sm[0:16, 2:3], scalar2=sm[0:16, 9:10],
                                                        op0=ALU.subtract, op1=ALU.mult), r=["o1", "smv", "srs"], w=["o1"])
            P.op("dve", lambda: nc.vector.tensor_tensor(out=gts[0:16, :], in0=o1[0:16, :], in1=us[0:16, :], op=ALU.mult),
                 r=["o1", "us"], w=["gts"])

            def trg1():
                for f in range(4):
                    i = nc.tensor.transpose(out=b3bf[:, f * 16:(f + 1) * 16], in_=gts[0:16, f * 128:(f + 1) * 128],
                                            identity=ident[0:16, 0:16])
                return i

            P.op("pe", trg1, r=["gts", "ident"], w=[bkey(3)])
            P.op("act", lambda: nc.scalar.copy(out=gsT, in_=b3bf[:, 0:64].rearrange("p (f t) -> p f t", f=4)), r=[bkey(3)], w=["gsT"])

            def mmy1(h=h):
                for n_ in range(2):
                    for f in range(4):
                        i = nc.tensor.matmul(banks[6 + n_][0:16, :], lhsT=gsT[:, f, :], rhs=wo[:, f, n_ * 512:(n_ + 1) * 512],
                                             start=(h == 0 and f == 0), stop=(h == 3 and f == 3))
                return i

            P.op("pe", mmy1, r=["gsT", ("wout", 0)], w=[bkey(6), bkey(7)])
            if 4 + h + 1 < len(sorder):
                load_wout(*sorder[4 + h + 1])
        P.op("dve", lambda: nc.vector.scalar_tensor_tensor(out=xs_t[0:16, 0:512], in0=banks[6][0:16, :], scalar=0.5, in1=xs_t[0:16, 0:512],
                                                           op0=ALU.mult, op1=ALU.add), r=[bkey(6), "xs_t"], w=["xs_t"])
        P.op("dve", lambda: nc.vector.scalar_tensor_tensor(out=xs_t[0:16, 512:1024], in0=banks[7][0:16, :], scalar=0.5,
                                                           in1=xs_t[0:16, 512:1024], op0=ALU.mult, op1=ALU.add),
             r=[bkey(7), "xs_t"], w=["xs_t"])
        P.op("pool", lambda: nc.gpsimd.memset(sm[0:16, 0:1], 0.0), w=["sm_ss"], tiny=True)
        P.op("act", lambda: nc.scalar.activation(out=xsb[0:16, :], in_=xs_t[0:16, :], func=AF.Square, accum_out=sm[0:16, 0:1]),
             r=["xs_t", "sm_ss"], w=["sm_ss", "xsb"], tiny=True)
        P.op("pool", lambda: nc.gpsimd.tensor_scalar(out=sm[0:16, 1:2], in0=sm[0:16, 0:1], scalar1=1.0 / D, scalar2=float(EPS),
                                                     op0=ALU.mult, op1=ALU.add), r=["sm_ss"], w=["sm_r"], tiny=True)
        P.op("pool", lambda: nc.gpsimd.tensor_tensor(out=sm[0:16, 1:2], in0=sm[0:16, 1:2], in1=nhalf[0:16, 0:1], op=ALU.pow),
             r=["sm_r", "nhalf"], w=["sm_r"], tiny=True)
        P.op("dve", lambda: nc.vector.scalar_tensor_tensor(out=xs_t[0:16, :], in0=xs_t[0:16, :], scalar=sm[0:16, 1:2], in1=gfs[0:16, :],
                                                           op0=ALU.mult, op1=ALU.mult), r=["xs_t", "sm_r", "gfs"], w=["xs_t"])
        P.dma("sp", lambda: nc.sync.dma_start(out=y_sample[:, :], in_=xs_t[0:16, :]), r=["xs_t"], w=[], key="sconst")
        P.barrier()

    order = []
    for seq in range(NSEQ if not stage.startswith("S") else 0):
        order += [(0, g) for g in range(4)] + ([(1, h) for h in range(4)] if stage != "L0" else [])
    loaded = {}

    def ensure_win(idx):
        if idx < len(order) and idx not in loaded:
            loaded[idx] = load_win(*order[idx])

    def wout_cb(idx):
        def cb():
            if idx < len(order):
                load_wout(*order[idx])
        return cb

    if stage != "L0":
        sample_phase()
        P.muted = False
    widx = 0
    if order:
        ensure_win(0)
        load_wout(*order[0])
    for seq in range(NSEQ if not stage.startswith("S") else 0):
        if seq == 0 or stage == "L0":
            for b in range(NB):
                load_x(seq, b)
        phase_a(0)
        if seq > 0:
            P.barrier()
        for pc in range(2):
            for hq in range(4):
                P.dma("sp", lambda pc=pc, hq=hq: nc.sync.dma_start(
                    out=biasT[:, pc, hq * 8:(hq + 1) * 8, :],
                    in_=bass.AP(tensor=biasA, offset=(pc * 32 + hq * 8) * ALEN + 127, ap=[[255, 128], [ALEN, 8], [1, 128]])),
                    r=["biasA"], w=["biasT"], key="biasT")
        P.op("pool", lambda: nc.gpsimd.memset(V65[:, :, 64:66], 1.0), w=[("V", 0), ("V", 1)])
        for g in range(4):
            ensure_win(widx + g)
            ensure_win(widx + g + 1)
            l0_group(seq, g, loaded[widx + g], wout_cb(widx + g + 1))
        widx += 4
        if stage != "L0":
            phase_a(1)
        P.barrier()
        if stage == "L0":
            for b in range(NB):
                P.dma("sp", lambda b=b, seq=seq: nc.sync.dma_start(out=dbg_x1[seq, b * 128:(b + 1) * 128, :], in_=xres[:, b, :]),
                      r=[("x", b)], w=[], key="x%d" % b)
            for b in range(NB):
                P.dma("sp", lambda b=b, seq=seq: nc.sync.dma_start(out=y_prompt[seq, b * 128:(b + 1) * 128, :], in_=xres[:, b, :]),
                      r=[("x", b)], w=[], key="x%d" % b)
            for h in range(4):
                P.dma("sp", lambda seq=seq, h=h: nc.sync.dma_start(out=ret_p[seq, h].rearrange("(c p) v -> p c v", p=128),
                                                              in_=xres[:, 0, :].rearrange("p (c v) -> p c v", c=2)),
                      r=[("x", 0)], w=[], key="rout")
            P.barrier()
            continue
        PP[:] = [0, 1, 2, 3]
        for h in range(4):
            ensure_win(widx + h)
            ensure_win(widx + h + 1)
            l1_head(seq, h, loaded[widx + h], wout_cb(widx + h + 1))
        widx += 4
        PP[:] = [0, 1]
        P.dma("sp", lambda: nc.sync.dma_start(out=gfin, in_=final_g[0:1, :].to_broadcast([128, D])), r=[], w=["gfin", "R", "Rbf"], key="const")
        final_norm(seq)

    P.emit()
    es.close()
    return nc, P


def _arr_win_attn(W):
    out = np.zeros((4, 128, 8, 1344), np.float32)
    for g in range(4):
        k = W[:, 2048 + g * 64:2048 + (g + 1) * 64]
        cols = np.concatenate([W[:, g * 512:(g + 1) * 512], k, np.zeros((D, 128), np.float32), k,
                               W[:, 2304 + g * 64:2304 + (g + 1) * 64], W[:, 2560 + g * 512:2560 + (g + 1) * 512]], axis=1)
        out[g] = cols.reshape(8, 128, 1344).transpose(1, 0, 2)
    return np.ascontiguousarray(out.reshape(4, 128, 8 * 1344))


def _arr_win_ret(W):
    out = np.zeros((4, 128, 8, 1536), np.float32)
    for h in range(4):
        cols = np.concatenate([W[:, h * 256:(h + 1) * 256], W[:, 1024 + h * 256:1024 + (h + 1) * 256],
                               W[:, 2048 + h * 512:2048 + (h + 1) * 512], W[:, 4096 + h * 512:4096 + (h + 1) * 512]], axis=1)
        out[h] = cols.reshape(8, 128, 1536).transpose(1, 0, 2)
    return np.ascontiguousarray(out.reshape(4, 128, 8 * 1536))


def _arr_wout(W):
    return np.ascontiguousarray(W.reshape(4, 4, 128, D).transpose(0, 2, 1, 3).reshape(4, 128, 4096))


_CACHE = {}


def _get_prog(stage):
    if stage not in _CACHE:
        _CACHE[stage] = build(stage)
    return _CACHE[stage]


def kernel(x_prompt, x_sample, cache_swa_k, cache_swa_v, state_ret, norm_g, final_norm_g, rel_bias,
           w_in_attn, attn_sinks, w_out_attn, w_in_ret, w_out_ret, _stage="full"):
    nc, P = _get_prog(_stage)
    f = lambda a: np.ascontiguousarray(np.asarray(a, dtype=np.float32))
    shared = {
        "w_in_attn_r": _arr_win_attn(f(w_in_attn)[0]), "w_out_attn_r": _arr_wout(f(w_out_attn)[0]),
        "w_in_ret_r": _arr_win_ret(f(w_in_ret)[0]), "w_out_ret_r": _arr_wout(f(w_out_ret)[0]),
        "norm_gT": np.ascontiguousarray(f(norm_g).reshape(2, 8, 128).transpose(0, 2, 1)),
        "final_norm_g": f(final_norm_g).reshape(1, D),
        "rel_bias": f(rel_bias), "attn_sinks": f(attn_sinks),
        "c_ident": HC["ident"], "c_etab": HC["etab"], "c_maskT": HC["maskT"], "c_rot": HC["rot"],
        "c_etab_s": HC["etab_s"], "c_eyeq": HC["eyeq"], "c_rot_s": HC["rot_s"],
    }
    xsm = f(x_sample).reshape(128, D)
    ck = f(cache_swa_k).reshape(128, 128, 256)
    cv = f(cache_swa_v).reshape(128, 128, 256)
    st = f(state_ret).reshape(128, 4, 256, 512)
    xp = f(x_prompt)
    in_maps = []
    for c in range(NCORES):
        m = dict(shared)
        m["x_prompt"] = xp[c * NSEQ:(c + 1) * NSEQ]
        if _stage != "L0":
            m["x_sample"] = xsm[c * SB:(c + 1) * SB]
            m["cache_k"] = ck[c * SB:(c + 1) * SB]
            m["cache_v"] = cv[c * SB:(c + 1) * SB]
            m["state_ret"] = st[c * SB:(c + 1) * SB]
        else:
            for k_ in ("c_etab_s", "c_eyeq", "c_rot_s"):
                m.pop(k_, None)
        in_maps.append(m)
    res = run_bass_kernel_spmd(nc, in_maps, core_ids=list(range(NCORES)))
    R = res.results
    if _stage.startswith("S"):
        return np.concatenate([r["y_sample"] for r in R], axis=0), np.concatenate([r["swa_k_sample"] for r in R], axis=0), np.concatenate([r["swa_v_sample"] for r in R], axis=0), np.concatenate([r["ret_state_sample"] for r in R], axis=0)
    y_prompt = np.concatenate([r["y_prompt"] for r in R], axis=0)
    swa_k = np.concatenate([r["swa_k_prompt"] for r in R], axis=0)[None]
    swa_v = np.concatenate([r["swa_v_prompt"] for r in R], axis=0)[None]
    ret_p = np.concatenate([r["ret_state_prompt"] for r in R], axis=0)[None]
    if _stage == "L0":
        return [y_prompt, None, swa_k, swa_v, ret_p, None, None, None], np.concatenate([r["dbg_x1"] for r in R], axis=0)
    y_sample = np.concatenate([r["y_sample"] for r in R], axis=0).reshape(128, 1, D)
    swa_k_s = np.concatenate([r["swa_k_sample"] for r in R], axis=0).reshape(1, 128, 128, 4, 64)
    swa_v_s = np.concatenate([r["swa_v_sample"] for r in R], axis=0).reshape(1, 128, 128, 4, 64)
    ret_s = np.concatenate([r["ret_state_sample"] for r in R], axis=0)[None]
    return (y_prompt, y_sample, swa_k, swa_v, ret_p, swa_k_s, swa_v_s, ret_s)
```

```python
import math
import types
from contextlib import ExitStack

import numpy as np
import ml_dtypes

import concourse.bass as bass
import concourse.mybir as mybir
from concourse.bass_utils import run_bass_kernel_spmd

F32 = mybir.dt.float32
BF16 = mybir.dt.bfloat16
AF = mybir.ActivationFunctionType
ALU = mybir.AluOpType

NCORES = 8
D = 1024
SEQ = 2048
NB = SEQ // 128
NSEQ = 2
SB = 16
EPS = 1e-6
PAST = 16384
NEG = -1.0e30
ROPE_BASE = 10000.0
ALEN = 130 * 256


def _freeze(fn):
    if fn.__closure__ is None:
        return fn
    cells = []
    for c in fn.__closure__:
        try:
            cells.append(types.CellType(c.cell_contents))
        except ValueError:
            cells.append(c)
    return types.FunctionType(fn.__code__, fn.__globals__, fn.__name__, fn.__defaults__, tuple(cells))


class Prog:
    def __init__(self, nc, es):
        self.nc = nc
        self.es = es
        self.ops = []
        self.muted = False
        self.engs = {"pe": nc.tensor, "dve": nc.vector, "act": nc.scalar, "pool": nc.gpsimd, "sp": nc.sync}

    def op(self, eng, fn, r=(), w=(), tiny=False):
        if self.muted:
            return
        self.ops.append(("c", eng, _freeze(fn), tuple(r), tuple(w), "tiny" if tiny else None))

    def dma(self, q, fn, r=(), w=(), key=None):
        assert key is not None
        if self.muted:
            return
        self.ops.append(("d", q, _freeze(fn), tuple(r), tuple(w), key))

    def barrier(self):
        if self.muted:
            return
        self.ops.append(("b", None, None, (), (), None))

    def emit(self):
        ops = self.ops
        n = len(ops)
        tl = [None] * n
        for i, o in enumerate(ops):
            if o[0] == "c":
                tl[i] = o[1]
            elif o[0] == "d":
                tl[i] = "dma:" + o[5]
        last_w = {}
        readers = {}
        last_on_tl = {}
        bar_deps = {}
        need = [None] * n
        signaling = [False] * n
        for i, o in enumerate(ops):
            kind, eng, fn, rs, ws, key = o
            if kind == "b":
                bar_deps = dict(last_on_tl)
                continue
            deps = {}

            def add(j):
                t = tl[j]
                if j > deps.get(t, -1):
                    deps[t] = j

            for b in rs:
                if b in last_w:
                    add(last_w[b])
            for b in ws:
                if b in last_w:
                    add(last_w[b])
                for j in readers.get(b, {}).values():
                    add(j)
            for j in bar_deps.values():
                add(j)
            nd = {}
            for t, j in deps.items():
                if kind == "c" and t == eng and ops[j][5] != "tiny":
                    continue
                nd[t] = j
                if ops[j][0] == "c":
                    signaling[j] = True
            need[i] = nd
            for b in rs:
                readers.setdefault(b, {})[tl[i]] = i
            for b in ws:
                last_w[b] = i
                readers[b] = {}
            last_on_tl[tl[i]] = i
        val = [0] * n
        cnt = {}
        for i, o in enumerate(ops):
            if o[0] == "c":
                if signaling[i]:
                    cnt[o[1]] = cnt.get(o[1], 0) + 1
                    val[i] = cnt[o[1]]
            elif o[0] == "d":
                cnt[tl[i]] = cnt.get(tl[i], 0) + 16
                val[i] = cnt[tl[i]]
        sems = {}
        for t in cnt:
            sems[t] = self.es.enter_context(self.nc.semaphore("s_" + t.replace(":", "_")))
        self.nsem = len(sems)
        waited = {e: {} for e in self.engs}
        for i, o in enumerate(ops):
            kind, eng, fn, rs, ws, key = o
            if kind == "b":
                continue
            e = self.engs[eng]
            for t, j in need[i].items():
                v = val[j]
                if waited[eng].get(t, 0) >= v:
                    continue
                e.wait_ge(sems[t], v)
                waited[eng][t] = v
            inst = fn()
            if kind == "d":
                inst.then_inc(sems[tl[i]], 16)
            elif signaling[i]:
                inst.then_inc(sems[eng], 1)
        sp = self.nc.sync
        for t, c in cnt.items():
            if waited["sp"].get(t, 0) < c:
                sp.wait_ge(sems[t], c)
        self.counts = cnt
        self.vals = val
        self.sig = signaling
        self.need = need


def _t5_bucket_np(rel):
    n = np.maximum(rel, 0)
    nf = np.maximum(n, 1).astype(np.float32)
    large = 16 + (np.log(nf / np.float32(16)) / np.float32(math.log(128 / 16)) * np.float32(16)).astype(np.int32)
    large = np.minimum(large, 31)
    return np.where(n < 16, n, large)


def _host_consts():
    c = {}
    c["ident"] = np.eye(128, dtype=np.float32).astype(ml_dtypes.bfloat16)
    E = np.zeros((33, 2, 256), np.float32)
    m = np.arange(256)
    rel_cur = m - 127
    ok_cur = (rel_cur >= 0) & (rel_cur <= 127)
    bc = _t5_bucket_np(rel_cur)
    rel_prev = m + 1
    ok_prev = m <= 126
    bp = _t5_bucket_np(rel_prev)
    for mm in range(256):
        if ok_prev[mm]:
            E[bp[mm], 0, mm] = 1.0
        else:
            E[32, 0, mm] = 1.0
        if ok_cur[mm]:
            E[bc[mm], 1, mm] = 1.0
        else:
            E[32, 1, mm] = 1.0
    c["etab"] = E.astype(ml_dtypes.bfloat16)
    Es = np.zeros((32, 128), np.float32)
    bs = _t5_bucket_np(127 - np.arange(128))
    Es[bs, np.arange(128)] = 1.0
    c["etab_s"] = Es.astype(ml_dtypes.bfloat16)
    jj = np.arange(128)[:, None]
    ii = np.arange(128)[None, :]
    c["maskT"] = (ii >= jj).astype(np.float32)
    half = 128
    inv = (np.float32(ROPE_BASE) ** (-np.arange(half, dtype=np.float32) / np.float32(half))).astype(np.float32)
    pos = np.arange(SEQ, dtype=np.float32)
    ang = pos[None, :] * inv[:, None]
    cos = np.cos(ang).astype(np.float32)
    sin = np.sin(ang).astype(np.float32)
    lg = np.log1p(-np.exp2(-5.0 - np.arange(4, dtype=np.float64)))
    tin = (np.arange(SEQ) % 128).astype(np.float64)
    rot = np.zeros((4, 4, 4, 128, 512), np.float32)
    for h in range(4):
        fq = np.exp((tin + 1.0) * lg[h])
        fk = np.exp(-(tin + 1.0) * lg[h]) / 16.0
        tabs = [cos * fq[None, :], sin * fq[None, :], cos * fk[None, :], sin * fk[None, :]]
        for t in range(4):
            rot[h, :, t] = tabs[t].astype(np.float32).reshape(128, 4, 512).transpose(1, 0, 2)
    c["rot"] = rot
    c["g128"] = [float(np.exp(128.0 * lg[h])) for h in range(4)]
    c["g1"] = [float(np.exp(lg[h])) for h in range(4)]
    angs = np.float32(PAST) * inv
    rs = np.stack([np.cos(angs), np.sin(angs)], axis=1).astype(np.float32)
    c["rot_s"] = np.concatenate([rs, rs / np.float32(16.0)], axis=1).astype(np.float32)
    c["eyeq"] = np.tile(np.eye(16, dtype=np.float32).reshape(1, 256), (128, 1)).astype(ml_dtypes.bfloat16)
    return c


HC = _host_consts()


def build(stage="full"):
    nc = bass.Bass("TRN2", target_bir_lowering=False)
    es = ExitStack()
    P = Prog(nc, es)

    def din(name, shape, dt=F32):
        return nc.dram_tensor(name, list(shape), dt, kind="ExternalInput").ap()

    def dout(name, shape, dt=F32):
        return nc.dram_tensor(name, list(shape), dt, kind="ExternalOutput").ap()

    x_prompt = din("x_prompt", [NSEQ, SEQ, D])
    w_in_attn = din("w_in_attn_r", [4, 128, 8 * 1344])
    w_out_attn = din("w_out_attn_r", [4, 128, 4096])
    w_in_ret = din("w_in_ret_r", [4, 128, 8 * 1536])
    w_out_ret = din("w_out_ret_r", [4, 128, 4096])
    norm_gT = din("norm_gT", [2, 128, 8])
    final_g = din("final_norm_g", [1, D])
    rel_bias = din("rel_bias", [32, 32])
    sinks = din("attn_sinks", [1, 32])
    c_ident = din("c_ident", [128, 128], BF16)
    c_etab = din("c_etab", [33, 2, 256], BF16)
    c_maskT = din("c_maskT", [128, 128])
    c_rot = din("c_rot", [4, 4, 4, 128, 512])

    if stage == "L0":
        din = lambda name, shape, dt=F32: None
        dout_real = dout
        dout = lambda name, shape, dt=F32: None
    x_sample = din("x_sample", [SB, D])
    cache_k = din("cache_k", [SB, 128, 256])
    cache_v = din("cache_v", [SB, 128, 256])
    state_ret = din("state_ret", [SB, 4, 256, 512])
    c_etab_s = din("c_etab_s", [32, 128], BF16)
    c_eyeq = din("c_eyeq", [128, 256], BF16)
    c_rot_s = din("c_rot_s", [128, 4])
    y_sample = dout("y_sample", [SB, D])
    swa_k_s = dout("swa_k_sample", [SB, 128, 256])
    swa_v_s = dout("swa_v_sample", [SB, 128, 256])
    ret_s = dout("ret_state_sample", [SB, 4, 256, 512])
    osamp = nc.dram_tensor("osamp", [4, SB, 512], F32, kind="Internal").ap()
    if stage == "L0":
        dout = dout_real
    y_prompt = dout("y_prompt", [NSEQ, SEQ, D])
    swa_k_p = dout("swa_k_prompt", [NSEQ, 128, 4, 64])
    swa_v_p = dout("swa_v_prompt", [NSEQ, 128, 4, 64])
    ret_p = dout("ret_state_prompt", [NSEQ, 4, 256, 512])
    dbg_x1 = dout("dbg_x1", [NSEQ, SEQ, D]) if stage == "L0" else None

    biasA = nc.dram_tensor("biasA", [2, 32, ALEN], BF16, kind="Internal")

    def sb(name, shape, dt=F32):
        return es.enter_context(nc.sbuf_tensor(name, list(shape), dt))

    def ps(name, shape=(128, 512), dt=F32):
        return es.enter_context(nc.psum_tensor(name, list(shape), dt))

    xres = sb("xres", [128, NB, D])
    hT = sb("hT", [128, 8, SEQ], BF16)
    win = [sb("win%d" % i, [128, 8, 1536], BF16) for i in range(2)]
    wout = [sb("wout0", [128, 4, D], BF16)]
    win_l0 = [w_[:, :, :].rearrange("p k c -> p (k c)")[:, 0:8 * 1344].rearrange("p (k c) -> p k c", k=8) for w_ in win]
    lscr = sb("lscr", [128, 4096])
    work = sb("work", [128, 37 * 256])
    ident = sb("ident", [128, 128], BF16)
    maskT = sb("maskT", [128, 128])
    gT = sb("gT", [128, 2, 8])
    es2 = sb("es2", [128, 32])
    ss = sb("ss", [128, NB])
    rstd = sb("rstd", [128, NB])
    small = sb("small", [128, 64])
    nhalf = sb("nhalf", [128, 2])

    banks = [ps("bank%d" % i) for i in range(8)]

    biasT = lscr[:, :].bitcast(BF16).rearrange("p (c h i) -> p c h i", c=2, h=32)
    rott = [lscr[:, s * 2048:(s + 1) * 2048].rearrange("p (t c) -> p t c", t=4) for s in range(2)]

    class Carver:
        def __init__(self):
            self.off = 0

        def f32(self, n):
            a = work[:, self.off:self.off + n]
            self.off += n
            assert self.off <= 37 * 256, self.off
            return a

        def bf(self, n):
            assert n % 2 == 0
            a = work[:, self.off:self.off + n // 2].bitcast(BF16)
            self.off += n // 2
            assert self.off <= 37 * 256, self.off
            return a

    cA = Carver()
    xs_bf = [cA.bf(1024), cA.bf(1024)]
    offA = cA.off
    c0 = Carver()
    c0.off = offA
    QT = [c0.bf(2048).rearrange("p (q t) -> p q t", q=4) for _ in range(2)]
    KTA = c0.bf(1024)
    KTB = c0.bf(1024)
    V65 = c0.bf(8 * 66).rearrange("p (b d) -> p b d", b=8)
    PT = [[[c0.bf(512).rearrange("p (q t) -> p q t", q=4) for _ in range(2)] for _ in range(2)] for _ in range(2)]
    t0 = c0.f32(512)
    u0 = [c0.f32(512), c0.f32(512)]
    o2 = c0.f32(512)
    og = c0.bf(512)
    ogT = c0.bf(512).rearrange("p (f t) -> p f t", f=4)
    kvout = c0.f32(128)
    c1 = Carver()
    c1.off = offA
    qT = [c1.bf(1024).rearrange("p (c t) -> p c t", c=2) for _ in range(2)]
    kT = [c1.bf(1024).rearrange("p (c t) -> p c t", c=2) for _ in range(2)]
    ra = c1.f32(512)
    rb = c1.f32(512)
    vb = [c1.bf(512), c1.bf(512)]
    t1 = c1.f32(512)
    u1 = [c1.f32(512), c1.f32(512)]
    sT = [c1.bf(128), c1.bf(128)]
    sraw = c1.f32(128)
    kt = [c1.bf(256), c1.bf(256)]
    Rflat = c1.f32(1024)
    Rst = Rflat.rearrange("p (c v) -> p c v", c=2)
    Rbf = c1.bf(1024).rearrange("p (c v) -> p c v", c=2)
    on = c1.f32(512)
    gtd = c1.bf(512)
    gTt = c1.bf(512).rearrange("p (f t) -> p f t", f=4)
    gfin = Rflat

    PP = [0, 1]
    BL = [[2, 3], [4, 5]]
    BO = 6
    BX = 7
    bkey = lambda i: ("ps", i)
    ppc = [0]

    def next_pp():
        b = PP[ppc[0] % len(PP)]
        ppc[0] += 1
        return b

    bx_bf = banks[BX][:, 0:256].bitcast(BF16)
    pT8 = banks[BX][:, :].bitcast(BF16).rearrange("p (k t) -> p k t", k=8)

    wslot = [0]

    def load_win(layer, g):
        s = wslot[0] % 2
        wslot[0] += 1
        flat = win[s][:, :, :].rearrange("p k c -> p (k c)")
        if layer == 0:
            n, src = 8 * 1344, w_in_attn
        else:
            n, src = 8 * 1536, w_in_ret
        P.dma("pool", lambda: nc.gpsimd.dma_start(out=flat[:, 0:n].rearrange("p (a b) -> p a b", a=6),
                                                  in_=src[g].rearrange("p (a b) -> p a b", a=6)),
              w=[("win", s)], key="win%d" % s)
        return s

    def load_wout(layer, g):
        wsrc = w_out_attn if layer == 0 else w_out_ret
        P.dma("pool", lambda: nc.gpsimd.dma_start(out=wout[0][:, :, :].rearrange("p f c -> p (f c)").rearrange("p (a b) -> p a b", a=2),
                                                  in_=wsrc[g].rearrange("p (a b) -> p a b", a=2)),
              w=[("wout", 0)], key="wout0")

    early_slot = None
    if stage != "L0":
        early_slot = load_win(0, 0)
        load_wout(0, 0)
    P.op("pool", lambda: nc.gpsimd.memset(nhalf[:, :], -0.5), w=["nhalf"])
    P.dma("sp", lambda: nc.sync.dma_start(out=ident[:, :], in_=c_ident[:, :]), w=["ident"], key="const")
    P.dma("sp", lambda: nc.sync.dma_start(out=maskT[:, :], in_=c_maskT[:, :]), w=["maskT"], key="const")
    P.dma("sp", lambda: nc.sync.dma_start(out=gT[:, :, :], in_=norm_gT.rearrange("l p k -> p l k")), w=["gT"], key="const")
    P.dma("sp", lambda: nc.sync.dma_start(out=es2[:, :], in_=sinks[0:1, :].to_broadcast([128, 32])), w=["es2"], key="const")
    rb33 = work[0:33, 0:32]
    rb33b = work[0:33, 32:48].bitcast(BF16)
    etab = work[0:33, 64:320].bitcast(BF16).rearrange("p (c m) -> p c m", c=2)
    tsb = work[0:32, 320:576].bitcast(BF16).rearrange("p (c m) -> p c m", c=2)
    P.op("dve", lambda: nc.vector.memset(work[0:64, 0:32], NEG), w=["rb33"], tiny=True)
    P.dma("sp", lambda: nc.sync.dma_start(out=work[0:32, 0:32], in_=rel_bias[:, :]), r=[], w=["rb33"], key="const")
    P.dma("sp", lambda: nc.sync.dma_start(out=etab, in_=c_etab[:, :, :]), w=["etab"], key="const")
    P.barrier()
    P.op("act", lambda: nc.scalar.activation(out=es2[:, :], in_=es2[:, :], func=AF.Exp), r=["es2"], w=["es2"], tiny=True)
    P.op("dve", lambda: nc.vector.tensor_scalar(out=es2[:, :], in0=es2[:, :], scalar1=2.0, scalar2=None, op0=ALU.mult),
         r=["es2"], w=["es2"], tiny=True)
    P.op("dve", lambda: nc.vector.tensor_scalar(out=rb33b, in0=rb33, scalar1=8.0, scalar2=None, op0=ALU.mult),
         r=["rb33"], w=["rb33b"], tiny=True)

    def mk_tp():
        nc.tensor.matmul(banks[0][0:32, 0:256], lhsT=rb33b, rhs=etab[:, 0, :], start=True, stop=True)
        return nc.tensor.matmul(banks[0][0:32, 256:512], lhsT=rb33b, rhs=etab[:, 1, :], start=True, stop=True)

    P.op("pe", mk_tp, r=["rb33b", "etab"], w=[bkey(0)])
    P.op("dve", lambda: nc.vector.tensor_copy(out=tsb, in_=banks[0][0:32, :].rearrange("p (c m) -> p c m", c=2)),
         r=[bkey(0)], w=["tsb"])
    def biasA_dmas(q="sp"):
        eng = nc.sync if q == "sp" else nc.scalar
        for pc in range(2):
            P.dma(q, lambda pc=pc: eng.dma_start(
                out=biasA.ap()[pc].rearrange("h (r m) -> h r m", m=256),
                in_=tsb[:, pc, :].unsqueeze(1).to_broadcast([32, 130, 256])), r=["tsb"], w=["biasA"], key="biasA")

    if stage == "L0":
        biasA_dmas()
    P.barrier()

    def phase_a(layer):
        P.op("pool", lambda: nc.gpsimd.memset(ss[:, :], 0.0), w=["ss"] + [("ss", b) for b in range(NB)])
        for b in range(NB):
            s = b % 2
            P.op("act", lambda b=b, s=s: nc.scalar.activation(out=xs_bf[s], in_=xres[:, b, :], func=AF.Square,
                                                               accum_out=ss[:, b:b + 1]),
                 r=[("x", b), "ss"], w=[("xs", s), ("ss", b)])
            P.op("pool", lambda b=b: nc.gpsimd.tensor_scalar(out=rstd[:, b:b + 1], in0=ss[:, b:b + 1], scalar1=1.0 / D,
                                                              scalar2=float(EPS), op0=ALU.mult, op1=ALU.add),
                 r=[("ss", b)], w=[("rstd", b)], tiny=True)
            P.op("pool", lambda b=b: nc.gpsimd.tensor_tensor(out=rstd[:, b:b + 1], in0=rstd[:, b:b + 1], in1=nhalf[:, 0:1],
                                                              op=ALU.pow),
                 r=[("rstd", b), "nhalf"], w=[("rstd", b)], tiny=True)
            P.op("dve", lambda b=b, s=s: nc.vector.tensor_scalar(out=xs_bf[s], in0=xres[:, b, :],
                                                                  scalar1=rstd[:, b:b + 1], scalar2=None,
                                                                  op0=ALU.mult),
                 r=[("x", b), ("rstd", b)], w=[("xs", s)])

            def tr(b=b, s=s):
                for k in range(8):
                    i = nc.tensor.transpose(out=pT8[:, k, :], in_=xs_bf[s][:, k * 128:(k + 1) * 128], identity=ident[:, :])
                return i

            P.op("pe", tr, r=[("xs", s), "ident"], w=[bkey(BX)])
            P.op("dve", lambda b=b, layer=layer: nc.vector.tensor_tensor(
                out=hT[:, :, b * 128:(b + 1) * 128], in0=pT8,
                in1=gT[:, layer, :].unsqueeze(2).to_broadcast([128, 8, 128]), op=ALU.mult),
                r=[bkey(BX), "gT"], w=[("hT", b)])

    def l0_group(seq, g, s, last_group_cb):
        wi, wo = win_l0[s], wout[0]
        wk = ("win", s)
        O7 = banks[BO][:, 0:455].rearrange("p (h d) -> p h d", h=7)
        den = small[:, 0:8]
        rr = small[:, 8:16]
        o2v = o2.rearrange("p (h d) -> p h d", h=8)

        def proj(sbi):
            ring = sbi % 2
            QTc = QT[sbi % 2]
            toks = slice(sbi * 512, (sbi + 1) * 512)
            hkeys = [("hT", sbi * 4 + i) for i in range(4)]
            for p in range(4):
                bk = next_pp()

                def mmq(p=p, bk=bk):
                    for k in range(8):
                        i = nc.tensor.matmul(banks[bk][:, :], lhsT=wi[:, k, p * 128:(p + 1) * 128], rhs=hT[:, k, toks],
                                             start=(k == 0), stop=(k == 7))
                    return i

                P.op("pe", mmq, r=[wk] + hkeys, w=[bkey(bk)])
                P.op("dve", lambda p=p, bk=bk: nc.vector.tensor_copy(out=QTc[:, p, :], in_=banks[bk][:, :]),
                     r=[bkey(bk)], w=[("QT", sbi % 2)])
            for which, (c0_, dst) in enumerate(((512, KTA), (640, KTB))):
                bk = next_pp()

                def mmk(c0_=c0_, bk=bk):
                    for k in range(8):
                        i = nc.tensor.matmul(banks[bk][:, :], lhsT=wi[:, k, c0_:c0_ + 128], rhs=hT[:, k, toks],
                                             start=(k == 0), stop=(k == 7))
                    return i

                P.op("pe", mmk, r=[wk] + hkeys, w=[bkey(bk)])
                P.op("act", lambda dst=dst, bk=bk: nc.scalar.copy(out=dst[:, ring * 512:(ring + 1) * 512], in_=banks[bk][:, :]),
                     r=[bkey(bk)], w=[("KT", which, ring)])
            bk = next_pp()

            def mmv(bk=bk):
                for blk in range(4):
                    b = sbi * 4 + blk
                    for k in range(8):
                        i = nc.tensor.matmul(banks[bk][:, blk * 64:(blk + 1) * 64], lhsT=hT[:, k, b * 128:(b + 1) * 128],
                                             rhs=wi[:, k, 768:832], start=(k == 0), stop=(k == 7))
                return i

            P.op("pe", mmv, r=[wk] + hkeys, w=[bkey(bk)])
            P.op("dve", lambda bk=bk: nc.vector.tensor_copy(
                out=V65[:, ring * 4:(ring + 1) * 4, 0:64], in_=banks[bk][:, 0:256].rearrange("p (b d) -> p b d", b=4)),
                r=[bkey(bk)], w=[("V", ring)])
            if sbi == 3:
                P.op("dve", lambda bk=bk: nc.vector.tensor_copy(out=kvout[:, 64:128], in_=banks[bk][:, 192:256]),
                     r=[bkey(bk)], w=["vout"])
                P.dma("sp", lambda: nc.sync.dma_start(out=swa_v_p[seq, :, g, :], in_=kvout[:, 64:128]), r=["vout"], w=[], key="vout")
                bk2 = next_pp()

                def mmko(bk2=bk2):
                    for k in range(8):
                        i = nc.tensor.matmul(banks[bk2][:, 0:64], lhsT=hT[:, k, 15 * 128:16 * 128], rhs=wi[:, k, 512:576],
                                             start=(k == 0), stop=(k == 7))
                    return i

                P.op("pe", mmko, r=[wk, ("hT", 15)], w=[bkey(bk2)])
                P.op("dve", lambda bk2=bk2: nc.vector.tensor_copy(out=kvout[:, 0:64], in_=banks[bk2][:, 0:64]),
                     r=[bkey(bk2)], w=["kout"])
                P.dma("sp", lambda: nc.sync.dma_start(out=swa_k_p[seq, :, g, :], in_=kvout[:, 0:64]), r=["kout"], w=[], key="kout")

        def stA(b):
            sbi, blk = b // 4, b % 4
            par = b % 2
            QTc = QT[sbi % 2]
            rb_cur = (sbi % 2) * 4 + blk
            rb_prev = (rb_cur - 1) % 8
            pcs = [1] if b == 0 else [0, 1]
            bg = next_pp()

            def mmg():
                for k in range(8):
                    i = nc.tensor.matmul(banks[bg][:, :], lhsT=hT[:, k, b * 128:(b + 1) * 128], rhs=wi[:, k, 832:1344],
                                         start=(k == 0), stop=(k == 7))
                return i

            P.op("pe", mmg, r=[wk, ("hT", b)], w=[bkey(bg)])
            P.op("act", lambda: nc.scalar.activation(out=t0, in_=banks[bg][:, :], func=AF.Tanh, scale=0.5), r=[bkey(bg)], w=["t0"])
            P.op("dve", lambda: nc.vector.scalar_tensor_tensor(out=u0[par], in0=t0, scalar=1.0, in1=banks[bg][:, :],
                                                               op0=ALU.add, op1=ALU.mult), r=[bkey(bg), "t0"], w=[("u0", par)])
            for m in range(2):
                KT = KTA if m == 0 else KTB
                for pc in pcs:
                    bl = BL[m][pc]
                    rbk = rb_prev if pc == 0 else rb_cur

                    def mml(bl=bl, KT=KT, rbk=rbk, pc=pc, m=m):
                        nc.tensor.matmul(banks[bl][:, :], lhsT=KT[:, rbk * 128:(rbk + 1) * 128],
                                         rhs=QTc[:, :, blk * 128:(blk + 1) * 128], start=True, stop=False)
                        return nc.tensor.matmul(banks[bl][:, :], lhsT=ident[:, :],
                                                rhs=biasT[:, pc, g * 8 + m:g * 8 + 8:2, :], start=False, stop=True)

                    P.op("pe", mml, r=[("QT", sbi % 2), ("KT", m, rbk // 4), "ident", "biasT"], w=[bkey(bl)])
                    P.op("act", lambda bl=bl, m=m, pc=pc: nc.scalar.activation(
                        out=PT[par][m][pc], in_=banks[bl][:, :].rearrange("p (q t) -> p q t", q=4),
                        func=AF.Exp, scale=0.125), r=[bkey(bl)], w=[("PT", par, m, pc)])

        def stB(b):
            sbi, blk = b // 4, b % 4
            par = b % 2
            rb_cur = (sbi % 2) * 4 + blk
            rb_prev = (rb_cur - 1) % 8
            pcs = [1] if b == 0 else [0, 1]

            def mmpv():
                for hl in range(8):
                    p, m = hl // 2, hl % 2
                    out = banks[BO][:, hl * 65:(hl + 1) * 65] if hl < 7 else banks[BX][:, 256:321]
                    for n_, pc in enumerate(pcs):
                        rbk = rb_prev if pc == 0 else rb_cur
                        i = nc.tensor.matmul(out, lhsT=PT[par][m][pc][:, p, :], rhs=V65[:, rbk, 0:65],
                                             start=(n_ == 0), stop=(n_ == len(pcs) - 1))
                return i

            P.op("pe", mmpv, r=[("PT", par, 0, 0), ("PT", par, 0, 1), ("PT", par, 1, 0), ("PT", par, 1, 1), ("V", 0), ("V", 1)],
                 w=[bkey(BO), bkey(BX)])
            P.op("dve", lambda: nc.vector.scalar_tensor_tensor(
                out=den[:, 0:7], in0=O7[:, :, 64], scalar=2.0, in1=es2[:, g * 8:g * 8 + 7], op0=ALU.mult, op1=ALU.add),
                r=[bkey(BO), "es2"], w=["den7"], tiny=True)
            P.op("dve", lambda: nc.vector.scalar_tensor_tensor(
                out=den[:, 7:8], in0=banks[BX][:, 320:321], scalar=2.0, in1=es2[:, g * 8 + 7:g * 8 + 8],
                op0=ALU.mult, op1=ALU.add), r=[bkey(BX), "es2"], w=["den1"], tiny=True)
            P.op("dve", lambda: nc.vector.reciprocal(out=rr, in_=den), r=["den7", "den1"], w=["rr"], tiny=True)
            P.op("dve", lambda: nc.vector.tensor_tensor(out=o2v[:, 0:7, :], in0=O7[:, :, 0:64],
                                                        in1=rr[:, 0:7].unsqueeze(2).to_broadcast([128, 7, 64]), op=ALU.mult),
                 r=[bkey(BO), "rr"], w=["o2a"])
            P.op("dve", lambda: nc.vector.tensor_scalar(out=o2[:, 448:512], in0=banks[BX][:, 256:320], scalar1=rr[:, 7:8],
                                                        scalar2=None, op0=ALU.mult), r=[bkey(BX), "rr"], w=["o2b"])
            P.op("dve", lambda: nc.vector.tensor_tensor(out=og, in0=o2, in1=u0[par], op=ALU.mult),
                 r=["o2a", "o2b", ("u0", par)], w=["og"])

        def stT(b):
            def tro():
                for f in range(4):
                    i = nc.tensor.transpose(out=bx_bf[:, f * 128:(f + 1) * 128], in_=og[:, f * 128:(f + 1) * 128],
                                            identity=ident[:, :])
                return i

            P.op("pe", tro, r=["og", "ident"], w=[bkey(BX)])
            P.op("dve", lambda: nc.vector.tensor_copy(out=ogT, in_=bx_bf.rearrange("p (f t) -> p f t", f=4)),
                 r=[bkey(BX)], w=["ogT"])

        def stY(b):
            for n_ in range(2):
                by = next_pp()

                def mmy(by=by, n_=n_):
                    for f in range(4):
                        i = nc.tensor.matmul(banks[by][:, :], lhsT=ogT[:, f, :], rhs=wo[:, f, n_ * 512:(n_ + 1) * 512],
                                             start=(f == 0), stop=(f == 3))
                    return i

                P.op("pe", mmy, r=["ogT", ("wout", 0)], w=[bkey(by)])
                P.op("dve", lambda by=by, n_=n_: nc.vector.tensor_tensor(
                    out=xres[:, b, n_ * 512:(n_ + 1) * 512], in0=xres[:, b, n_ * 512:(n_ + 1) * 512],
                    in1=banks[by][:, :], op=ALU.add), r=[bkey(by), ("x", b)], w=[("x", b)])

        proj(0)
        for i in range(NB + 2):
            if 0 <= i - 2 < NB:
                stT(i - 2)
            if i < NB:
                stA(i)
            if 0 <= i - 1 < NB:
                stB(i - 1)
            if 0 <= i - 2 < NB:
                stY(i - 2)
            if i % 4 == 1 and i // 4 + 1 < 4:
                proj(i // 4 + 1)
        last_group_cb()

    def l1_head(seq, h, s, last_cb):
        wi, wo = win[s], wout[0]
        wk = ("win", s)
        G = HC["g128"][h]
        P.op("pool", lambda: nc.gpsimd.memset(Rst, 0.0), w=["R"])
        P.op("pool", lambda: nc.gpsimd.memset(Rbf, 0.0), w=["Rbf"])
        st6 = small[:, 16:22]
        mv = small[:, 22:24]
        rs_ = small[:, 24:25]
        nb_ = small[:, 25:26]
        b3v = banks[BX][:, 256:384].bitcast(BF16)
        kxa = kxb = bkey(BX)

        def proj(sbi, parts=(0, 1)):
            rs = (h * 4 + sbi) % 2
            par = sbi % 2
            toks = slice(sbi * 512, (sbi + 1) * 512)
            hkeys = [("hT", sbi * 4 + i) for i in range(4)]
            if 0 in parts:
                P.dma("sp", lambda: nc.sync.dma_start(out=rott[rs], in_=c_rot[h, sbi].rearrange("t j c -> j t c")),
                      w=[("rot", rs)], key="rot%d" % rs)
            for which, (c0_, dstT) in enumerate(((0, qT[par]), (256, kT[par]))):
                if which not in parts:
                    continue
                b1 = next_pp()
                b2 = next_pp()

                def mmqk(c0_=c0_, b1=b1, b2=b2):
                    for dc, bk in ((0, b1), (1, b2)):
                        for k in range(8):
                            i = nc.tensor.matmul(banks[bk][:, :], lhsT=wi[:, k, c0_ + dc * 128:c0_ + (dc + 1) * 128],
                                                 rhs=hT[:, k, toks], start=(k == 0), stop=(k == 7))
                    return i

                P.op("pe", mmqk, r=[wk] + hkeys, w=[bkey(b1), bkey(b2)])
                C = rott[rs][:, 2 * which, :]
                S = rott[rs][:, 2 * which + 1, :]

                def rot(b1=b1, b2=b2, C=C, S=S, dstT=dstT):
                    nc.vector.tensor_tensor(out=ra, in0=banks[b1][:, :], in1=C, op=ALU.mult)
                    nc.vector.tensor_tensor(out=rb, in0=banks[b2][:, :], in1=S, op=ALU.mult)
                    nc.vector.tensor_tensor(out=dstT[:, 0, :], in0=ra, in1=rb, op=ALU.subtract)
                    nc.vector.tensor_tensor(out=ra, in0=banks[b1][:, :], in1=S, op=ALU.mult)
                    nc.vector.tensor_tensor(out=rb, in0=banks[b2][:, :], in1=C, op=ALU.mult)
                    return nc.vector.tensor_tensor(out=dstT[:, 1, :], in0=ra, in1=rb, op=ALU.add)

                P.op("dve", rot, r=[bkey(b1), bkey(b2), ("rot", rs)], w=[("qk", which, par), "rab"])

        def stA(b):
            sbi, blk = b // 4, b % 4
            par, sp_ = b % 2, sbi % 2
            cs = slice(blk * 128, (blk + 1) * 128)

            bv = next_pp()

            def mmv():
                for k in range(8):
                    i = nc.tensor.matmul(banks[bv][:, :], lhsT=hT[:, k, b * 128:(b + 1) * 128], rhs=wi[:, k, 512:1024],
                                         start=(k == 0), stop=(k == 7))
                return i

            P.op("pe", mmv, r=[wk, ("hT", b)], w=[bkey(bv)])
            P.op("act", lambda: nc.scalar.copy(out=vb[par], in_=banks[bv][:, :]), r=[bkey(bv)], w=[("v", par)])
            bg = next_pp()

            def mmg():
                for k in range(8):
                    i = nc.tensor.matmul(banks[bg][:, :], lhsT=hT[:, k, b * 128:(b + 1) * 128], rhs=wi[:, k, 1024:1536],
                                         start=(k == 0), stop=(k == 7))
                return i

            P.op("pe", mmg, r=[wk, ("hT", b)], w=[bkey(bg)])
            P.op("act", lambda: nc.scalar.activation(out=t1, in_=banks[bg][:, :], func=AF.Tanh, scale=0.5), r=[bkey(bg)], w=["t1"])
            P.op("dve", lambda: nc.vector.scalar_tensor_tensor(out=u1[par], in0=t1, scalar=1.0, in1=banks[bg][:, :],
                                                               op0=ALU.add, op1=ALU.mult), r=[bkey(bg), "t1"], w=[("u1", par)])

            def mms():
                nc.tensor.matmul(banks[BX][:, 384:512], lhsT=kT[sp_][:, 0, cs], rhs=qT[sp_][:, 0, cs], start=True, stop=False)
                return nc.tensor.matmul(banks[BX][:, 384:512], lhsT=kT[sp_][:, 1, cs], rhs=qT[sp_][:, 1, cs], start=False, stop=True)

            P.op("pe", mms, r=[("qk", 0, sp_), ("qk", 1, sp_)], w=[kxb])
            P.op("act", lambda: nc.scalar.copy(out=sraw, in_=banks[BX][:, 384:512]), r=[kxb], w=["sraw"])
            P.op("pool", lambda: nc.gpsimd.tensor_tensor(out=sT[par], in0=sraw, in1=maskT[:, :], op=ALU.mult),
                 r=["sraw", "maskT"], w=[("sT", par)])

            def trk():
                nc.tensor.transpose(out=b3v[:, 0:128], in_=kT[sp_][:, 0, cs], identity=ident[:, :])
                return nc.tensor.transpose(out=b3v[:, 128:256], in_=kT[sp_][:, 1, cs], identity=ident[:, :])

            P.op("pe", trk, r=[("qk", 1, sp_), "ident"], w=[kxb])
            P.op("act", lambda: nc.scalar.activation(out=kt[par], in_=b3v, func=AF.Identity, scale=G), r=[kxb], w=[("kt", par)])

        def stB(b):
            sbi, blk = b // 4, b % 4
            par, sp_ = b % 2, sbi % 2
            cs = slice(blk * 128, (blk + 1) * 128)

            def mmo():
                nc.tensor.matmul(banks[4][:, :], lhsT=sT[par], rhs=vb[par], start=True, stop=False)
                nc.tensor.matmul(banks[4][:, :], lhsT=qT[sp_][:, 0, cs], rhs=Rbf[:, 0, :], start=False, stop=False)
                return nc.tensor.matmul(banks[4][:, :], lhsT=qT[sp_][:, 1, cs], rhs=Rbf[:, 1, :], start=False, stop=True)

            P.op("pe", mmo, r=[("sT", par), ("v", par), ("qk", 0, sp_), "Rbf"], w=[bkey(4)])

            def mmu():
                nc.tensor.matmul(banks[5][:, :], lhsT=kt[par][:, 0:128], rhs=vb[par], start=True, stop=True)
                return nc.tensor.matmul(banks[6][:, :], lhsT=kt[par][:, 128:256], rhs=vb[par], start=True, stop=True)

            P.op("pe", mmu, r=[("kt", par), ("v", par)], w=[bkey(5), bkey(6)])

            def upd():
                nc.vector.scalar_tensor_tensor(out=Rst[:, 0, :], in0=Rst[:, 0, :], scalar=G, in1=banks[5][:, :],
                                               op0=ALU.mult, op1=ALU.add)
                return nc.vector.scalar_tensor_tensor(out=Rst[:, 1, :], in0=Rst[:, 1, :], scalar=G, in1=banks[6][:, :],
                                                      op0=ALU.mult, op1=ALU.add)

            P.op("dve", upd, r=[bkey(5), bkey(6), "R"], w=["R"])
            if b < NB - 1:
                P.op("act", lambda: nc.scalar.copy(out=Rbf, in_=Rst), r=["R"], w=["Rbf"])
            else:
                P.dma("sp", lambda: nc.sync.dma_start(out=ret_p[seq, h].rearrange("(c p) v -> p c v", p=128), in_=Rst),
                      r=["R"], w=[], key="rout")
            P.op("dve", lambda: nc.vector.bn_stats(out=st6, in_=banks[4][:, :]), r=[bkey(4)], w=["st6"], tiny=True)
            P.op("dve", lambda: nc.vector.bn_aggr(out=mv, in_=st6), r=["st6"], w=["mv"], tiny=True)
            P.op("pool", lambda: nc.gpsimd.tensor_scalar(out=rs_, in0=mv[:, 1:2], scalar1=4.0, scalar2=float(4.0 * EPS),
                                                         op0=ALU.mult, op1=ALU.add), r=["mv"], w=["rs_"], tiny=True)
            P.op("pool", lambda: nc.gpsimd.tensor_tensor(out=rs_, in0=rs_, in1=nhalf[:, 0:1], op=ALU.pow),
                 r=["rs_", "nhalf"], w=["rs_"], tiny=True)
            P.op("pool", lambda: nc.gpsimd.tensor_scalar(out=nb_, in0=mv[:, 0:1], scalar1=rs_, scalar2=-1.0,
                                                         op0=ALU.mult, op1=ALU.mult), r=["mv", "rs_"], w=["nb_"], tiny=True)

        def stB2(b):
            par = b % 2
            P.op("act", lambda: nc.scalar.activation(out=on, in_=banks[4][:, :], func=AF.Identity, scale=rs_, bias=nb_),
                 r=[bkey(4), "rs_", "nb_"], w=["on"])
            P.op("dve", lambda: nc.vector.tensor_tensor(out=gtd, in0=on, in1=u1[par], op=ALU.mult),
                 r=["on", ("u1", par)], w=["gtd"])

        def stT(b):
            def trg():
                for f in range(4):
                    i = nc.tensor.transpose(out=bx_bf[:, f * 128:(f + 1) * 128], in_=gtd[:, f * 128:(f + 1) * 128],
                                            identity=ident[:, :])
                return i

            P.op("pe", trg, r=["gtd", "ident"], w=[kxa])
            P.op("act", lambda: nc.scalar.copy(out=gTt, in_=bx_bf.rearrange("p (f t) -> p f t", f=4)), r=[kxa], w=["gTt"])

        def stY(b):
            for n_ in range(2):
                by = next_pp()

                def mmy(by=by, n_=n_):
                    for f in range(4):
                        i = nc.tensor.matmul(banks[by][:, :], lhsT=gTt[:, f, :], rhs=wo[:, f, n_ * 512:(n_ + 1) * 512],
                                             start=(f == 0), stop=(f == 3))
                    return i

                P.op("pe", mmy, r=["gTt", ("wout", 0)], w=[bkey(by)])
                P.op("dve", lambda by=by, n_=n_: nc.vector.tensor_tensor(
                    out=xres[:, b, n_ * 512:(n_ + 1) * 512], in0=xres[:, b, n_ * 512:(n_ + 1) * 512],
                    in1=banks[by][:, :], op=ALU.add), r=[bkey(by), ("x", b)], w=[("x", b)])

        proj(0)
        for i in range(NB + 3):
            if 0 <= i - 3 < NB:
                stT(i - 3)
            if 0 <= i - 2 < NB:
                stB2(i - 2)
            if i < NB:
                stA(i)
            if 0 <= i - 1 < NB:
                stB(i - 1)
            if 0 <= i - 3 < NB:
                stY(i - 3)
            if i % 4 == 1 and i // 4 + 1 < 4:
                proj(i // 4 + 1, parts=(0,))
            if i % 4 == 2 and i // 4 + 1 < 4:
                proj(i // 4 + 1, parts=(1,))
        last_cb()

    def load_x(seq, b, q="sp"):
        eng = nc.sync if q == "sp" else nc.gpsimd
        P.dma(q, lambda: eng.dma_start(out=xres[:, b, :], in_=x_prompt[seq, b * 128:(b + 1) * 128, :]),
              w=[("x", b)], key="x%d" % b)

    def final_norm(seq):
        P.op("pool", lambda: nc.gpsimd.memset(ss[:, :], 0.0), w=["ss"] + [("ss", b) for b in range(NB)])
        for b in range(NB):
            P.op("act", lambda b=b: nc.scalar.activation(out=xs_bf[0], in_=xres[:, b, :], func=AF.Square, accum_out=ss[:, b:b + 1]),
                 r=[("x", b), "ss"], w=[("xs", 0), ("ss", b)])
            P.op("pool", lambda b=b: nc.gpsimd.tensor_scalar(out=rstd[:, b:b + 1], in0=ss[:, b:b + 1], scalar1=1.0 / D,
                                                              scalar2=float(EPS), op0=ALU.mult, op1=ALU.add),
                 r=[("ss", b)], w=[("rstd", b)], tiny=True)
            P.op("pool", lambda b=b: nc.gpsimd.tensor_tensor(out=rstd[:, b:b + 1], in0=rstd[:, b:b + 1], in1=nhalf[:, 0:1],
                                                              op=ALU.pow),
                 r=[("rstd", b), "nhalf"], w=[("rstd", b)], tiny=True)
            P.op("dve", lambda b=b: nc.vector.scalar_tensor_tensor(out=xres[:, b, :], in0=xres[:, b, :], scalar=rstd[:, b:b + 1],
                                                                   in1=gfin, op0=ALU.mult, op1=ALU.mult),
                 r=[("x", b), ("rstd", b), "gfin"], w=[("x", b)])
            P.dma("sp", lambda b=b: nc.sync.dma_start(out=y_prompt[seq, b * 128:(b + 1) * 128, :], in_=xres[:, b, :]),
                  r=[("x", b)], w=[], key="x%d" % b)
        if seq + 1 < NSEQ:
            for b in range(NB):
                load_x(seq + 1, b, q="pool")


    def sample_phase():
        hw = hT[:, :, :].rearrange("p k t -> p (k t)").bitcast(F32)
        off = [0]

        def f32(n):
            a = hw[:, off[0]:off[0] + n]
            off[0] += n
            assert off[0] <= 8192
            return a

        def bf(n):
            a = hw[:, off[0]:off[0] + n // 2].bitcast(BF16)
            off[0] += n // 2
            assert off[0] <= 8192
            return a

        xs_t = f32(1024)
        xsb = bf(1024)
        hsT = bf(128).rearrange("p (k t) -> p k t", k=8)
        us = f32(512)
        os_ = f32(512)
        t_s = f32(512)
        ogs = bf(512)
        gsT = bf(64).rearrange("p (f t) -> p f t", f=4)
        QTs2 = [bf(64).rearrange("p (q t) -> p q t", q=4) for _ in range(2)]
        us2 = [us, f32(512)]
        knv = f32(128)
        Kpad2 = [bf(192), bf(192)]
        KTp2 = [bf(256).rearrange("p (m j) -> p m j", m=2) for _ in range(2)]
        PTs2 = [bf(8), bf(8)]
        biasS = bf(32)
        etS = bf(128)
        es2s = f32(8).rearrange("p (k m) -> p k m", k=4)
        sm = f32(16)
        ob2 = [f32(128).rearrange("p (m d) -> p m d", m=2) for _ in range(2)]
        qk4 = f32(64).rearrange("p (i t) -> p i t", i=4)
        qf = f32(32).rearrange("p (c t) -> p c t", c=2)
        kf = f32(32).rearrange("p (c t) -> p c t", c=2)
        tmpr = f32(16)
        prodf = bf(32).rearrange("p (c t) -> p c t", c=2)
        qTs = bf(32).rearrange("p (c t) -> p c t", c=2)
        kTs = bf(32).rearrange("p (c t) -> p c t", c=2)
        ktok = bf(256)
        vtokb = bf(512)
        o1 = f32(512)
        gts = bf(512)
        Zq = bf(512).rearrange("p (c s t) -> p c s t", c=2, s=16)
        Zk = work[:, 1024:3072].bitcast(BF16).rearrange("p (s d) -> p s d", s=16)
        eyeq = bf(256).rearrange("p (s t) -> p s t", s=16)
        rots = f32(4)
        onesf = bf(2)
        gfs = work[:, 3072:4096]
        Kw = xres[:, 0:4, :].rearrange("p a (b c) -> p (a b) c", c=256)
        Vw = xres[:, 4:8, :].rearrange("p a (b c) -> p (a b) c", c=256)
        Vw65_2 = [xres[:, r_, :].bitcast(BF16)[:, 0:16 * 66].rearrange("p (b d) -> p b d", b=16) for r_ in (8, 14)]
        Rb = [xres[:, 9 + i, :].rearrange("p (c v) -> p c v", c=2) for i in range(4)]
        Rbfs = [xres[:, 13, :].bitcast(BF16)[:, i * 1024:(i + 1) * 1024].rearrange("p (c v) -> p c v", c=2) for i in range(2)]
        b3bf = banks[3][:, :].bitcast(BF16)
        b0bf = banks[0][:, :].bitcast(BF16)

        P.dma("sp", lambda: nc.sync.dma_start(out=xs_t[0:16, :], in_=x_sample[:, :]), w=["xs_t"], key="sconst")
        for pi, (p0, p1) in enumerate(((0, 112), (112, 127))):
            P.dma("sp", lambda p0=p0, p1=p1: nc.sync.dma_start(out=Kw[p0:p1, :, :],
                                                                in_=cache_k[:, 1 + p0:1 + p1, :].rearrange("b j c -> j b c")),
                  w=[("Kwl", pi)], key="sconst")
            P.dma("sp", lambda p0=p0, p1=p1: nc.sync.dma_start(out=Vw[p0:p1, :, :],
                                                                in_=cache_v[:, 1 + p0:1 + p1, :].rearrange("b j c -> j b c")),
                  w=[("Vwl", pi)], key="sconst")
        P.dma("sp", lambda: nc.sync.dma_start(out=eyeq, in_=c_eyeq.rearrange("p (s t) -> p s t", s=16)), w=["eyeq"], key="sconst")
        P.dma("sp", lambda: nc.sync.dma_start(out=rots, in_=c_rot_s[:, :]), w=["rots"], key="sconst")
        P.dma("sp", lambda: nc.sync.dma_start(out=etS[0:32, :], in_=c_etab_s[:, :]), w=["etS"], key="sconst")
        P.dma("sp", lambda: nc.sync.dma_start(out=es2s[0:4, :, :],
                                              in_=sinks[0].rearrange("(k p m) -> p k m", k=4, p=4, m=2)),
              w=["es2s"], key="sconst")
        P.dma("sp", lambda: nc.sync.dma_start(out=gfs[0:16, :], in_=final_g[0:1, :].to_broadcast([16, D])), w=["gfs"], key="sconst")
        P.op("pool", lambda: nc.gpsimd.memset(onesf, 1.0), w=["onesf"], tiny=True)
        P.op("pool", lambda: nc.gpsimd.memset(Kpad2[0], 0.0), w=[("Kpad", 0)])
        P.op("pool", lambda: nc.gpsimd.memset(Kpad2[1], 0.0), w=[("Kpad", 1)])
        P.op("pool", lambda: nc.gpsimd.memset(Vw65_2[0][:, :, 64:66], 1.0), w=[("Vw65", 0)], tiny=True)
        P.op("pool", lambda: nc.gpsimd.memset(Vw65_2[1][:, :, 64:66], 1.0), w=[("Vw65", 1)], tiny=True)
        P.barrier()
        P.dma("act", lambda: nc.scalar.dma_start(out=swa_k_s[:, 0:127, :], in_=cache_k[:, 1:128, :]), w=[], key="cshift")
        P.dma("act", lambda: nc.scalar.dma_start(out=swa_v_s[:, 0:127, :], in_=cache_v[:, 1:128, :]), w=[], key="cshift")
        biasA_dmas(q="act")
        P.op("act", lambda: nc.scalar.activation(out=es2s[0:4, :, :], in_=es2s[0:4, :, :], func=AF.Exp), r=["es2s"], w=["es2s"], tiny=True)
        P.op("dve", lambda: nc.vector.tensor_scalar(out=es2s[0:4, :, :], in0=es2s[0:4, :, :], scalar1=2.0, scalar2=None, op0=ALU.mult),
             r=["es2s"], w=["es2s"], tiny=True)
        P.op("pe", lambda: nc.tensor.matmul(banks[0][:, 0:32], lhsT=etS[0:32, :],
                                            rhs=rb33b[0:32, :].rearrange("b (k p m) -> b k m p", k=4, p=4, m=2),
                                            start=True, stop=True), r=["etS", "rb33b"], w=[bkey(0)])
        P.op("dve", lambda: nc.vector.tensor_copy(out=biasS, in_=banks[0][:, 0:32]), r=[bkey(0)], w=["biasS"], tiny=True)

        def s_norm(layer):
            P.op("pool", lambda: nc.gpsimd.memset(sm[0:16, 0:1], 0.0), w=["sm_ss"], tiny=True)
            P.op("act", lambda: nc.scalar.activation(out=xsb[0:16, :], in_=xs_t[0:16, :], func=AF.Square, accum_out=sm[0:16, 0:1]),
                 r=["xs_t", "sm_ss"], w=["sm_ss", "xsb"], tiny=True)
            P.op("pool", lambda: nc.gpsimd.tensor_scalar(out=sm[0:16, 1:2], in0=sm[0:16, 0:1], scalar1=1.0 / D, scalar2=float(EPS),
                                                         op0=ALU.mult, op1=ALU.add), r=["sm_ss"], w=["sm_r"], tiny=True)
            P.op("pool", lambda: nc.gpsimd.tensor_tensor(out=sm[0:16, 1:2], in0=sm[0:16, 1:2], in1=nhalf[0:16, 0:1], op=ALU.pow),
                 r=["sm_r", "nhalf"], w=["sm_r"], tiny=True)
            P.op("dve", lambda: nc.vector.tensor_scalar(out=xsb[0:16, :], in0=xs_t[0:16, :], scalar1=sm[0:16, 1:2], scalar2=None,
                                                        op0=ALU.mult), r=["xs_t", "sm_r"], w=["xsb"])

            def tr():
                for k in range(8):
                    i = nc.tensor.transpose(out=b3bf[:, k * 16:(k + 1) * 16], in_=xsb[0:16, k * 128:(k + 1) * 128],
                                            identity=ident[0:16, 0:16])
                return i

            P.op("pe", tr, r=["xsb", "ident"], w=[bkey(3)])
            P.op("dve", lambda layer=layer: nc.vector.tensor_tensor(
                out=hsT, in0=b3bf[:, 0:128].rearrange("p (k t) -> p k t", k=8),
                in1=gT[:, layer, :].unsqueeze(2).to_broadcast([128, 8, 16]), op=ALU.mult),
                r=[bkey(3), "gT"], w=["hsT"])

        sorder = [(0, g_) for g_ in range(4)] + [(1, h_) for h_ in range(4)]
        spre = {0: early_slot}

        def sget(i):
            if i < len(sorder) and i not in spre:
                spre[i] = load_win(*sorder[i])
            return spre.get(i)


        sget(0)
        s_norm(0)
        def front(g):
            s = sget(g)
            gp = g % 2
            wi = win_l0[s]
            wk = ("win", s)

            def mmq():
                for p in range(4):
                    for k in range(8):
                        i = nc.tensor.matmul(banks[0][:, p * 16:(p + 1) * 16], lhsT=wi[:, k, p * 128:(p + 1) * 128], rhs=hsT[:, k, :],
                                             start=(k == 0), stop=(k == 7))
                return i

            P.op("pe", mmq, r=[wk, "hsT"], w=[bkey(0)])
            P.op("dve", lambda: nc.vector.tensor_copy(out=QTs2[gp], in_=banks[0][:, 0:64].rearrange("p (q t) -> p q t", q=4)),
                 r=[bkey(0)], w=[("QTs", gp)])

            def mmkv():
                for j, c0_ in enumerate((512, 768)):
                    for k in range(8):
                        i = nc.tensor.matmul(banks[1][0:16, j * 64:(j + 1) * 64], lhsT=hsT[:, k, :], rhs=wi[:, k, c0_:c0_ + 64],
                                             start=(k == 0), stop=(k == 7))
                return i

            P.op("pe", mmkv, r=[wk, "hsT"], w=[bkey(1)])
            P.op("dve", lambda: nc.vector.tensor_copy(out=knv[0:16, :], in_=banks[1][0:16, 0:128]), r=[bkey(1)], w=["knv"])
            P.dma("sp", lambda g=g: nc.sync.dma_start(out=swa_k_s[:, 127, g * 64:(g + 1) * 64], in_=knv[0:16, 0:64]),
                  r=["knv"], w=["skd"], key="skn")
            P.dma("sp", lambda g=g: nc.sync.dma_start(out=swa_v_s[:, 127, g * 64:(g + 1) * 64], in_=knv[0:16, 64:128]),
                  r=["knv"], w=["svd"], key="skn")
            P.dma("sp", lambda g=g: nc.sync.dma_start(out=Kw[127:128, :, g * 64:(g + 1) * 64],
                                                      in_=swa_k_s[:, 127:128, g * 64:(g + 1) * 64].rearrange("b o d -> o b d")),
                  r=["skd", "svd"], w=[("Kw", g)], key="skn2k")
            P.dma("sp", lambda g=g: nc.sync.dma_start(out=Vw[127:128, :, g * 64:(g + 1) * 64],
                                                      in_=swa_v_s[:, 127:128, g * 64:(g + 1) * 64].rearrange("b o d -> o b d")),
                  r=["skd", "svd"], w=[("Vw", g)], key="skn2v")

            def mmg():
                for k in range(8):
                    i = nc.tensor.matmul(banks[2][0:16, :], lhsT=hsT[:, k, :], rhs=wi[:, k, 832:1344], start=(k == 0), stop=(k == 7))
                return i

            P.op("pe", mmg, r=[wk, "hsT"], w=[bkey(2)])
            P.op("act", lambda: nc.scalar.activation(out=t_s[0:16, :], in_=banks[2][0:16, :], func=AF.Tanh, scale=0.5),
                 r=[bkey(2)], w=["t_s"])
            P.op("dve", lambda: nc.vector.scalar_tensor_tensor(out=us2[gp][0:16, :], in0=t_s[0:16, :], scalar=1.0, in1=banks[2][0:16, :],
                                                               op0=ALU.add, op1=ALU.mult), r=[bkey(2), "t_s"], w=[("us", gp)])
            P.op("dve", lambda g=g: nc.vector.tensor_copy(out=Vw65_2[gp][:, :, 0:64], in_=Vw[:, :, g * 64:(g + 1) * 64]),
                 r=[("Vw", g)], w=[("Vw65", gp)])

        front(0)
        for g in range(4):
            s = sget(g)
            sget(g + 1)
            gp = g % 2
            wo = wout[0]
            def S1(b):
                pr = b % 2
                P.op("dve", lambda: nc.vector.tensor_copy(out=Kpad2[pr][:, 64:128], in_=Kw[:, b, g * 64:(g + 1) * 64]),
                     r=[("Kw", g)], w=[("Kpad", pr)])

                def trk():
                    nc.tensor.transpose(out=b3bf[:, 0:128], in_=Kpad2[pr][:, 64:192], identity=ident[:, :])
                    return nc.tensor.transpose(out=b3bf[:, 128:256], in_=Kpad2[pr][:, 0:128], identity=ident[:, :])

                P.op("pe", trk, r=[("Kpad", pr), "ident"], w=[bkey(3)])
                P.op("act", lambda: nc.scalar.copy(out=KTp2[pr], in_=b3bf[:, 0:256].rearrange("p (m j) -> p m j", m=2)),
                     r=[bkey(3)], w=[("KTp", pr)])

            def S2(b):
                pr = b % 2

                def mml():
                    nc.tensor.matmul(banks[4][:, 0:8], lhsT=ident[:, :], rhs=biasS[:, g * 8:(g + 1) * 8], start=True, stop=False)
                    nc.tensor.matmul(banks[4][:, 0:4], lhsT=KTp2[pr][:, 0, :], rhs=QTs2[gp][:, :, b], start=False, stop=False)
                    return nc.tensor.matmul(banks[4][:, 4:8], lhsT=KTp2[pr][:, 1, :], rhs=QTs2[gp][:, :, b], start=False, stop=True)

                P.op("pe", mml, r=[("KTp", pr), ("QTs", gp), "biasS", "ident"], w=[bkey(4)])
                P.op("act", lambda: nc.scalar.activation(out=PTs2[pr], in_=banks[4][:, 0:8], func=AF.Exp, scale=0.125),
                     r=[bkey(4)], w=[("PTs", pr)], tiny=True)

            def S3(b):
                pr = b % 2

                def mmpv():
                    nc.tensor.matmul(banks[5][0:4, 0:65], lhsT=PTs2[pr][:, 0:4], rhs=Vw65_2[gp][:, b, 0:65], start=True, stop=True)
                    return nc.tensor.matmul(banks[5][0:4, 65:130], lhsT=PTs2[pr][:, 4:8], rhs=Vw65_2[gp][:, b, 0:65], start=True, stop=True)

                P.op("pe", mmpv, r=[("PTs", pr), ("Vw65", gp)], w=[bkey(5)])
                O2 = banks[5][0:4, 0:130].rearrange("p (m d) -> p m d", m=2)
                P.op("dve", lambda: nc.vector.scalar_tensor_tensor(out=sm[0:4, 4:6], in0=O2[:, :, 64], scalar=2.0,
                                                                   in1=es2s[0:4, g, :], op0=ALU.mult, op1=ALU.add),
                     r=[bkey(5), "es2s"], w=["sden"], tiny=True)
                P.op("dve", lambda: nc.vector.reciprocal(out=sm[0:4, 6:8], in_=sm[0:4, 4:6]), r=["sden"], w=["srr"], tiny=True)
                P.op("dve", lambda: nc.vector.tensor_tensor(out=ob2[pr][0:4, :, :], in0=O2[:, :, 0:64],
                                                            in1=sm[0:4, 6:8].unsqueeze(2).to_broadcast([4, 2, 64]), op=ALU.mult),
                     r=[bkey(5), "srr"], w=[("ob", pr)])
                P.dma("sp", lambda: nc.sync.dma_start(out=osamp[g, b, :].rearrange("(p m d) -> p m d", p=4, m=2),
                                                      in_=ob2[pr][0:4, :, :]), r=[("ob", pr)], w=[("osd", pr)], key="osamp%d" % pr)

            for i in range(SB + 2):
                if i < SB:
                    S1(i)
                if 0 <= i - 1 < SB:
                    S2(i - 1)
                if 0 <= i - 2 < SB:
                    S3(i - 2)
                if i == 8 and g + 1 < 4:
                    front(g + 1)
            P.dma("sp", lambda g=g: nc.sync.dma_start(out=os_[0:16, :], in_=osamp[g, :, :]), r=[("osd", 0), ("osd", 1)], w=["os_"], key="osamp2")
            P.op("dve", lambda: nc.vector.tensor_tensor(out=ogs[0:16, :], in0=os_[0:16, :], in1=us2[gp][0:16, :], op=ALU.mult),
                 r=["os_", ("us", gp)], w=["ogs"])

            def trg():
                for f in range(4):
                    i = nc.tensor.transpose(out=b3bf[:, f * 16:(f + 1) * 16], in_=ogs[0:16, f * 128:(f + 1) * 128],
                                            identity=ident[0:16, 0:16])
                return i

            P.op("pe", trg, r=["ogs", "ident"], w=[bkey(3)])
            P.op("act", lambda: nc.scalar.copy(out=gsT, in_=b3bf[:, 0:64].rearrange("p (f t) -> p f t", f=4)), r=[bkey(3)], w=["gsT"])

            def mmy(g=g):
                for n_ in range(2):
                    for f in range(4):
                        i = nc.tensor.matmul(banks[6 + n_][0:16, :], lhsT=gsT[:, f, :], rhs=wo[:, f, n_ * 512:(n_ + 1) * 512],
                                             start=(g == 0 and f == 0), stop=(g == 3 and f == 3))
                return i

            P.op("pe", mmy, r=["gsT", ("wout", 0)], w=[bkey(6), bkey(7)])
            if g + 1 < len(sorder):
                load_wout(*sorder[g + 1])
        P.op("dve", lambda: nc.vector.tensor_tensor(out=xs_t[0:16, 0:512], in0=xs_t[0:16, 0:512], in1=banks[6][0:16, :], op=ALU.add),
             r=[bkey(6), "xs_t"], w=["xs_t"])
        P.op("dve", lambda: nc.vector.tensor_tensor(out=xs_t[0:16, 512:1024], in0=xs_t[0:16, 512:1024], in1=banks[7][0:16, :], op=ALU.add),
             r=[bkey(7), "xs_t"], w=["xs_t"])

        if stage == "S0":
            P.dma("sp", lambda: nc.sync.dma_start(out=y_sample[:, :], in_=xs_t[0:16, :]), r=["xs_t"], w=[], key="sconst")
            P.barrier()
            return
        import os as _os
        _dbg = _os.environ.get("KDBG", "")

        def cut(name):
            if _dbg == name:
                P.dma("sp", lambda: nc.sync.dma_start(out=y_sample[:, :], in_=xs_t[0:16, :]), r=["xs_t"], w=[], key="sconst")
                P.barrier()
                P.muted = True

        s_norm(1)
        it = 0
        for h in range(4):
            s = sget(4 + h)
            sget(4 + h + 1)
            wi, wo = win[s], wout[0]
            wk = ("win", s)
            G1 = HC["g1"][h]

            def mmqk():
                for idx, c0_ in enumerate((0, 128, 256, 384)):
                    for k in range(8):
                        i = nc.tensor.matmul(banks[0][:, idx * 16:(idx + 1) * 16], lhsT=wi[:, k, c0_:c0_ + 128], rhs=hsT[:, k, :],
                                             start=(k == 0), stop=(k == 7))
                return i

            P.op("pe", mmqk, r=[wk, "hsT"], w=[bkey(0)])
            P.op("dve", lambda: nc.vector.tensor_copy(out=qk4, in_=banks[0][:, 0:64].rearrange("p (i t) -> p i t", i=4)),
                 r=[bkey(0)], w=["qk4"], tiny=True)
            for (src0, dst, cc, sc, nm) in ((0, qf, 0, 1, "qf"), (2, kf, 2, 3, "kf")):
                P.op("dve", lambda src0=src0, sc=sc: nc.vector.tensor_scalar(out=tmpr, in0=qk4[:, src0 + 1, :], scalar1=rots[:, sc:sc + 1],
                                                                             scalar2=None, op0=ALU.mult), r=["qk4", "rots"], w=["tmpr"], tiny=True)
                P.op("dve", lambda src0=src0, dst=dst, cc=cc: nc.vector.scalar_tensor_tensor(
                    out=dst[:, 0, :], in0=qk4[:, src0, :], scalar=rots[:, cc:cc + 1], in1=tmpr, op0=ALU.mult, op1=ALU.subtract),
                    r=["qk4", "rots", "tmpr"], w=[nm + "0"], tiny=True)
                P.op("dve", lambda src0=src0, cc=cc: nc.vector.tensor_scalar(out=tmpr, in0=qk4[:, src0 + 1, :], scalar1=rots[:, cc:cc + 1],
                                                                             scalar2=None, op0=ALU.mult), r=["qk4", "rots", nm + "0"], w=["tmpr"], tiny=True)
                P.op("dve", lambda src0=src0, dst=dst, sc=sc: nc.vector.scalar_tensor_tensor(
                    out=dst[:, 1, :], in0=qk4[:, src0, :], scalar=rots[:, sc:sc + 1], in1=tmpr, op0=ALU.mult, op1=ALU.add),
                    r=["qk4", "rots", "tmpr"], w=[nm + "1"], tiny=True)
            cut("c1")
            P.op("dve", lambda: nc.vector.tensor_copy(out=qTs, in_=qf), r=["qf0", "qf1"], w=["qTs"], tiny=True)
            P.op("dve", lambda: nc.vector.tensor_copy(out=kTs, in_=kf), r=["kf0", "kf1"], w=["kTs"], tiny=True)
            P.op("dve", lambda: nc.vector.tensor_tensor(out=prodf, in0=qf, in1=kf, op=ALU.mult), r=["qf0", "qf1", "kf0", "kf1"], w=["prodf"], tiny=True)

            def mmdot():
                nc.tensor.matmul(banks[0][0:16, 128:130], lhsT=prodf[:, 0, :], rhs=onesf[:, 0:2], start=True, stop=False)
                return nc.tensor.matmul(banks[0][0:16, 128:130], lhsT=prodf[:, 1, :], rhs=onesf[:, 0:2], start=False, stop=True)

            P.op("pe", mmdot, r=["prodf", "onesf"], w=[bkey(0)])
            P.op("dve", lambda: nc.vector.tensor_copy(out=sm[0:16, 8:9], in_=banks[0][0:16, 128:129]), r=[bkey(0)], w=["sdot"], tiny=True)

            cut("c2")

            def mmv():
                for k in range(8):
                    i = nc.tensor.matmul(banks[1][0:16, :], lhsT=hsT[:, k, :], rhs=wi[:, k, 512:1024], start=(k == 0), stop=(k == 7))
                return i

            P.op("pe", mmv, r=[wk, "hsT"], w=[bkey(1)])
            P.op("act", lambda: nc.scalar.copy(out=vtokb[0:16, :], in_=banks[1][0:16, :]), r=[bkey(1)], w=["vtokb"])
            P.op("dve", lambda: nc.vector.tensor_scalar(out=o1[0:16, :], in0=banks[1][0:16, :], scalar1=sm[0:16, 8:9], scalar2=None,
                                                        op0=ALU.mult), r=[bkey(1), "sdot", "vtokb"], w=["o1"])

            def mmg1():
                for k in range(8):
                    i = nc.tensor.matmul(banks[2][0:16, :], lhsT=hsT[:, k, :], rhs=wi[:, k, 1024:1536], start=(k == 0), stop=(k == 7))
                return i

            P.op("pe", mmg1, r=[wk, "hsT"], w=[bkey(2)])
            P.op("act", lambda: nc.scalar.activation(out=t_s[0:16, :], in_=banks[2][0:16, :], func=AF.Tanh, scale=0.5),
                 r=[bkey(2)], w=["t_s"])
            P.op("dve", lambda: nc.vector.scalar_tensor_tensor(out=us[0:16, :], in0=t_s[0:16, :], scalar=1.0, in1=banks[2][0:16, :],
                                                               op0=ALU.add, op1=ALU.mult), r=[bkey(2), "t_s"], w=["us"])

            cut("c3")
            def trk1():
                nc.tensor.transpose(out=b3bf[0:16, 0:128], in_=kTs[:, 0, :], identity=ident[:, :])
                return nc.tensor.transpose(out=b3bf[0:16, 128:256], in_=kTs[:, 1, :], identity=ident[:, :])

            P.op("pe", trk1, r=["kTs", "ident"], w=[bkey(3)])
            P.op("act", lambda: nc.scalar.copy(out=ktok[0:16, :], in_=b3bf[0:16, 0:256]), r=[bkey(3)], w=["ktok"])
            cut("c4")
            P.op("dve", lambda: nc.vector.tensor_tensor(out=Zq, in0=qTs.unsqueeze(2).to_broadcast([128, 2, 16, 16]),
                                                        in1=eyeq.unsqueeze(1).to_broadcast([128, 2, 16, 16]), op=ALU.mult),
                 r=["qTs", "eyeq"], w=["Zq"])
            P.op("dve", lambda: nc.vector.tensor_tensor(out=Zk[0:16, :, :], in0=ktok[0:16, :].unsqueeze(1).to_broadcast([16, 16, 256]),
                                                        in1=ident[0:16, 0:16].unsqueeze(2).to_broadcast([16, 16, 256]), op=ALU.mult),
                 r=["ktok", "ident"], w=["Zk"])
            cut("c5")
            for b in range(SB if "norloop" not in _dbg else 0):
                sl = it % 4
                s2 = it % 2
                ub = 2 + 2 * (it % 2)
                it += 1
                P.dma("sp", lambda b=b, h=h, sl=sl: nc.sync.dma_start(out=Rb[sl], in_=state_ret[b, h].rearrange("(c p) v -> p c v", p=128)),
                      w=[("Rb", sl)], key="rin%d" % sl)
                P.op("act", lambda sl=sl, s2=s2: nc.scalar.copy(out=Rbfs[s2], in_=Rb[sl]), r=[("Rb", sl)], w=[("Rbfs", s2)])

                def mmc(b=b, s2=s2):
                    nc.tensor.matmul(banks[1][0:16, :], lhsT=Zq[:, 0, b, :], rhs=Rbfs[s2][:, 0, :], start=(b == 0), stop=False)
                    return nc.tensor.matmul(banks[1][0:16, :], lhsT=Zq[:, 1, b, :], rhs=Rbfs[s2][:, 1, :], start=False, stop=(b == SB - 1))

                P.op("pe", mmc, r=["Zq", ("Rbfs", s2), "o1", "vtokb"], w=[bkey(1)])

                def mmu(b=b, ub=ub):
                    nc.tensor.matmul(banks[ub][:, :], lhsT=Zk[0:16, b, 0:128], rhs=vtokb[0:16, :], start=True, stop=True)
                    return nc.tensor.matmul(banks[ub + 1][:, :], lhsT=Zk[0:16, b, 128:256], rhs=vtokb[0:16, :], start=True, stop=True)

                P.op("pe", mmu, r=["Zk", "vtokb"], w=[bkey(ub), bkey(ub + 1)])

                def upd(sl=sl, ub=ub, G1=G1):
                    nc.vector.scalar_tensor_tensor(out=Rb[sl][:, 0, :], in0=Rb[sl][:, 0, :], scalar=G1, in1=banks[ub][:, :],
                                                   op0=ALU.mult, op1=ALU.add)
                    return nc.vector.scalar_tensor_tensor(out=Rb[sl][:, 1, :], in0=Rb[sl][:, 1, :], scalar=G1, in1=banks[ub + 1][:, :],
                                                          op0=ALU.mult, op1=ALU.add)

                P.op("dve", upd, r=[bkey(ub), bkey(ub + 1), ("Rb", sl), ("Rbfs", s2)], w=[("Rb", sl)])
                P.dma("pool", lambda b=b, h=h, sl=sl: nc.gpsimd.dma_start(out=ret_s[b, h].rearrange("(c p) v -> p c v", p=128), in_=Rb[sl]),
                      r=[("Rb", sl)], w=[], key="rout_s%d" % sl)
            P.op("dve", lambda G1=G1: nc.vector.scalar_tensor_tensor(out=o1[0:16, :], in0=banks[1][0:16, :], scalar=G1, in1=o1[0:16, :],
                                                                     op0=ALU.mult, op1=ALU.add), r=[bkey(1), "o1"], w=["o1"], tiny=True)
            P.op("dve", lambda: nc.vector.bn_stats(out=sm[0:16, 10:16], in_=o1[0:16, :]), r=["o1"], w=["sst6"], tiny=True)
            P.op("dve", lambda: nc.vector.bn_aggr(out=sm[0:16, 2:4], in_=sm[0:16, 10:16]), r=["sst6"], w=["smv"], tiny=True)
            P.op("pool", lambda: nc.gpsimd.tensor_scalar(out=sm[0:16, 9:10], in0=sm[0:16, 3:4], scalar1=float(EPS), scalar2=None, op0=ALU.add),
                 r=["smv"], w=["srs"], tiny=True)
            P.op("pool", lambda: nc.gpsimd.tensor_tensor(out=sm[0:16, 9:10], in0=sm[0:16, 9:10], in1=nhalf[0:16, 0:1], op=ALU.pow),
                 r=["srs", "nhalf"], w=["srs"], tiny=True)
            P.op("dve", lambda: nc.vector.tensor_scalar(out=o1[0:16, :], in0=o1[0:16, :], scalar1=sm[0:16, 2:3], scalar2=sm[0:16, 9:10],
                                                        op0=ALU.subtract, op1=ALU.mult), r=["o1", "smv", "srs"], w=["o1"])
            P.op("dve", lambda: nc.vector.tensor_tensor(out=gts[0:16, :], in0=o1[0:16, :], in1=us[0:16, :], op=ALU.mult),
                 r=["o1", "us"], w=["gts"])

            def trg1():
                for f in range(4):
                    i = nc.tensor.transpose(out=b3bf[:, f * 16:(f + 1) * 16], in_=gts[0:16, f * 128:(f + 1) * 128],
                                            identity=ident[0:16, 0:16])
                return i

            P.op("pe", trg1, r=["gts", "ident"], w=[bkey(3)])
            P.op("act", lambda: nc.scalar.copy(out=gsT, in_=b3bf[:, 0:64].rearrange("p (f t) -> p f t", f=4)), r=[bkey(3)], w=["gsT"])

            def mmy1(h=h):
                for n_ in range(2):
                    for f in range(4):
                        i = nc.tensor.matmul(banks[6 + n_][0:16, :], lhsT=gsT[:, f, :], rhs=wo[:, f, n_ * 512:(n_ + 1) * 512],
                                             start=(h == 0 and f == 0), stop=(h == 3 and f == 3))
                return i

            P.op("pe", mmy1, r=["gsT", ("wout", 0)], w=[bkey(6), bkey(7)])
            if 4 + h + 1 < len(sorder):
                load_wout(*sorder[4 + h + 1])
        P.op("dve", lambda: nc.vector.scalar_tensor_tensor(out=xs_t[0:16, 0:512], in0=banks[6][0:16, :], scalar=0.5, in1=xs_t[0:16, 0:512],
                                                           op0=ALU.mult, op1=ALU.add), r=[bkey(6), "xs_t"], w=["xs_t"])
        P.op("dve", lambda: nc.vector.scalar_tensor_tensor(out=xs_t[0:16, 512:1024], in0=banks[7][0:16, :], scalar=0.5,
                                                           in1=xs_t[0:16, 512:1024], op0=ALU.mult, op1=ALU.add),
             r=[bkey(7), "xs_t"], w=["xs_t"])
        P.op("pool", lambda: nc.gpsimd.memset(sm[0:16, 0:1], 0.0), w=["sm_ss"], tiny=True)
        P.op("act", lambda: nc.scalar.activation(out=xsb[0:16, :], in_=xs_t[0:16, :], func=AF.Square, accum_out=sm[0:16, 0:1]),
             r=["xs_t", "sm_ss"], w=["sm_ss", "xsb"], tiny=True)
        P.op("pool", lambda: nc.gpsimd.tensor_scalar(out=sm[0:16, 1:2], in0=sm[0:16, 0:1], scalar1=1.0 / D, scalar2=float(EPS),
                                                     op0=ALU.mult, op1=ALU.add), r=["sm_ss"], w=["sm_r"], tiny=True)
        P.op("pool", lambda: nc.gpsimd.tensor_tensor(out=sm[0:16, 1:2], in0=sm[0:16, 1:2], in1=nhalf[0:16, 0:1], op=ALU.pow),
             r=["sm_r", "nhalf"], w=["sm_r"], tiny=True)
        P.op("dve", lambda: nc.vector.scalar_tensor_tensor(out=xs_t[0:16, :], in0=xs_t[0:16, :], scalar=sm[0:16, 1:2], in1=gfs[0:16, :],
                                                           op0=ALU.mult, op1=ALU.mult), r=["xs_t", "sm_r", "gfs"], w=["xs_t"])
        P.dma("sp", lambda: nc.sync.dma_start(out=y_sample[:, :], in_=xs_t[0:16, :]), r=["xs_t"], w=[], key="sconst")
        P.barrier()

    order = []
    for seq in range(NSEQ if not stage.startswith("S") else 0):
        order += [(0, g) for g in range(4)] + ([(1, h) for h in range(4)] if stage != "L0" else [])
    loaded = {}

    def ensure_win(idx):
        if idx < len(order) and idx not in loaded:
            loaded[idx] = load_win(*order[idx])

    def wout_cb(idx):
        def cb():
            if idx < len(order):
                load_wout(*order[idx])
        return cb

    if stage != "L0":
        sample_phase()
        P.muted = False
    widx = 0
    if order:
        ensure_win(0)
        load_wout(*order[0])
    for seq in range(NSEQ if not stage.startswith("S") else 0):
        if seq == 0 or stage == "L0":
            for b in range(NB):
                load_x(seq, b)
        phase_a(0)
        if seq > 0:
            P.barrier()
        for pc in range(2):
            for hq in range(4):
                P.dma("sp", lambda pc=pc, hq=hq: nc.sync.dma_start(
                    out=biasT[:, pc, hq * 8:(hq + 1) * 8, :],
                    in_=bass.AP(tensor=biasA, offset=(pc * 32 + hq * 8) * ALEN + 127, ap=[[255, 128], [ALEN, 8], [1, 128]])),
                    r=["biasA"], w=["biasT"], key="biasT")
        P.op("pool", lambda: nc.gpsimd.memset(V65[:, :, 64:66], 1.0), w=[("V", 0), ("V", 1)])
        for g in range(4):
            ensure_win(widx + g)
            ensure_win(widx + g + 1)
            l0_group(seq, g, loaded[widx + g], wout_cb(widx + g + 1))
        widx += 4
        if stage != "L0":
            phase_a(1)
        P.barrier()
        if stage == "L0":
            for b in range(NB):
                P.dma("sp", lambda b=b, seq=seq: nc.sync.dma_start(out=dbg_x1[seq, b * 128:(b + 1) * 128, :], in_=xres[:, b, :]),
                      r=[("x", b)], w=[], key="x%d" % b)
            for b in range(NB):
                P.dma("sp", lambda b=b, seq=seq: nc.sync.dma_start(out=y_prompt[seq, b * 128:(b + 1) * 128, :], in_=xres[:, b, :]),
                      r=[("x", b)], w=[], key="x%d" % b)
            for h in range(4):
                P.dma("sp", lambda seq=seq, h=h: nc.sync.dma_start(out=ret_p[seq, h].rearrange("(c p) v -> p c v", p=128),
                                                              in_=xres[:, 0, :].rearrange("p (c v) -> p c v", c=2)),
                      r=[("x", 0)], w=[], key="rout")
            P.barrier()
            continue
        PP[:] = [0, 1, 2, 3]
        for h in range(4):
            ensure_win(widx + h)
            ensure_win(widx + h + 1)
            l1_head(seq, h, loaded[widx + h], wout_cb(widx + h + 1))
        widx += 4
        PP[:] = [0, 1]
        P.dma("sp", lambda: nc.sync.dma_start(out=gfin, in_=final_g[0:1, :].to_broadcast([128, D])), r=[], w=["gfin", "R", "Rbf"], key="const")
        final_norm(seq)

    P.emit()
    es.close()
    return nc, P


def _arr_win_attn(W):
    out = np.zeros((4, 128, 8, 1344), np.float32)
    for g in range(4):
        k = W[:, 2048 + g * 64:2048 + (g + 1) * 64]
        cols = np.concatenate([W[:, g * 512:(g + 1) * 512], k, np.zeros((D, 128), np.float32), k,
                               W[:, 2304 + g * 64:2304 + (g + 1) * 64], W[:, 2560 + g * 512:2560 + (g + 1) * 512]], axis=1)
        out[g] = cols.reshape(8, 128, 1344).transpose(1, 0, 2)
    return np.ascontiguousarray(out.reshape(4, 128, 8 * 1344))


def _arr_win_ret(W):
    out = np.zeros((4, 128, 8, 1536), np.float32)
    for h in range(4):
        cols = np.concatenate([W[:, h * 256:(h + 1) * 256], W[:, 1024 + h * 256:1024 + (h + 1) * 256],
                               W[:, 2048 + h * 512:2048 + (h + 1) * 512], W[:, 4096 + h * 512:4096 + (h + 1) * 512]], axis=1)
        out[h] = cols.reshape(8, 128, 1536).transpose(1, 0, 2)
    return np.ascontiguousarray(out.reshape(4, 128, 8 * 1536))


def _arr_wout(W):
    return np.ascontiguousarray(W.reshape(4, 4, 128, D).transpose(0, 2, 1, 3).reshape(4, 128, 4096))


_CACHE = {}


def _get_prog(stage):
    if stage not in _CACHE:
        _CACHE[stage] = build(stage)
    return _CACHE[stage]


def kernel(x_prompt, x_sample, cache_swa_k, cache_swa_v, state_ret, norm_g, final_norm_g, rel_bias,
           w_in_attn, attn_sinks, w_out_attn, w_in_ret, w_out_ret, _stage="full"):
    nc, P = _get_prog(_stage)
    f = lambda a: np.ascontiguousarray(np.asarray(a, dtype=np.float32))
    shared = {
        "w_in_attn_r": _arr_win_attn(f(w_in_attn)[0]), "w_out_attn_r": _arr_wout(f(w_out_attn)[0]),
        "w_in_ret_r": _arr_win_ret(f(w_in_ret)[0]), "w_out_ret_r": _arr_wout(f(w_out_ret)[0]),
        "norm_gT": np.ascontiguousarray(f(norm_g).reshape(2, 8, 128).transpose(0, 2, 1)),
        "final_norm_g": f(final_norm_g).reshape(1, D),
        "rel_bias": f(rel_bias), "attn_sinks": f(attn_sinks),
        "c_ident": HC["ident"], "c_etab": HC["etab"], "c_maskT": HC["maskT"], "c_rot": HC["rot"],
        "c_etab_s": HC["etab_s"], "c_eyeq": HC["eyeq"], "c_rot_s": HC["rot_s"],
    }
    xsm = f(x_sample).reshape(128, D)
    ck = f(cache_swa_k).reshape(128, 128, 256)
    cv = f(cache_swa_v).reshape(128, 128, 256)
    st = f(state_ret).reshape(128, 4, 256, 512)
    xp = f(x_prompt)
    in_maps = []
    for c in range(NCORES):
        m = dict(shared)
        m["x_prompt"] = xp[c * NSEQ:(c + 1) * NSEQ]
        if _stage != "L0":
            m["x_sample"] = xsm[c * SB:(c + 1) * SB]
            m["cache_k"] = ck[c * SB:(c + 1) * SB]
            m["cache_v"] = cv[c * SB:(c + 1) * SB]
            m["state_ret"] = st[c * SB:(c + 1) * SB]
        else:
            for k_ in ("c_etab_s", "c_eyeq", "c_rot_s"):
                m.pop(k_, None)
        in_maps.append(m)
    res = run_bass_kernel_spmd(nc, in_maps, core_ids=list(range(NCORES)))
    R = res.results
    if _stage.startswith("S"):
        return np.concatenate([r["y_sample"] for r in R], axis=0), np.concatenate([r["swa_k_sample"] for r in R], axis=0), np.concatenate([r["swa_v_sample"] for r in R], axis=0), np.concatenate([r["ret_state_sample"] for r in R], axis=0)
    y_prompt = np.concatenate([r["y_prompt"] for r in R], axis=0)
    swa_k = np.concatenate([r["swa_k_prompt"] for r in R], axis=0)[None]
    swa_v = np.concatenate([r["swa_v_prompt"] for r in R], axis=0)[None]
    ret_p = np.concatenate([r["ret_state_prompt"] for r in R], axis=0)[None]
    if _stage == "L0":
        return [y_prompt, None, swa_k, swa_v, ret_p, None, None, None], np.concatenate([r["dbg_x1"] for r in R], axis=0)
    y_sample = np.concatenate([r["y_sample"] for r in R], axis=0).reshape(128, 1, D)
    swa_k_s = np.concatenate([r["swa_k_sample"] for r in R], axis=0).reshape(1, 128, 128, 4, 64)
    swa_v_s = np.concatenate([r["swa_v_sample"] for r in R], axis=0).reshape(1, 128, 128, 4, 64)
    ret_s = np.concatenate([r["ret_state_sample"] for r in R], axis=0)[None]
    return (y_prompt, y_sample, swa_k, swa_v, ret_p, swa_k_s, swa_v_s, ret_s)
```
